# Optimizing a Trainium2 kernel written in Bass

```python
import jax, jax.numpy as jnp
from jax import lax
import numpy as np

D_MODEL = 1024
BATCH = 16
SEQ = 4096
DEPTH = 1

HEAD_DIM = 64
BRANCH_WIDTH = D_MODEL // 2
A_GROUPS = ((128, 1), (512, 4), (2048, 16))
A_HEADS = BRANCH_WIDTH // HEAD_DIM
B_HEADS = BRANCH_WIDTH // HEAD_DIM
B_KV_HEADS = B_HEADS // 4
M_HEADS = 4
M_HEAD_DIM = BRANCH_WIDTH // M_HEADS
N_MEM = 256
GRID_W = 64
ROPE_THETA = 10000.0
Q_BLOCK = 128
NORM_EPS = 1e-6
NEG_INF = -1e30
N_BRANCHES = 3

A_QKV = 3 * len(A_GROUPS) * A_HEADS * HEAD_DIM
B_Q = B_HEADS * HEAD_DIM
B_KV = B_KV_HEADS * HEAD_DIM
M_Q = M_HEADS * M_HEAD_DIM
SPLIT_SIZES = (A_QKV, B_Q, B_KV, B_KV, M_Q, BRANCH_WIDTH, BRANCH_WIDTH, BRANCH_WIDTH, N_BRANCHES * D_MODEL)
IN_WIDTH = A_QKV + B_Q + 2 * B_KV + M_Q + 3 * BRANCH_WIDTH + N_BRANCHES * D_MODEL

kernel_name = 'hybrid_dilated_axial_memory_block'


def rms_norm(x, g):
    xf = x.astype(jnp.float32)
    y = xf * lax.rsqrt(jnp.mean(xf * xf, axis=-1, keepdims=True) + NORM_EPS)
    return (y * g.astype(jnp.float32)).astype(x.dtype)


def rope(x, pos):
    dr = x.shape[-1]
    half = dr // 2
    inv = jnp.power(ROPE_THETA, -jnp.arange(half, dtype=jnp.float32) * 2.0 / dr)
    ang = pos.astype(jnp.float32)[:, None] * inv[None, :]
    cos = jnp.cos(ang)[:, None, :]
    sin = jnp.sin(ang)[:, None, :]
    xf = x.astype(jnp.float32)
    x1, x2 = xf[..., :half], xf[..., half:]
    return jnp.concatenate([x1 * cos - x2 * sin, x1 * sin + x2 * cos], axis=-1).astype(x.dtype)


def axial_rope(x, row_pos, col_pos):
    half = x.shape[-1] // 2
    return jnp.concatenate([rope(x[..., :half], row_pos), rope(x[..., half:], col_pos)], axis=-1)


def dilated_window_attention(q, k, v, window, dilation):
    b, s, h, dh = q.shape
    half = window // (2 * dilation)
    blk = half
    L = s // dilation
    nb = -(-L // blk)
    lp = nb * blk
    z = b * dilation

    def regroup(t):
        return t.reshape(b, L, dilation, h, dh).transpose(0, 2, 1, 3, 4).reshape(z, L, h, dh)

    qg, kg, vg = regroup(q), regroup(k), regroup(v)
    qb = jnp.pad(qg, ((0, 0), (0, lp - L), (0, 0), (0, 0))).reshape(z, nb, blk, h, dh)

    def band(t):
        tp = jnp.pad(t, ((0, 0), (blk, lp - L + blk), (0, 0), (0, 0))).reshape(z, nb + 2, blk, h, dh)
        return jnp.concatenate([tp[:, :-2], tp[:, 1:-1], tp[:, 2:]], axis=2)

    kw, vw = band(kg), band(vg)
    qpos = jnp.arange(nb)[:, None] * blk + jnp.arange(blk)[None, :]
    kpos = (jnp.arange(nb)[:, None] - 1) * blk + jnp.arange(3 * blk)[None, :]
    valid = ((jnp.abs(qpos[:, :, None] - kpos[:, None, :]) <= half)
             & (kpos[:, None, :] >= 0) & (kpos[:, None, :] < L))
    sc = jnp.einsum('znqhd,znkhd->znhqk', qb, kw).astype(jnp.float32) * (dh ** -0.5)
    sc = jnp.where(valid[None, :, None], sc, NEG_INF)
    m = jnp.max(sc, axis=-1, keepdims=True)
    p = jnp.exp(sc - m)
    l = jnp.sum(p, axis=-1, keepdims=True)
    o = jnp.einsum('znhqk,znkhd->znqhd', (p / l).astype(v.dtype), vw)
    lse = (m + jnp.log(l))[..., 0]
    o = o.reshape(b, dilation, lp, h, dh)[:, :, :L].transpose(0, 2, 1, 3, 4).reshape(b, s, h, dh)
    lse = lse.transpose(0, 1, 3, 2).reshape(b, dilation, lp, h)[:, :, :L]
    lse = lse.transpose(0, 2, 1, 3).reshape(b, s, h)
    return o, lse


def setup_inputs(seed: int = 0) -> dict:
    key = jax.random.key(seed)
    ks = jax.random.split(key, 16)
    nrm = jax.random.normal
    f32 = jnp.float32
    return {
        'x': nrm(ks[0], (BATCH, SEQ, D_MODEL), f32),
        'mem': nrm(ks[1], (BATCH, N_MEM, D_MODEL), f32),
        'g_pre': 1.0 + 0.05 * nrm(ks[2], (DEPTH, D_MODEL), f32),
        'w_in': nrm(ks[3], (DEPTH, D_MODEL, IN_WIDTH), f32) * D_MODEL ** -0.5,
        'b_merge': 0.1 * nrm(ks[4], (DEPTH, N_BRANCHES * D_MODEL), f32),
        'q_norm': 1.0 + 0.05 * nrm(ks[5], (DEPTH, HEAD_DIM), f32),
        'k_norm': 1.0 + 0.05 * nrm(ks[6], (DEPTH, HEAD_DIM), f32),
        'g_mem': 1.0 + 0.05 * nrm(ks[7], (DEPTH, D_MODEL), f32),
        'w_mem_kv': nrm(ks[8], (DEPTH, D_MODEL, 2 * M_Q), f32) * D_MODEL ** -0.5,
        'w_br_a': nrm(ks[9], (DEPTH, BRANCH_WIDTH, D_MODEL), f32) * BRANCH_WIDTH ** -0.5,
        'w_br_b': nrm(ks[10], (DEPTH, BRANCH_WIDTH, D_MODEL), f32) * BRANCH_WIDTH ** -0.5,
        'w_br_m': nrm(ks[11], (DEPTH, BRANCH_WIDTH, D_MODEL), f32) * BRANCH_WIDTH ** -0.5,
        'w_out': nrm(ks[12], (DEPTH, D_MODEL, D_MODEL), f32) * D_MODEL ** -0.5,
        'g_post': 1.0 + 0.05 * nrm(ks[13], (DEPTH, D_MODEL), f32),
    }


def reference(x, mem, g_pre, w_in, b_merge, q_norm, k_norm, g_mem, w_mem_kv, w_br_a, w_br_b, w_br_m, w_out, g_post):
    b, s, d = x.shape
    dt = x.dtype
    pos = jnp.arange(s, dtype=jnp.float32)
    rows = s // GRID_W
    row_pos = jnp.repeat(jnp.arange(rows, dtype=jnp.float32), GRID_W)
    col_pos = (jnp.arange(s) % GRID_W).astype(jnp.float32)
    split_at = np.cumsum(SPLIT_SIZES)[:-1].tolist()
    n_groups = len(A_GROUPS)
    kv_rep = B_HEADS // B_KV_HEADS
    n_q_blocks = s // Q_BLOCK

    for layer in range(DEPTH):
        h = rms_norm(x, g_pre[layer])
        a_qkv, bq, bk, bv, mq, ga, gb, gm, mg = jnp.split(h @ w_in[layer], split_at, axis=-1)

        a_qkv = a_qkv.reshape(b, s, 3, n_groups * A_HEADS, HEAD_DIM)
        aq = rope(a_qkv[:, :, 0], pos)
        ak = rope(a_qkv[:, :, 1], pos)
        av = a_qkv[:, :, 2]
        outs, lses = [], []
        for gi, (window, dil) in enumerate(A_GROUPS):
            hs = slice(gi * A_HEADS, (gi + 1) * A_HEADS)
            o_g, lse_g = dilated_window_attention(aq[:, :, hs], ak[:, :, hs], av[:, :, hs], window, dil)
            outs.append(o_g)
            lses.append(lse_g)
        wts = jax.nn.softmax(jnp.stack(lses, axis=0), axis=0).astype(dt)
        oa = jnp.einsum('gbsh,gbshd->bshd', wts, jnp.stack(outs, axis=0)).reshape(b, s, BRANCH_WIDTH)

        qb_ = axial_rope(rms_norm(bq.reshape(b, s, B_HEADS, HEAD_DIM), q_norm[layer]), row_pos, col_pos)
        kb_ = axial_rope(rms_norm(bk.reshape(b, s, B_KV_HEADS, HEAD_DIM), k_norm[layer]), row_pos, col_pos)
        vb_ = bv.reshape(b, s, B_KV_HEADS, HEAD_DIM)
        q_blocks = qb_.reshape(b, n_q_blocks, Q_BLOCK, B_KV_HEADS, kv_rep, HEAD_DIM).transpose(1, 0, 2, 3, 4, 5)

        def attend(qc, kb_=kb_, vb_=vb_):
            sc = jnp.einsum('bqkgd,bskd->bkgqs', qc, kb_).astype(jnp.float32) * (HEAD_DIM ** -0.5)
            p = jax.nn.softmax(sc, axis=-1).astype(vb_.dtype)
            return jnp.einsum('bkgqs,bskd->bqkgd', p, vb_)

        ob = lax.map(attend, q_blocks).transpose(1, 0, 2, 3, 4, 5).reshape(b, s, B_Q)

        kvm = rms_norm(mem, g_mem[layer]) @ w_mem_kv[layer]
        n_mem = mem.shape[1]
        km = kvm[..., :M_Q].reshape(b, n_mem, M_HEADS, M_HEAD_DIM)
        vm = kvm[..., M_Q:].reshape(b, n_mem, M_HEADS, M_HEAD_DIM)
        sm = jnp.einsum('bshd,bnhd->bhsn', mq.reshape(b, s, M_HEADS, M_HEAD_DIM), km).astype(jnp.float32)
        pm = jax.nn.softmax(sm * (M_HEAD_DIM ** -0.5), axis=-1).astype(dt)
        om = jnp.einsum('bhsn,bnhd->bshd', pm, vm).reshape(b, s, M_Q)

        ya = (oa * jax.nn.silu(ga)) @ w_br_a[layer]
        yb = (ob * jax.nn.silu(gb)) @ w_br_b[layer]
        ym = (om * jax.nn.silu(gm)) @ w_br_m[layer]
        gates = jax.nn.sigmoid((mg + b_merge[layer]).astype(jnp.float32)).astype(dt).reshape(b, s, N_BRANCHES, d)
        merged = gates[:, :, 0] * ya + gates[:, :, 1] * yb + gates[:, :, 2] * ym
        out = merged @ w_out[layer]
        x = x + rms_norm(out, g_post[layer])
    return x
```

```python
import numpy as np
from contextlib import ExitStack
import concourse.bass as bass
import concourse.mybir as mybir
from concourse.bass_utils import run_bass_kernel_spmd

F32 = mybir.dt.float32
BF16 = mybir.dt.bfloat16
AF = mybir.ActivationFunctionType
ALU = mybir.AluOpType

NCORES = 8
NB = 2
SEQ = 4096
D = 1024
INW = 10496
AQ0, AK0, AV0, BQ0, BK0, BV0, MQ0, GA0, GB0, GM0, MG0 = 0, 1536, 3072, 4608, 5120, 5248, 5376, 5888, 6400, 6912, 7424
DILS = (1, 4, 16)
EPS = 1e-6
POOL_ELEMS = 102400
import os
STAGES = os.environ.get("MK_STAGES", "0mabp2")
ASUB = os.environ.get("MK_ASUB", "vkte")
AGR = os.environ.get("MK_AG", "012")


class Op:
    __slots__ = ("eng", "fn", "deps", "sig", "is_dma", "dma_key", "waits", "inc")

    def __init__(self, eng, fn, is_dma=False, dma_key=None):
        self.eng = eng
        self.fn = fn
        self.deps = []
        self.is_dma = is_dma
        self.dma_key = dma_key
        self.sig = None
        self.waits = []
        self.inc = None


class Res:
    __slots__ = ("w", "r")

    def __init__(self):
        self.w = None
        self.r = {}


class Sched:
    def __init__(self):
        self.ops = []
        self.res = {}
        self.dry = False
        self.last = {}
        self.last_dma = {}

    def _res(self, k):
        r = self.res.get(k)
        if r is None:
            r = self.res[k] = Res()
        return r

    def op(self, eng, fn, reads=(), writes=(), dma_key=None):
        if self.dry:
            return None
        o = Op(eng, fn, is_dma=dma_key is not None, dma_key=dma_key)
        deps = []
        for k in reads:
            r = self._res(k)
            if r.w is not None:
                deps.append((r.w, 0))
        for k in writes:
            r = self._res(k)
            if r.w is not None:
                deps.append((r.w, 1))
            for ro in r.r.values():
                deps.append((ro, 2))
        seen = set()
        for d, kind in deps:
            if id(d) in seen or d is o:
                continue
            if d.eng == eng and not d.is_dma and not o.is_dma:
                if eng == "pe" or kind != 0:
                    continue
            seen.add(id(d))
            o.deps.append(d)
        rk = ("dma", len(self.ops)) if o.is_dma else eng
        for k in reads:
            self._res(k).r[rk] = o
        for k in writes:
            r = self._res(k)
            r.w = o
            r.r = {}
        self.ops.append(o)
        if o.is_dma:
            self.last_dma[dma_key] = o
        else:
            self.last[eng] = o
        return o

    def barrier(self, engines):
        if self.dry:
            return
        prev = list(self.last.values()) + list(self.last_dma.values())
        for e in engines:
            o = Op(e, lambda eng: eng.nop())
            o.deps = [p for p in prev]
            self.ops.append(o)
            self.last[e] = o
        self.res = {}

    def finalize(self):
        need = set()
        for o in self.ops:
            for d in o.deps:
                need.add(id(d))
        cnt = {}
        for o in self.ops:
            if o.is_dma:
                k = ("dma", o.dma_key)
                cnt[k] = cnt.get(k, 0) + 16
                o.sig = (k, cnt[k])
                o.inc = (k, 16)
            elif id(o) in need:
                k = ("eng", o.eng)
                cnt[k] = cnt.get(k, 0) + 1
                o.sig = (k, cnt[k])
                o.inc = (k, 1)
        waited = {}
        for o in self.ops:
            w = waited.setdefault(o.eng, {})
            best = {}
            for d in o.deps:
                k, v = d.sig
                if w.get(k, 0) >= v:
                    continue
                if best.get(k, 0) < v:
                    best[k] = v
            for k, v in best.items():
                w[k] = v
                o.waits.append((k, v))
        return sorted(cnt.keys(), key=str)

    def emit(self, engname, eng, semmap):
        for o in self.ops:
            if o.eng != engname:
                continue
            for k, v in o.waits:
                eng.wait_ge(semmap[k], v)
            ins = o.fn(eng)
            if o.inc is not None:
                ins.then_inc(semmap[o.inc[0]], o.inc[1])


def RC(name, c0, n, gran=512):
    return ["%s.%d" % (name, t) for t in range(c0 // gran, (c0 + n - 1) // gran + 1)]


def PS(*banks):
    return ["ps%d" % b for b in banks]


def _const_tables():
    t = np.arange(SEQ, dtype=np.float64)
    d = np.arange(128)
    dd = d % 64
    fA = dd % 32
    invA = 10000.0 ** (-fA.astype(np.float64) * 2.0 / 64.0)
    angA = invA[:, None] * t[None, :]
    cosA = np.cos(angA)
    sinA = np.sin(angA) * np.where(dd < 32, -1.0, 1.0)[:, None]
    PA = np.zeros((128, 128), np.float32)
    for i in range(128):
        part = i + 32 if (i % 64) < 32 else i - 32
        PA[part, i] = 1.0
    j = dd % 32
    fB = j % 16
    invB = 10000.0 ** (-fB.astype(np.float64) * 2.0 / 32.0)
    rowp = np.floor(t / 64.0)
    colp = t % 64
    posB = np.where((dd // 32)[:, None] == 0, rowp[None, :], colp[None, :])
    angB = invB[:, None] * posB
    cosB = np.cos(angB)
    sinB = np.sin(angB) * np.where(j < 16, -1.0, 1.0)[:, None]
    PBm = np.zeros((128, 128), np.float32)
    for i in range(128):
        part = i + 16 if (i % 32) < 16 else i - 16
        PBm[part, i] = 1.0
    BD = np.zeros((128, 128), np.float32)
    BD[0:64, 0:64] = 1.0
    BD[64:128, 64:128] = 1.0
    ii = np.arange(128)[:, None]
    cc = np.arange(128)[None, :]
    M1 = (ii >= cc).astype(np.float32)
    M2 = (ii <= cc).astype(np.float32)
    MF = np.concatenate([M1, M2] * 4, axis=1)
    ident = np.eye(128, dtype=np.float32)
    f = lambda a: np.ascontiguousarray(a, dtype=np.float32)
    return dict(cosA=f(cosA), sinA=f(sinA), cosB=f(cosB), sinB=f(sinB), PA=f(PA), PB=f(PBm), BD=f(BD),
                MF=f(MF), ident=f(ident))


def build_program():
    nc = bass.Bass("TRN2", target_bir_lowering=False)
    din = lambda n, s: nc.dram_tensor(n, s, F32, kind="ExternalInput").ap()
    x = din("x", [NB, SEQ, D])
    mem = din("mem", [NB, 256, D])
    w_in = din("w_in", [D, INW])
    w_mem = din("w_mem", [D, 1024])
    w_br = [din("w_br%d" % i, [512, D]) for i in range(3)]
    w_out = din("w_out", [D, D])
    g_pre = din("g_pre", [1, D])
    g_mem = din("g_mem", [1, D])
    g_post = din("g_post", [1, D])
    bm = din("bm", [128, 24])
    qn = din("qn", [128, 1])
    kn = din("kn", [128, 1])
    tab = {k: din(k, [128, SEQ]) for k in ("cosA", "sinA", "cosB", "sinB")}
    cPA = din("PA", [128, 128])
    cPB = din("PB", [128, 128])
    cBD = din("BD", [128, 128])
    cMF = din("MF", [128, 1024])
    cID = din("ident", [128, 128])
    y = nc.dram_tensor("y", [NB, SEQ, D], F32, kind="ExternalOutput").ap()
    uscr = nc.dram_tensor("uscr", [NB, 3, 4, 128, SEQ], BF16).ap()

    S = Sched()
    with ExitStack() as es:
        pool = es.enter_context(nc.sbuf_tensor("pool", [128, POOL_ELEMS], BF16))
        ps = es.enter_context(nc.psum_tensor("ps", [128, 8, 512], F32))
        psT = ps[:, 7, :].bitcast(BF16)

        cur = [0]

        def alloc(n, dt=BF16):
            ne = n if dt == BF16 else 2 * n
            o = cur[0]
            cur[0] += ne
            assert cur[0] <= POOL_ELEMS, ("sbuf overflow", cur[0])
            a = pool[:, o:o + ne]
            return a.bitcast(F32) if dt == F32 else a

        ident = alloc(128); PA = alloc(128); PBm = alloc(128); BD = alloc(128); TWOS = alloc(128)
        MF = alloc(1024)
        gpre_bc = alloc(1024, F32); gpost_bc = alloc(1024, F32); gmem_bc = alloc(1024, F32)
        bmh = alloc(24, F32); qn_t = alloc(1, F32); kn_t = alloc(1, F32)
        NEGH = alloc(2, F32)
        EPSB = alloc(2, F32)
        SS = alloc(64, F32)
        phase_base = cur[0]

        hT = alloc(8 * SEQ).rearrange("p (k t) -> p k t", k=8)
        cosA = alloc(SEQ); sinA = alloc(SEQ)
        R1a = alloc(8192); R1b = alloc(8192)
        ACCE = R1a.bitcast(F32); ACCO = R1b.bitcast(F32)
        cosB = R1a[:, 0:SEQ]; sinB = R1a[:, SEQ:2 * SEQ]
        mnT = R1b[:, 4096:6144].rearrange("p (k t) -> p k t", k=8)
        KmT = R1b[:, 6144:7168].rearrange("p (h t) -> p h t", h=4)
        Vm = R1b[:, 7168:8192].rearrange("p (j c) -> p j c", j=2)
        QT = alloc(SEQ); KT = alloc(SEQ)
        XT = [R1a[:, 2048 * k:2048 * (k + 1)].bitcast(F32) for k in range(4)]
        XN = [R1b[:, 1024 * k:1024 * (k + 1)] for k in range(4)]
        JUNK = KT[:, 2048:3072]
        R1A = ["R1a.%d" % k for k in range(4)]
        R1B = ["R1b.%d" % k for k in range(4)] + ["R1b.m"]
        VTraw = alloc(32 * 192)
        VT = VTraw.rearrange("p (j c) -> p j c", c=192)
        VT4 = VTraw.rearrange("p (j a c) -> p j a c", a=3, c=64)
        WM = VTraw[:, 0:4096].rearrange("p (k c) -> p k c", k=8)
        GW = alloc(1024).rearrange("p (k c) -> p k c", k=8)
        NWS = 4
        WS = [alloc(1024).rearrange("p (k c) -> p k c", k=8) for _ in range(NWS)]
        QRAW = [alloc(512) for _ in range(2)]
        T0 = [alloc(512, F32) for _ in range(2)]
        T1 = [alloc(512, F32) for _ in range(2)]
        T2 = [alloc(512, F32) for _ in range(2)]
        OO = [alloc(512, F32) for _ in range(2)]
        PT = [alloc(1024) for _ in range(2)]
        SQ = [alloc(512) for _ in range(2)]
        RS = [alloc(512, F32) for _ in range(2)]
        UST = [alloc(512) for _ in range(2)]
        endA = cur[0]

        cur[0] = phase_base
        WMG = alloc(8 * 3072).rearrange("p (k c) -> p k c", k=8)
        WBR = [alloc(4 * 1024).rearrange("p (k c) -> p k c", k=4) for _ in range(3)]
        WOUT = alloc(8 * 1024).rearrange("p (k c) -> p k c", k=8)
        XB = [alloc(4 * 1024, F32).rearrange("p (j c) -> p j c", j=4) for _ in range(2)]
        HB = alloc(8 * 512).rearrange("p (k t) -> p k t", k=8)
        XNB = [alloc(1024) for _ in range(2)]
        UB = [alloc(12 * 512).rearrange("p (r k t) -> p r k t", r=3, k=4) for _ in range(2)]
        MT = alloc(8 * 512).rearrange("p (k t) -> p k t", k=8)
        TGB = [alloc(512, F32) for _ in range(2)]
        MACC = alloc(512, F32)
        TMPB = [alloc(512, F32)]
        YT = [alloc(1024, F32) for _ in range(2)]
        JUNKB = alloc(1024)
        endB = cur[0]

        pe = lambda fn, r=(), w=(): S.op("pe", fn, r, w)
        act = lambda fn, r=(), w=(): S.op("act", fn, r, w)
        dve = lambda fn, r=(), w=(): S.op("dve", fn, r, w)
        gp = lambda fn, r=(), w=(): S.op("pool", fn, r, w)
        spd = lambda fn, r, w, key: S.op("sp", fn, r, w, dma_key=key)
        gpd = lambda fn, r, w, key: S.op("pool", fn, r, w, dma_key=key)

        w_in_v = w_in.rearrange("(k p) c -> p k c", p=128)
        w_mem_v = w_mem.rearrange("(k p) c -> p k c", p=128)
        w_out_v = w_out.rearrange("(k p) c -> p k c", p=128)
        w_br_v = [w.rearrange("(k p) c -> p k c", p=128) for w in w_br]

        class WStream:
            def __init__(self):
                self.specs = []
                self.i = 0
                self.issued = 0

            def _issue(self, k):
                slot = k % NWS
                for (c0, n, d0) in self.specs[k]:
                    srcv = w_in_v
                    if c0 < 0:
                        srcv = w_mem_v
                        c0 = -c0 - 1
                    gpd(lambda e, slot=slot, c0=c0, n=n, d0=d0, srcv=srcv: e.dma_start(out=WS[slot][:, :, d0:d0 + n], in_=srcv[:, :, c0:c0 + n]),
                        [], ["WS%d" % slot], "WS%d" % slot)

            def next(self, pieces):
                if S.dry:
                    self.specs.append(pieces)
                    return 0
                k = self.i
                self.i += 1
                while self.issued < min(len(self.specs), k + NWS - 1):
                    self._issue(self.issued)
                    self.issued += 1
                return k % NWS

        W = WStream()

        def tokslice(g, p0, n):
            d = DILS[g]
            if d == 1:
                return slice(p0, p0 + n)
            L = SEQ // d
            r, m0 = p0 // L, p0 % L
            assert m0 + n <= L
            s0 = m0 * d + r
            return slice(s0, s0 + (n - 1) * d + 1, d)

        HT_ALL = ["hT.%d" % t for t in range(8)]

        def ht_res(g, p0, n):
            return RC("hT", p0, n) if DILS[g] == 1 else HT_ALL

        def proj_fm(slot, g, p0, n, bank, col0=0, wt=None, wres=None):
            sl = tokslice(g, p0, n)
            if wt is None:
                wt = WS[slot]
                wres = ["WS%d" % slot]
            for kc in range(8):
                pe(lambda e, kc=kc: e.matmul(ps[:, bank, col0:col0 + n], lhsT=wt[:, kc, :], rhs=hT[:, kc, sl],
                                             start=(kc == 0), stop=(kc == 7)),
                   list(wres) + ht_res(g, p0, n), PS(bank))

        ctr = {"rope": 0, "pt": 0, "ep": 0, "x": 0, "pj": 0}

        def rope_p1(src, src_res, src_is_psum, cs, sn, perm, sl, dst, dst_res, n, tres=("tabs",), dsplit=1, rbank=2, copy_eng="act"):
            i = ctr["rope"] % 2
            ctr["rope"] += 1
            rw = (lambda r, w: ([], r + w)) if src_is_psum else (lambda r, w: (r, w))
            r_, w_ = rw(src_res, ["QRAW%d" % i])
            if copy_eng == "act":
                act(lambda e: e.activation(out=QRAW[i][:, 0:n], in_=src, func=AF.Copy), r_, w_)
            else:
                gp(lambda e: e.tensor_copy(out=QRAW[i][:, 0:n], in_=src), r_, w_)
            r_, w_ = rw(src_res, ["T1_%d" % i])
            dve(lambda e: e.tensor_tensor(out=T1[i][:, 0:n], in0=src, in1=cs[:, sl], op=ALU.mult), r_ + list(tres), w_)
            return (i, sn, perm, sl, dst, dst_res, n, tres, dsplit, rbank)

        def rope_p2(st):
            i, sn, perm, sl, dst, dst_res, n, tres, dsplit, rbank = st
            pe(lambda e: e.matmul(ps[:, rbank, 0:n], lhsT=perm, rhs=QRAW[i][:, 0:n], start=True, stop=True),
               ["QRAW%d" % i, "const"], PS(rbank))
            dve(lambda e: e.tensor_tensor(out=T2[i][:, 0:n], in0=ps[:, rbank, 0:n], in1=sn[:, sl], op=ALU.mult),
                list(tres), PS(rbank) + ["T2_%d" % i])
            a0, a1 = T1[i][:, 0:n], T2[i][:, 0:n]
            if dsplit > 1:
                a0 = a0.rearrange("p (m r) -> p m r", r=dsplit)
                a1 = a1.rearrange("p (m r) -> p m r", r=dsplit)
            gp(lambda e: e.tensor_tensor(out=dst, in0=a0, in1=a1, op=ALU.add),
               ["T1_%d" % i, "T2_%d" % i], dst_res)

        def rope_tile(*a, **k):
            rope_p2(rope_p1(*a, **k))

        def gate_part(slot, tt, bank, wt=None, wres=None):
            i = ctr["ep"] % 2
            ctr["ep"] += 1
            proj_fm(slot, 0, tt * 512, 512, bank, wt=wt, wres=wres)
            act(lambda e: e.activation(out=T0[i][:], in_=ps[:, bank, :], func=AF.Tanh, scale=0.5), [], PS(bank) + ["T0_%d" % i])
            dve(lambda e: e.scalar_tensor_tensor(out=T0[i][:], in0=T0[i][:], scalar=1.0, in1=ps[:, bank, :], op0=ALU.add, op1=ALU.mult),
                ["T0_%d" % i], PS(bank) + ["T0_%d" % i])
            return i

        def gate_tail(i, b, br, chunk, tt, num_e, num_o, l_e, l_o, src_res, src_psum):
            rr, ww = ([], list(src_res)) if src_psum else (list(src_res), [])
            oo = "OO%d" % i
            if num_o is None:
                dve(lambda e: e.reciprocal(out=OO[i][:], in_=l_e), rr, ww + [oo])
                dve(lambda e: e.tensor_tensor(out=OO[i][:], in0=num_e, in1=OO[i][:], op=ALU.mult), rr + [oo], ww + [oo])
            else:
                dve(lambda e: e.reciprocal(out=OO[i][0:64, :], in_=l_e), rr, ww + [oo])
                dve(lambda e: e.reciprocal(out=OO[i][64:128, :], in_=l_o), rr, ww + [oo])
                dve(lambda e: e.tensor_tensor(out=OO[i][0:64, :], in0=num_e, in1=OO[i][0:64, :], op=ALU.mult), rr + [oo], ww + [oo])
                dve(lambda e: e.tensor_tensor(out=OO[i][64:128, :], in0=num_o, in1=OO[i][64:128, :], op=ALU.mult), rr + [oo], ww + [oo])
            gp(lambda e: e.tensor_tensor(out=UST[i][:], in0=OO[i][:], in1=T0[i][:], op=ALU.mult), [oo, "T0_%d" % i], ["UST%d" % i])
            spd(lambda e: e.dma_start(out=uscr[b, br, chunk, :, tt * 512:(tt + 1) * 512], in_=UST[i][:]),
                ["UST%d" % i], [], "UST%d" % i)

        def norm_rows(src_dram, gbc, i, sscol):
            xr = ["R1a.%d" % i]
            nr = ["R1b.%d" % i]
            spd(lambda e: e.dma_start(out=XT[i][:], in_=src_dram), [], xr, "XT%d" % i)
            gp(lambda e: e.memset(SS[:, sscol:sscol + 1], 0.0), [], ["SS%d" % sscol])
            act(lambda e: e.activation(out=JUNK[:], in_=XT[i][:], func=AF.Square, accum_out=SS[:, sscol:sscol + 1]),
                xr + ["SS%d" % sscol], ["KT.4", "KT.5", "SS%d" % sscol])
            dve(lambda e: e.tensor_scalar(out=SS[:, sscol + 1:sscol + 2], in0=SS[:, sscol:sscol + 1], scalar1=1.0 / D, scalar2=EPS,
                                          op0=ALU.mult, op1=ALU.add), ["SS%d" % sscol], ["SS%d" % sscol])
            gp(lambda e: e.tensor_tensor(out=SS[:, sscol + 2:sscol + 3], in0=SS[:, sscol + 1:sscol + 2], in1=NEGH[:, 0:1], op=ALU.pow),
               ["SS%d" % sscol, "const"], ["SS%d" % sscol])
            dve(lambda e: e.scalar_tensor_tensor(out=XN[i][:], in0=XT[i][:], scalar=SS[:, sscol + 2:sscol + 3], in1=gbc[:],
                                                 op0=ALU.mult, op1=ALU.mult), xr + ["SS%d" % sscol, "const"], nr)
            return nr

        def transpose_rows(i, nr, dst3, dst_res):
            for kc in range(8):
                pe(lambda e, kc=kc: e.transpose(out=psT[:, kc * 128:(kc + 1) * 128], in_=XN[i][:, kc * 128:(kc + 1) * 128], identity=ident[:]),
                   nr + ["const"], PS(7))
            act(lambda e: e.activation(out=dst3, in_=psT.rearrange("p (k t) -> p k t", k=8), func=AF.Copy), [], PS(7) + dst_res)

        def setup():
            for (dst, src, nm) in ((ident, cID, "cid"), (PA, cPA, "cpa"), (PBm, cPB, "cpb"), (BD, cBD, "cbd"), (MF, cMF, "cmf")):
                gpd(lambda e, dst=dst, src=src: e.dma_start(out=dst[:], in_=src), [], ["const"], nm)
            for (dst, src, nm) in ((gpre_bc, g_pre, "cg1"), (gpost_bc, g_post, "cg2"), (gmem_bc, g_mem, "cg3")):
                spd(lambda e, dst=dst, src=src: e.dma_start(out=dst[:], in_=src.partition_broadcast(128)), [], ["const"], nm)
            spd(lambda e: e.dma_start(out=bmh[:], in_=bm), [], ["const"], "cbm")
            spd(lambda e: e.dma_start(out=qn_t[:], in_=qn), [], ["const"], "cqn")
            spd(lambda e: e.dma_start(out=kn_t[:], in_=kn), [], ["const"], "ckn")
            gp(lambda e: e.memset(TWOS[:], 2.0), [], ["const"])
            gp(lambda e: e.memset(NEGH[:], -0.5), [], ["const"])
            gp(lambda e: e.memset(EPSB[:], EPS), [], ["const"])
            dve(lambda e: e.tensor_scalar(out=bmh[:], in0=bmh[:], scalar1=0.5, scalar2=None, op0=ALU.mult), ["const"], ["const"])
            gpd(lambda e: e.dma_start(out=cosA[:], in_=tab["cosA"]), [], ["tabs"], "tcA")
            gpd(lambda e: e.dma_start(out=sinA[:], in_=tab["sinA"]), [], ["tabs"], "tsA")

        def stage0(b):
            pend = None
            for tt in range(32):
                i = tt % 4
                nr = norm_rows(x[b, tt * 128:(tt + 1) * 128, :], gpre_bc, i, 4 * (tt % 8))
                if pend is not None:
                    transpose_rows(*pend)
                pend = (i, nr, hT[:, :, tt * 128:(tt + 1) * 128], ["hT.%d" % (tt // 4)])
            transpose_rows(*pend)

        def mixer_m(b):
            for j in range(2):
                nr = norm_rows(mem[b, j * 128:(j + 1) * 128, :], gmem_bc, j, 32 + 4 * j)
                transpose_rows(j, nr, mnT[:, :, j * 128:(j + 1) * 128], ["R1b.m"])
            gpd(lambda e: e.dma_start(out=WM[:], in_=w_mem_v[:, :, 512:1024]), [], ["VT"], "WM")
            for j in range(2):
                for kc in range(8):
                    pe(lambda e, kc=kc, j=j: e.matmul(ps[:, 0, :], lhsT=mnT[:, kc, j * 128:(j + 1) * 128], rhs=WM[:, kc, :],
                                                       start=(kc == 0), stop=(kc == 7)), ["R1b.m", "VT"], PS(0))
                act(lambda e, j=j: e.activation(out=Vm[:, j, :], in_=ps[:, 0, :], func=AF.Copy), [], PS(0) + ["R1b.m"])
            for h in range(4):
                slot = W.next([(-(h * 128) - 1, 128, 0)])
                for kc in range(8):
                    pe(lambda e, kc=kc, slot=slot: e.matmul(ps[:, 1, 0:256], lhsT=WS[slot][:, kc, :], rhs=mnT[:, kc, :],
                                                             start=(kc == 0), stop=(kc == 7)), ["WS%d" % slot, "R1b.m"], PS(1))
                act(lambda e, h=h: e.activation(out=KmT[:, h, :], in_=ps[:, 1, 0:256], func=AF.Copy), [], PS(1) + ["R1b.m"])
            scale = 128.0 ** -0.5
            for h in range(4):
                slot = W.next([(MQ0 + h * 128, 128, 0)])
                for tt in range(8):
                    bank = tt % 2
                    proj_fm(slot, 0, tt * 512, 512, bank)
                    act(lambda e, tt=tt, bank=bank: e.activation(out=QT[:, tt * 512:(tt + 1) * 512], in_=ps[:, bank, :], func=AF.Copy),
                        [], PS(bank) + ["QT.%d" % tt])
                gslot = W.next([(GM0 + h * 128, 128, 0)])
                pend = None
                for tt in range(8):
                    i = ctr["pt"] % 2
                    ctr["pt"] += 1
                    ab, lb = (5, 6) if tt % 2 == 0 else (0, 1)
                    gb_ = 2 if tt % 2 == 0 else 7
                    for j in range(2):
                        pe(lambda e, j=j, tt=tt, h=h: e.matmul(ps[:, 3 + j, :], lhsT=KmT[:, h, j * 128:(j + 1) * 128], rhs=QT[:, tt * 512:(tt + 1) * 512],
                                                                 start=True, stop=True), ["R1b.m", "QT.%d" % tt], PS(3 + j))
                    gi = gate_part(gslot, tt, gb_)
                    act(lambda e, i=i: e.activation(out=PT[i][:].rearrange("p (a c) -> p a c", a=2), in_=ps[:, 3:5, :], func=AF.Exp, scale=scale),
                        [], PS(3, 4) + ["PT%d" % i])
                    for j in range(2):
                        pe(lambda e, j=j, i=i, h=h, ab=ab: e.matmul(ps[:, ab, :], lhsT=Vm[:, j, h * 128:(h + 1) * 128], rhs=PT[i][:, j * 512:(j + 1) * 512],
                                                                     start=(j == 0), stop=(j == 1)), ["R1b.m", "PT%d" % i], PS(ab))
                    for j in range(2):
                        pe(lambda e, j=j, i=i, lb=lb: e.matmul(ps[:, lb, :], lhsT=TWOS[:], rhs=PT[i][:, j * 512:(j + 1) * 512],
                                                                start=(j == 0), stop=(j == 1)), ["const", "PT%d" % i], PS(lb))
                    if pend is not None:
                        gate_tail(*pend)
                    pend = (gi, b, 2, h, tt, ps[:, ab, :], None, ps[:, lb, :], None, PS(ab, lb), True)
                gate_tail(*pend)

        def a_units(g):
            d = DILS[g]
            L = SEQ // d
            batches = []
            for r in range(d):
                pc = r * L
                units = [(pc, 64, [pc // 128], 1)]
                for a in range(L // 128 - 1):
                    units.append((pc + 128 * a + 64, 128, [pc // 128 + a, pc // 128 + a + 1], 0))
                units.append((pc + L - 64, 64, [(pc + L) // 128 - 1], 2))
                curb = None
                for u in units:
                    if curb is None or (u[0] + u[1] - curb[0]) > 512:
                        curb = [u[0], 0, []]
                        batches.append(curb)
                    curb[2].append(u)
                    curb[1] = u[0] + u[1] - curb[0]
            return batches

        def mixer_a_job(b, g, hp):
            d = DILS[g]
            L = SEQ // d
            ntile = min(512, L)
            cq = g * 512 + hp * 128
            slot = W.next([(AV0 + cq, 128, 0)])
            for j4 in range(8 if "v" in ASUB else 0):
                bank = j4 % 2
                for jj in range(4):
                    j = j4 * 4 + jj
                    sl = tokslice(g, 128 * j, 128)
                    for kc in range(8):
                        pe(lambda e, kc=kc, sl=sl, jj=jj, bank=bank, slot=slot: e.matmul(ps[:, bank, jj * 128:(jj + 1) * 128], lhsT=hT[:, kc, sl], rhs=WS[slot][:, kc, :],
                                                                                  start=(kc == 0), stop=(kc == 7)),
                           ["WS%d" % slot] + ht_res(g, 128 * j, 128), PS(bank))
                act(lambda e, j4=j4, bank=bank: e.activation(out=VT4[:, j4 * 4:(j4 + 1) * 4, 0:3:2, :],
                                                             in_=ps[:, bank, :].rearrange("p (j a c) -> p j a c", j=4, a=2), func=AF.Copy),
                    [], PS(bank) + ["VT"])
                ep_step()
            for (c0, dstT, nm) in ((AK0 + cq, KT, "KT"), (AQ0 + cq, QT, "QT")):
                slot = W.next([(c0, 128, 0)])
                pend = None
                for tt in range(8 if "k" in ASUB else 0):
                    bank = tt % 2
                    proj_fm(slot, 0, tt * 512, 512, bank)
                    if d == 1:
                        dst = dstT[:, tt * 512:(tt + 1) * 512]
                        dres = RC(nm, tt * 512, 512)
                    else:
                        dst = dstT[:, :].rearrange("p (r m) -> p m r", r=d)[:, tt * 512 // d:(tt + 1) * 512 // d, :]
                        dres = RC(nm, 0, SEQ)
                    st = rope_p1(ps[:, bank, :], PS(bank), True, cosA, sinA, PA[:], slice(tt * 512, (tt + 1) * 512),
                                 dst, dres, 512, dsplit=d)
                    if pend is not None:
                        rope_p2(pend)
                    pend = st
                    ep_step()
                if pend is not None:
                    rope_p2(pend)
            if "t" not in ASUB:
                return
            allu = []
            for bi, (plo, width, units) in enumerate(a_units(g)):
                accs = (5, 6) if bi % 2 == 0 else (0, 1)
                for ui, u in enumerate(units):
                    allu.append((plo, width, accs, u, ui == len(units) - 1))
            supers = [allu[k:k + 2] for k in range(0, len(allu), 2)]
            OFFS = {0: [0, 128], 1: [192], 2: [0]}

            def emit_qk(sidx):
                banks = (3, 4) if sidx % 2 == 0 else (2, 7)
                for si, (plo, width, accs, (q0, nq, ktiles, kind), last) in enumerate(supers[sidx]):
                    for hh in range(2):
                        rows = slice(64 * hh, 64 * hh + 64)
                        for t, kt in enumerate(ktiles):
                            co = si * 256 + OFFS[kind][t]
                            pe(lambda e, rows=rows, kt=kt, co=co, q0=q0, nq=nq, bk=banks[hh]: e.matmul(ps[:, bk, co:co + nq], lhsT=KT[rows, kt * 128:(kt + 1) * 128],
                                                                                                 rhs=QT[rows, q0:q0 + nq], start=True, stop=True),
                               RC("KT", kt * 128, 128) + RC("QT", q0, nq), PS(banks[hh]))

            def emit_rest(sidx):
                i = sidx % 2
                banks = (3, 4) if i == 0 else (2, 7)
                bsl = slice(3, 5) if i == 0 else slice(2, 8, 5)
                act(lambda e: e.activation(out=PT[i][:].rearrange("p (h c) -> p h c", h=2), in_=ps[:, bsl, :], func=AF.Exp, scale=0.125),
                    [], PS(*banks) + ["PT%d" % i])
                dve(lambda e: e.tensor_tensor(out=PT[i][:], in0=PT[i][:], in1=MF[:], op=ALU.mult), ["PT%d" % i, "const"], ["PT%d" % i])
                for si, (plo, width, accs, (q0, nq, ktiles, kind), last) in enumerate(supers[sidx]):
                    c0 = q0 - plo
                    for hh in range(2):
                        accb = accs[hh]
                        for t, kt in enumerate(ktiles):
                            co = hh * 512 + si * 256 + OFFS[kind][t]
                            pe(lambda e, hh=hh, accb=accb, kt=kt, co=co, c0=c0, nq=nq, t=t, nk=len(ktiles):
                               e.matmul(ps[:, accb, c0:c0 + nq], lhsT=VT[:, kt, 64 * hh:64 * hh + 128], rhs=PT[i][:, co:co + nq],
                                        start=(t == 0), stop=(t == nk - 1)),
                               ["VT", "PT%d" % i], PS(accb))
                    if last:
                        ae, ao = accs
                        sl = tokslice(g, plo, width)
                        if g == 0:
                            act(lambda e, sl=sl, ae=ae, width=width: e.activation(out=ACCE[:, sl], in_=ps[:, ae, 0:width], func=AF.Copy), [], PS(ae) + R1A)
                            act(lambda e, sl=sl, ao=ao, width=width: e.activation(out=ACCO[:, sl], in_=ps[:, ao, 0:width], func=AF.Copy), [], PS(ao) + R1B)
                        else:
                            dve(lambda e, sl=sl, ae=ae, width=width: e.tensor_tensor(out=ACCE[:, sl], in0=ps[:, ae, 0:width], in1=ACCE[:, sl], op=ALU.add),
                                R1A, PS(ae) + R1A)
                            dve(lambda e, sl=sl, ao=ao, width=width: e.tensor_tensor(out=ACCO[:, sl], in0=ps[:, ao, 0:width], in1=ACCO[:, sl], op=ALU.add),
                                R1B, PS(ao) + R1B)

            emit_qk(0)
            for sidx in range(len(supers)):
                if sidx + 1 < len(supers):
                    emit_qk(sidx + 1)
                emit_rest(sidx)

        ep_queue = []

        def ep_step():
            if ep_queue:
                ep_queue.pop(0)()

        def queue_epilogue(b, hp):
            st = {"pend": None}

            def mk(k):
                def step():
                    gi = None
                    if k < 8:
                        gi = gate_part(None, k, 3 if k % 2 == 0 else 4, wt=GW, wres=["GW"])
                    if st["pend"] is not None:
                        gate_tail(*st["pend"])
                        st["pend"] = None
                    if k < 8:
                        cs = slice(k * 512, (k + 1) * 512)
                        st["pend"] = (gi, b, 0, hp, k, ACCE[0:64, cs], ACCO[64:128, cs], ACCE[64:128, cs], ACCO[0:64, cs], R1A + R1B, False)
                return step
            for k in range(9):
                ep_queue.append(mk(k))

        def mixer_a(b):
            gp(lambda e: e.memset(VT[:, :, 64:128], 2.0), [], ["VT"])
            for hp in range(4):
                for g in range(3):
                    if g == 1 and "e" in ASUB:
                        gpd(lambda e, hp=hp: e.dma_start(out=GW[:], in_=w_in_v[:, :, GA0 + hp * 128:GA0 + (hp + 1) * 128]), [], ["GW"], "GW")
                    if str(g) in AGR:
                        mixer_a_job(b, g, hp)
                if "e" in ASUB:
                    queue_epilogue(b, hp)
                    while ep_queue:
                        ep_step()

        def qk_norm_rope(slot, gain, dstT, nm):
            PB4 = (0, 1, 4, 5)
            sts = {}

            def stA(tt):
                bank = PB4[tt % 4]
                i = tt % 2
                proj_fm(slot, 0, tt * 512, 512, bank)
                act(lambda e: e.activation(out=SQ[i][:], in_=ps[:, bank, :], func=AF.Square), [], PS(bank) + ["SQ%d" % i])

            def stB(tt):
                bank = PB4[tt % 4]
                i = tt % 2
                pe(lambda e: e.matmul(ps[:, 2, :], lhsT=BD[:], rhs=SQ[i][:], start=True, stop=True), ["SQ%d" % i, "const"], PS(2))
                act(lambda e: e.activation(out=RS[i][:], in_=ps[:, 2, :], func=AF.Ln, scale=1.0 / 64.0, bias=EPSB[:, 0:1]), ["const"], PS(2) + ["RS%d" % i])
                act(lambda e: e.activation(out=RS[i][:], in_=RS[i][:], func=AF.Exp, scale=-0.5), ["RS%d" % i], ["RS%d" % i])
                dve(lambda e: e.scalar_tensor_tensor(out=T0[i][:], in0=ps[:, bank, :], scalar=gain[:, 0:1], in1=RS[i][:],
                                                     op0=ALU.mult, op1=ALU.mult), ["RS%d" % i, "const"], PS(bank) + ["T0_%d" % i])
                sl = slice(tt * 512, (tt + 1) * 512)
                sts[tt] = rope_p1(T0[i][:], ["T0_%d" % i], False, cosB, sinB, PBm[:], sl, dstT[:, sl], RC(nm, tt * 512, 512), 512,
                                  tres=R1A, rbank=3, copy_eng="pool")

            for k in range(10):
                if k < 8:
                    stA(k)
                if 0 <= k - 1 < 8:
                    stB(k - 1)
                if 0 <= k - 2 < 8:
                    rope_p2(sts[k - 2])

        def mixer_b(b):
            gpd(lambda e: e.dma_start(out=cosB, in_=tab["cosB"]), [], R1A, "tcB")
            gpd(lambda e: e.dma_start(out=sinB, in_=tab["sinB"]), [], R1A, "tsB")
            for kv in range(2):
                slot = W.next([(BV0 + kv * 64, 64, 0), (BV0 + kv * 64, 64, 64)])
                for j4 in range(8):
                    bank = j4 % 2
                    for jj in range(4):
                        j = j4 * 4 + jj
                        for kc in range(8):
                            pe(lambda e, kc=kc, j=j, jj=jj, bank=bank, slot=slot: e.matmul(ps[:, bank, jj * 128:(jj + 1) * 128], lhsT=hT[:, kc, j * 128:(j + 1) * 128],
                                                                                 rhs=WS[slot][:, kc, :], start=(kc == 0), stop=(kc == 7)),
                               ["WS%d" % slot, "hT.%d" % (j // 4)], PS(bank))
                    act(lambda e, j4=j4, bank=bank: e.activation(out=VT4[:, j4 * 4:(j4 + 1) * 4, 0:3:2, :],
                                                                 in_=ps[:, bank, :].rearrange("p (j a c) -> p j a c", j=4, a=2), func=AF.Copy),
                        [], PS(bank) + ["VT"])
                slot = W.next([(BK0 + kv * 64, 64, 0), (BK0 + kv * 64, 64, 64)])
                qk_norm_rope(slot, kn_t, KT, "KT")
                for hq in range(2):
                    c = kv * 2 + hq
                    slot = W.next([(BQ0 + c * 128, 128, 0)])
                    qk_norm_rope(slot, qn_t, QT, "QT")
                    gslot = W.next([(GB0 + c * 128, 128, 0)])
                    gnext = gate_part(gslot, 0, 5)
                    for tt in range(8):
                        ae, ao = (4, 5) if tt % 2 == 0 else (6, 7)
                        qs = slice(tt * 512, (tt + 1) * 512)
                        gi = gnext

                        def qk(kc, u):
                            for hh in range(2):
                                rows = slice(64 * hh, 64 * hh + 64)
                                pe(lambda e, rows=rows, hh=hh, kc=kc, u=u, qs=qs: e.matmul(ps[:, 2 * u + hh, :], lhsT=KT[rows, kc * 128:(kc + 1) * 128], rhs=QT[rows, qs],
                                                                                  start=True, stop=True),
                                   RC("KT", kc * 128, 128) + ["QT.%d" % tt], PS(2 * u + hh))
                        qk(0, 0)
                        for kc in range(32):
                            u = kc % 2
                            i = kc % 2
                            if kc + 1 < 32:
                                qk(kc + 1, (kc + 1) % 2)
                            act(lambda e, u=u, i=i: e.activation(out=PT[i][:].rearrange("p (a c) -> p a c", a=2), in_=ps[:, 2 * u:2 * u + 2, :],
                                                                 func=AF.Exp, scale=0.125), [], PS(2 * u, 2 * u + 1) + ["PT%d" % i])
                            pe(lambda e, kc=kc, i=i, ae=ae: e.matmul(ps[:, ae, :], lhsT=VT[:, kc, 0:128], rhs=PT[i][:, 0:512], start=(kc == 0), stop=(kc == 31)),
                               ["VT", "PT%d" % i], PS(ae))
                            pe(lambda e, kc=kc, i=i, ao=ao: e.matmul(ps[:, ao, :], lhsT=VT[:, kc, 64:192], rhs=PT[i][:, 512:1024], start=(kc == 0), stop=(kc == 31)),
                               ["VT", "PT%d" % i], PS(ao))
                        if tt + 1 < 8:
                            gnext = gate_part(gslot, tt + 1, 7 if tt % 2 == 0 else 5)
                        gate_tail(gi, b, 1, c, tt, ps[0:64, ae, :], ps[64:128, ao, :], ps[64:128, ae, :], ps[0:64, ao, :],
                                  PS(ae, ao), True)

        def phase_b():
            for k3 in range(3):
                gpd(lambda e, k3=k3: e.dma_start(out=WMG[:, :, k3 * 1024:(k3 + 1) * 1024], in_=w_in_v[:, :, MG0 + k3 * 1024:MG0 + (k3 + 1) * 1024]),
                    [], ["WMG%d" % k3], "WMG%d" % k3)
                gpd(lambda e, k3=k3: e.dma_start(out=WBR[k3][:], in_=w_br_v[k3]), [], ["WBR%d" % k3], "WBR%d" % k3)
            gpd(lambda e: e.dma_start(out=WOUT[:], in_=w_out_v), [], ["WOUT"], "WOUT")
            tiles = [(b, tt) for b in range(NB) for tt in range(8)]

            def loads(n):
                b, tt = tiles[n]
                i = n % 2
                spd(lambda e: e.dma_start(out=XB[i][:], in_=x[b, tt * 512:(tt + 1) * 512, :].rearrange("(j p) c -> p j c", p=128)),
                    [], ["XB%d" % i], "XB%d" % i)
                for br in range(3):
                    spd(lambda e, br=br: e.dma_start(out=UB[i][:, br, :, :], in_=uscr[b, br, :, :, tt * 512:(tt + 1) * 512].rearrange("k p t -> p k t")),
                        [], ["UB%d" % i], "UB%d_%d" % (i, br))

            def norm_p1(i, j):
                k = j % 2
                sc = 4 * j
                gp(lambda e: e.memset(SS[:, sc:sc + 1], 0.0), [], ["SSn%d" % j])
                act(lambda e: e.activation(out=JUNKB[:], in_=XB[i][:, j, :], func=AF.Square, accum_out=SS[:, sc:sc + 1]),
                    ["XB%d" % i, "SSn%d" % j], ["JUNKB", "SSn%d" % j])
                dve(lambda e: e.tensor_scalar(out=SS[:, sc + 1:sc + 2], in0=SS[:, sc:sc + 1], scalar1=1.0 / D, scalar2=EPS, op0=ALU.mult, op1=ALU.add),
                    ["SSn%d" % j], ["SSn%d" % j])
                gp(lambda e: e.tensor_tensor(out=SS[:, sc + 2:sc + 3], in0=SS[:, sc + 1:sc + 2], in1=NEGH[:, 0:1], op=ALU.pow), ["SSn%d" % j, "const"], ["SSn%d" % j])
                dve(lambda e: e.scalar_tensor_tensor(out=XNB[k][:], in0=XB[i][:, j, :], scalar=SS[:, sc + 2:sc + 3], in1=gpre_bc[:],
                                                     op0=ALU.mult, op1=ALU.mult), ["XB%d" % i, "SSn%d" % j, "const"], ["XNB%d" % k])
                return k

            def norm_p2(j, k):
                for kc in range(8):
                    pe(lambda e, kc=kc: e.transpose(out=psT[:, kc * 128:(kc + 1) * 128], in_=XNB[k][:, kc * 128:(kc + 1) * 128], identity=ident[:]),
                       ["XNB%d" % k, "const"], PS(7))
                act(lambda e: e.activation(out=HB[:, :, j * 128:(j + 1) * 128], in_=psT.rearrange("p (k t) -> p k t", k=8), func=AF.Copy),
                    [], PS(7) + ["HB"])

            def build_hb(i):
                kk = norm_p1(i, 0)
                for j in range(4):
                    kn_ = norm_p1(i, j + 1) if j + 1 < 4 else None
                    norm_p2(j, kk)
                    kk = kn_

            def tile_body(n, b, tt, i):
                have_next = n + 1 < len(tiles)
                if have_next:
                    loads(n + 1)
                for c in range(8):
                    for br in range(3):
                        q = (c * 3 + br) % 2
                        bm_, by_ = (0, 1) if q == 0 else (2, 3)
                        for kc in range(8):
                            pe(lambda e, kc=kc, br=br, c=c, bm_=bm_: e.matmul(ps[:, bm_, :], lhsT=WMG[:, kc, br * 1024 + c * 128:br * 1024 + (c + 1) * 128], rhs=HB[:, kc, :],
                                                                             start=(kc == 0), stop=(kc == 7)), ["WMG%d" % br, "HB"], PS(bm_))
                        for k4 in range(4):
                            pe(lambda e, k4=k4, br=br, c=c, by_=by_: e.matmul(ps[:, by_, :], lhsT=WBR[br][:, k4, c * 128:(c + 1) * 128], rhs=UB[i][:, br, k4, :],
                                                                             start=(k4 == 0), stop=(k4 == 3)), ["WBR%d" % br, "UB%d" % i], PS(by_))
                        act(lambda e, q=q, bm_=bm_, br=br, c=c: e.activation(out=TGB[q][:], in_=ps[:, bm_, :], func=AF.Tanh, scale=0.5,
                                                                             bias=bmh[:, br * 8 + c:br * 8 + c + 1]), ["const"], PS(bm_) + ["TGB%d" % q])
                        if br == 0:
                            dve(lambda e, q=q, by_=by_: e.scalar_tensor_tensor(out=MACC[:], in0=TGB[q][:], scalar=1.0, in1=ps[:, by_, :], op0=ALU.add, op1=ALU.mult),
                                ["TGB%d" % q], PS(by_) + ["MACC"])
                        else:
                            dve(lambda e, q=q, by_=by_: e.scalar_tensor_tensor(out=TMPB[0][:], in0=TGB[q][:], scalar=1.0, in1=ps[:, by_, :], op0=ALU.add, op1=ALU.mult),
                                ["TGB%d" % q], PS(by_) + ["TMPB0"])
                            gp(lambda e, q=q: e.tensor_tensor(out=MACC[:], in0=MACC[:], in1=TMPB[0][:], op=ALU.add), ["MACC", "TMPB0"], ["MACC"])
                    act(lambda e, c=c: e.activation(out=MT[:, c, :], in_=MACC[:], func=AF.Copy, scale=0.5), ["MACC"], ["MT"])
                inext = (n + 1) % 2
                kk = norm_p1(inext, 0) if have_next else None
                for j in range(4):
                    k = j % 2
                    b0, b1 = (4, 5) if j % 2 == 0 else (0, 1)
                    for hf, bk in ((0, b0), (1, b1)):
                        for kc in range(8):
                            pe(lambda e, kc=kc, j=j, hf=hf, bk=bk: e.matmul(ps[:, bk, :], lhsT=MT[:, kc, j * 128:(j + 1) * 128], rhs=WOUT[:, kc, hf * 512:(hf + 1) * 512],
                                                                           start=(kc == 0), stop=(kc == 7)), ["MT", "WOUT"], PS(bk))
                    if have_next:
                        kn_ = norm_p1(inext, j + 1) if j + 1 < 4 else None
                        norm_p2(j, kk)
                        kk = kn_
                    sc = 32 + 4 * j
                    gp(lambda e, sc=sc: e.memset(SS[:, sc:sc + 1], 0.0), [], ["SSp%d" % j])
                    act(lambda e, sc=sc, b0=b0: e.activation(out=JUNKB[:].rearrange("p (a c) -> p a c", a=2), in_=ps[:, b0:b0 + 2, :], func=AF.Square,
                                                           accum_out=SS[:, sc:sc + 1]), ["SSp%d" % j], PS(b0, b1) + ["JUNKB", "SSp%d" % j])
                    dve(lambda e, sc=sc: e.tensor_scalar(out=SS[:, sc + 1:sc + 2], in0=SS[:, sc:sc + 1], scalar1=1.0 / D, scalar2=EPS, op0=ALU.mult, op1=ALU.add),
                        ["SSp%d" % j], ["SSp%d" % j])
                    gp(lambda e, sc=sc: e.tensor_tensor(out=SS[:, sc + 2:sc + 3], in0=SS[:, sc + 1:sc + 2], in1=NEGH[:, 0:1], op=ALU.pow), ["SSp%d" % j, "const"], ["SSp%d" % j])
                    dve(lambda e, k=k, sc=sc, b0=b0: e.scalar_tensor_tensor(out=YT[k][:].rearrange("p (a c) -> p a c", a=2), in0=ps[:, b0:b0 + 2, :],
                                                                          scalar=SS[:, sc + 2:sc + 3], in1=gpost_bc[:].rearrange("p (a c) -> p a c", a=2),
                                                                          op0=ALU.mult, op1=ALU.mult), ["SSp%d" % j, "const"], PS(b0, b1) + ["YT%d" % k])
                    gp(lambda e, k=k, j=j: e.tensor_tensor(out=YT[k][:], in0=YT[k][:], in1=XB[i][:, j, :], op=ALU.add), ["YT%d" % k, "XB%d" % i], ["YT%d" % k])
                    spd(lambda e, k=k, j=j: e.dma_start(out=y[b, tt * 512 + j * 128:tt * 512 + (j + 1) * 128, :], in_=YT[k][:]),
                        ["YT%d" % k], ["yout%d" % k], "YT%d" % k)

            loads(0)
            build_hb(0)
            for n, (b, tt) in enumerate(tiles):
                tile_body(n, b, tt, n % 2)
            S.op("sp", lambda e: e.nop(), ["yout0", "yout1"], [])

        def gen():
            ctr.update({k: 0 for k in ctr})
            setup()
            for b in range(NB if "2" in STAGES else 1):
                if "0" in STAGES:
                    stage0(b)
                if "m" in STAGES:
                    mixer_m(b)
                if "a" in STAGES:
                    mixer_a(b)
                if "b" in STAGES:
                    mixer_b(b)
            S.barrier(["pe", "act", "dve", "pool", "sp"])
            if "p" in STAGES:
                phase_b()


        S.dry = True
        gen()
        S.dry = False
        gen()
        keys = S.finalize()
        semmap = {k: es.enter_context(nc.semaphore("s%d" % n)) for n, k in enumerate(keys)}
        with nc.Block() as block:
            @block.sync
            def _(e):
                S.emit("sp", e, semmap)

            @block.tensor
            def _(e):
                S.emit("pe", e, semmap)

            @block.scalar
            def _(e):
                S.emit("act", e, semmap)

            @block.vector
            def _(e):
                S.emit("dve", e, semmap)

            @block.gpsimd
            def _(e):
                S.emit("pool", e, semmap)
    return nc


_CACHE = {}


def kernel(x, mem, g_pre, w_in, b_merge, q_norm, k_norm, g_mem, w_mem_kv, w_br_a, w_br_b, w_br_m, w_out, g_post):
    f = lambda a: np.ascontiguousarray(np.asarray(a), dtype=np.float32)
    x = f(x); mem = f(mem)
    consts = _const_tables()
    shared = {
        "w_in": f(w_in)[0], "w_mem": f(w_mem_kv)[0], "w_br0": f(w_br_a)[0], "w_br1": f(w_br_b)[0], "w_br2": f(w_br_m)[0],
        "w_out": f(w_out)[0], "g_pre": f(g_pre), "g_mem": f(g_mem), "g_post": f(g_post),
        "bm": np.ascontiguousarray(f(b_merge)[0].reshape(24, 128).T),
        "qn": np.ascontiguousarray(np.tile(f(q_norm)[0], 2).reshape(128, 1)),
        "kn": np.ascontiguousarray(np.tile(f(k_norm)[0], 2).reshape(128, 1)),
    }
    shared.update(consts)
    if "nc" not in _CACHE:
        _CACHE["nc"] = build_program()
    nc = _CACHE["nc"]
    in_maps = []
    for c in range(NCORES):
        m = dict(shared)
        m["x"] = np.ascontiguousarray(x[c * NB:(c + 1) * NB])
        m["mem"] = np.ascontiguousarray(mem[c * NB:(c + 1) * NB])
        in_maps.append(m)
    res = run_bass_kernel_spmd(nc, in_maps, core_ids=list(range(NCORES)))
    out = np.concatenate([np.asarray(r["y"]) for r in res.results], axis=0)
    return out.astype(np.float32)
```

```python
import numpy as np
from contextlib import ExitStack
import concourse.bass as bass
import concourse.mybir as mybir
from concourse.bass_utils import run_bass_kernel_spmd

F32 = mybir.dt.float32
BF16 = mybir.dt.bfloat16
AF = mybir.ActivationFunctionType
ALU = mybir.AluOpType

NCORES = 8
NB = 2
SEQ = 4096
D = 1024
INW = 10496
AQ0, AK0, AV0, BQ0, BK0, BV0, MQ0, GA0, GB0, GM0, MG0 = 0, 1536, 3072, 4608, 5120, 5248, 5376, 5888, 6400, 6912, 7424
DILS = (1, 4, 16)
EPS = 1e-6
POOL_ELEMS = 102400
import os
STAGES = os.environ.get("MK_STAGES", "0mabp2")
ASUB = os.environ.get("MK_ASUB", "vkte")
AGR = os.environ.get("MK_AG", "012")


class Op:
    __slots__ = ("eng", "fn", "deps", "sig", "is_dma", "dma_key", "waits", "inc")

    def __init__(self, eng, fn, is_dma=False, dma_key=None):
        self.eng = eng
        self.fn = fn
        self.deps = []
        self.is_dma = is_dma
        self.dma_key = dma_key
        self.sig = None
        self.waits = []
        self.inc = None


class Res:
    __slots__ = ("w", "r")

    def __init__(self):
        self.w = None
        self.r = {}


class Sched:
    def __init__(self):
        self.ops = []
        self.res = {}
        self.dry = False
        self.last = {}
        self.last_dma = {}

    def _res(self, k):
        r = self.res.get(k)
        if r is None:
            r = self.res[k] = Res()
        return r

    def op(self, eng, fn, reads=(), writes=(), dma_key=None):
        if self.dry:
            return None
        o = Op(eng, fn, is_dma=dma_key is not None, dma_key=dma_key)
        deps = []
        for k in reads:
            r = self._res(k)
            if r.w is not None:
                deps.append((r.w, 0))
        for k in writes:
            r = self._res(k)
            if r.w is not None:
                deps.append((r.w, 1))
            for ro in r.r.values():
                deps.append((ro, 2))
        seen = set()
        for d, kind in deps:
            if id(d) in seen or d is o:
                continue
            if d.eng == eng and not d.is_dma and not o.is_dma:
                if eng == "pe" or kind != 0:
                    continue
            seen.add(id(d))
            o.deps.append(d)
        rk = ("dma", len(self.ops)) if o.is_dma else eng
        for k in reads:
            self._res(k).r[rk] = o
        for k in writes:
            r = self._res(k)
            r.w = o
            r.r = {}
        self.ops.append(o)
        if o.is_dma:
            self.last_dma[dma_key] = o
        else:
            self.last[eng] = o
        return o

    def barrier(self, engines):
        if self.dry:
            return
        prev = list(self.last.values()) + list(self.last_dma.values())
        for e in engines:
            o = Op(e, lambda eng: eng.nop())
            o.deps = [p for p in prev]
            self.ops.append(o)
            self.last[e] = o
        self.res = {}

    def finalize(self):
        need = set()
        for o in self.ops:
            for d in o.deps:
                need.add(id(d))
        cnt = {}
        for o in self.ops:
            if o.is_dma:
                k = ("dma", o.dma_key)
                cnt[k] = cnt.get(k, 0) + 16
                o.sig = (k, cnt[k])
                o.inc = (k, 16)
            elif id(o) in need:
                k = ("eng", o.eng)
                cnt[k] = cnt.get(k, 0) + 1
                o.sig = (k, cnt[k])
                o.inc = (k, 1)
        waited = {}
        for o in self.ops:
            w = waited.setdefault(o.eng, {})
            best = {}
            for d in o.deps:
                k, v = d.sig
                if w.get(k, 0) >= v:
                    continue
                if best.get(k, 0) < v:
                    best[k] = v
            for k, v in best.items():
                w[k] = v
                o.waits.append((k, v))
        return sorted(cnt.keys(), key=str)

    def emit(self, engname, eng, semmap):
        for o in self.ops:
            if o.eng != engname:
                continue
            for k, v in o.waits:
                eng.wait_ge(semmap[k], v)
            ins = o.fn(eng)
            if o.inc is not None:
                ins.then_inc(semmap[o.inc[0]], o.inc[1])


def RC(name, c0, n, gran=512):
    return ["%s.%d" % (name, t) for t in range(c0 // gran, (c0 + n - 1) // gran + 1)]


def PS(*banks):
    return ["ps%d" % b for b in banks]


def _const_tables():
    t = np.arange(SEQ, dtype=np.float64)
    d = np.arange(128)
    dd = d % 64
    fA = dd % 32
    invA = 10000.0 ** (-fA.astype(np.float64) * 2.0 / 64.0)
    angA = invA[:, None] * t[None, :]
    cosA = np.cos(angA)
    sinA = np.sin(angA) * np.where(dd < 32, -1.0, 1.0)[:, None]
    PA = np.zeros((128, 128), np.float32)
    for i in range(128):
        part = i + 32 if (i % 64) < 32 else i - 32
        PA[part, i] = 1.0
    j = dd % 32
    fB = j % 16
    invB = 10000.0 ** (-fB.astype(np.float64) * 2.0 / 32.0)
    rowp = np.floor(t / 64.0)
    colp = t % 64
    posB = np.where((dd // 32)[:, None] == 0, rowp[None, :], colp[None, :])
    angB = invB[:, None] * posB
    cosB = np.cos(angB)
    sinB = np.sin(angB) * np.where(j < 16, -1.0, 1.0)[:, None]
    PBm = np.zeros((128, 128), np.float32)
    for i in range(128):
        part = i + 16 if (i % 32) < 16 else i - 16
        PBm[part, i] = 1.0
    BD = np.zeros((128, 128), np.float32)
    BD[0:64, 0:64] = 1.0
    BD[64:128, 64:128] = 1.0
    ii = np.arange(128)[:, None]
    cc = np.arange(128)[None, :]
    M1 = (ii >= cc).astype(np.float32)
    M2 = (ii <= cc).astype(np.float32)
    MF = np.concatenate([M1, M2] * 4, axis=1)
    ident = np.eye(128, dtype=np.float32)
    f = lambda a: np.ascontiguousarray(a, dtype=np.float32)
    return dict(cosA=f(cosA), sinA=f(sinA), cosB=f(cosB), sinB=f(sinB), PA=f(PA), PB=f(PBm), BD=f(BD),
                MF=f(MF), ident=f(ident))


def build_program():
    nc = bass.Bass("TRN2", target_bir_lowering=False)
    din = lambda n, s: nc.dram_tensor(n, s, F32, kind="ExternalInput").ap()
    x = din("x", [NB, SEQ, D])
    mem = din("mem", [NB, 256, D])
    w_in = din("w_in", [D, INW])
    w_mem = din("w_mem", [D, 1024])
    w_br = [din("w_br%d" % i, [512, D]) for i in range(3)]
    w_out = din("w_out", [D, D])
    g_pre = din("g_pre", [1, D])
    g_mem = din("g_mem", [1, D])
    g_post = din("g_post", [1, D])
    bm = din("bm", [128, 24])
    qn = din("qn", [128, 1])
    kn = din("kn", [128, 1])
    tab = {k: din(k, [128, SEQ]) for k in ("cosA", "sinA", "cosB", "sinB")}
    cPA = din("PA", [128, 128])
    cPB = din("PB", [128, 128])
    cBD = din("BD", [128, 128])
    cMF = din("MF", [128, 1024])
    cID = din("ident", [128, 128])
    y = nc.dram_tensor("y", [NB, SEQ, D], F32, kind="ExternalOutput").ap()
    uscr = nc.dram_tensor("uscr", [NB, 3, 4, 128, SEQ], BF16).ap()

    S = Sched()
    with ExitStack() as es:
        pool = es.enter_context(nc.sbuf_tensor("pool", [128, POOL_ELEMS], BF16))
        ps = es.enter_context(nc.psum_tensor("ps", [128, 8, 512], F32))
        psT = ps[:, 7, :].bitcast(BF16)

        cur = [0]

        def alloc(n, dt=BF16):
            ne = n if dt == BF16 else 2 * n
            o = cur[0]
            cur[0] += ne
            assert cur[0] <= POOL_ELEMS, ("sbuf overflow", cur[0])
            a = pool[:, o:o + ne]
            return a.bitcast(F32) if dt == F32 else a

        ident = alloc(128); PA = alloc(128); PBm = alloc(128); BD = alloc(128); TWOS = alloc(128)
        MF = alloc(1024)
        gpre_bc = alloc(1024, F32); gpost_bc = alloc(1024, F32); gmem_bc = alloc(1024, F32)
        bmh = alloc(24, F32); qn_t = alloc(1, F32); kn_t = alloc(1, F32)
        NEGH = alloc(512, F32)
        EPSB = alloc(2, F32)
        SS = alloc(64, F32)
        phase_base = cur[0]

        hT = alloc(8 * SEQ).rearrange("p (k t) -> p k t", k=8)
        cosA = alloc(SEQ); sinA = alloc(SEQ)
        R1a = alloc(8192); R1b = alloc(8192)
        ACCE = R1a.bitcast(F32); ACCO = R1b.bitcast(F32)
        cosB = R1a[:, 0:SEQ]; sinB = R1a[:, SEQ:2 * SEQ]
        mnT = R1b[:, 4096:6144].rearrange("p (k t) -> p k t", k=8)
        KmT = R1b[:, 6144:7168].rearrange("p (h t) -> p h t", h=4)
        Vm = R1b[:, 7168:8192].rearrange("p (j c) -> p j c", j=2)
        QT = alloc(SEQ); KT = alloc(SEQ)
        XT = [R1a[:, 2048 * k:2048 * (k + 1)].bitcast(F32) for k in range(4)]
        XN = [R1b[:, 1024 * k:1024 * (k + 1)] for k in range(4)]
        JUNK = KT[:, 2048:3072]
        R1A = ["R1a.%d" % k for k in range(4)]
        R1B = ["R1b.%d" % k for k in range(4)] + ["R1b.m"]
        VTraw = alloc(32 * 192)
        VT = VTraw.rearrange("p (j c) -> p j c", c=192)
        VT4 = VTraw.rearrange("p (j a c) -> p j a c", a=3, c=64)
        WM = VTraw[:, 0:4096].rearrange("p (k c) -> p k c", k=8)
        NWS = 4
        WS = [alloc(1024).rearrange("p (k c) -> p k c", k=8) for _ in range(NWS)]
        QRAW = [alloc(512) for _ in range(2)]
        T0 = [alloc(512, F32) for _ in range(2)]
        T1 = [alloc(512, F32) for _ in range(2)]
        T2 = [alloc(512, F32) for _ in range(2)]
        OO = [alloc(512, F32) for _ in range(2)]
        PT = [alloc(1024) for _ in range(2)]
        SQ = [alloc(512) for _ in range(2)]
        RS = [alloc(512, F32) for _ in range(2)]
        UST = [alloc(512) for _ in range(2)]
        endA = cur[0]

        cur[0] = phase_base
        WMG = alloc(8 * 3072).rearrange("p (k c) -> p k c", k=8)
        WBR = [alloc(4 * 1024).rearrange("p (k c) -> p k c", k=4) for _ in range(3)]
        WOUT = alloc(8 * 1024).rearrange("p (k c) -> p k c", k=8)
        XB = [alloc(4 * 1024, F32).rearrange("p (j c) -> p j c", j=4) for _ in range(2)]
        HB = alloc(8 * 512).rearrange("p (k t) -> p k t", k=8)
        XNB = [alloc(1024) for _ in range(2)]
        UB = [alloc(12 * 512).rearrange("p (r k t) -> p r k t", r=3, k=4) for _ in range(2)]
        MT = alloc(8 * 512).rearrange("p (k t) -> p k t", k=8)
        TGB = [alloc(512, F32) for _ in range(2)]
        MACC = alloc(512, F32)
        TMPB = [alloc(512, F32)]
        YT = [alloc(1024, F32) for _ in range(2)]
        JUNKB = alloc(1024)
        endB = cur[0]

        pe = lambda fn, r=(), w=(): S.op("pe", fn, r, w)
        act = lambda fn, r=(), w=(): S.op("act", fn, r, w)
        dve = lambda fn, r=(), w=(): S.op("dve", fn, r, w)
        gp = lambda fn, r=(), w=(): S.op("pool", fn, r, w)
        spd = lambda fn, r, w, key: S.op("sp", fn, r, w, dma_key=key)
        gpd = lambda fn, r, w, key: S.op("pool", fn, r, w, dma_key=key)

        w_in_v = w_in.rearrange("(k p) c -> p k c", p=128)
        w_mem_v = w_mem.rearrange("(k p) c -> p k c", p=128)
        w_out_v = w_out.rearrange("(k p) c -> p k c", p=128)
        w_br_v = [w.rearrange("(k p) c -> p k c", p=128) for w in w_br]

        class WStream:
            def __init__(self):
                self.specs = []
                self.i = 0
                self.issued = 0

            def _issue(self, k):
                slot = k % NWS
                for (c0, n, d0) in self.specs[k]:
                    srcv = w_in_v
                    if c0 < 0:
                        srcv = w_mem_v
                        c0 = -c0 - 1
                    gpd(lambda e, slot=slot, c0=c0, n=n, d0=d0, srcv=srcv: e.dma_start(out=WS[slot][:, :, d0:d0 + n], in_=srcv[:, :, c0:c0 + n]),
                        [], ["WS%d" % slot], "WS%d" % slot)

            def next(self, pieces):
                if S.dry:
                    self.specs.append(pieces)
                    return 0
                k = self.i
                self.i += 1
                while self.issued < min(len(self.specs), k + NWS - 1):
                    self._issue(self.issued)
                    self.issued += 1
                return k % NWS

        W = WStream()

        def tokslice(g, p0, n):
            d = DILS[g]
            if d == 1:
                return slice(p0, p0 + n)
            L = SEQ // d
            r, m0 = p0 // L, p0 % L
            assert m0 + n <= L
            s0 = m0 * d + r
            return slice(s0, s0 + (n - 1) * d + 1, d)

        HT_ALL = ["hT.%d" % t for t in range(8)]

        def ht_res(g, p0, n):
            return RC("hT", p0, n) if DILS[g] == 1 else HT_ALL

        def proj_fm(slot, g, p0, n, bank, col0=0):
            sl = tokslice(g, p0, n)
            for kc in range(8):
                pe(lambda e, kc=kc: e.matmul(ps[:, bank, col0:col0 + n], lhsT=WS[slot][:, kc, :], rhs=hT[:, kc, sl],
                                             start=(kc == 0), stop=(kc == 7)),
                   ["WS%d" % slot] + ht_res(g, p0, n), PS(bank))

        ctr = {"rope": 0, "pt": 0, "ep": 0, "x": 0, "pj": 0}

        def rope_p1(src, src_res, src_is_psum, cs, sn, perm, sl, dst, dst_res, n, tres=("tabs",), dsplit=1, rbank=2, copy_eng="act"):
            i = ctr["rope"] % 2
            ctr["rope"] += 1
            rw = (lambda r, w: ([], r + w)) if src_is_psum else (lambda r, w: (r, w))
            r_, w_ = rw(src_res, ["QRAW%d" % i])
            if copy_eng == "act":
                act(lambda e: e.activation(out=QRAW[i][:, 0:n], in_=src, func=AF.Copy), r_, w_)
            else:
                gp(lambda e: e.tensor_copy(out=QRAW[i][:, 0:n], in_=src), r_, w_)
            r_, w_ = rw(src_res, ["T1_%d" % i])
            dve(lambda e: e.tensor_tensor(out=T1[i][:, 0:n], in0=src, in1=cs[:, sl], op=ALU.mult), r_ + list(tres), w_)
            return (i, sn, perm, sl, dst, dst_res, n, tres, dsplit, rbank)

        def rope_p2(st):
            i, sn, perm, sl, dst, dst_res, n, tres, dsplit, rbank = st
            pe(lambda e: e.matmul(ps[:, rbank, 0:n], lhsT=perm, rhs=QRAW[i][:, 0:n], start=True, stop=True),
               ["QRAW%d" % i, "const"], PS(rbank))
            dve(lambda e: e.tensor_tensor(out=T2[i][:, 0:n], in0=ps[:, rbank, 0:n], in1=sn[:, sl], op=ALU.mult),
                list(tres), PS(rbank) + ["T2_%d" % i])
            a0, a1 = T1[i][:, 0:n], T2[i][:, 0:n]
            if dsplit > 1:
                a0 = a0.rearrange("p (m r) -> p m r", r=dsplit)
                a1 = a1.rearrange("p (m r) -> p m r", r=dsplit)
            gp(lambda e: e.tensor_tensor(out=dst, in0=a0, in1=a1, op=ALU.add),
               ["T1_%d" % i, "T2_%d" % i], dst_res)

        def rope_tile(*a, **k):
            rope_p2(rope_p1(*a, **k))

        def gate_part(slot, tt, bank):
            i = ctr["ep"] % 2
            ctr["ep"] += 1
            proj_fm(slot, 0, tt * 512, 512, bank)
            act(lambda e: e.activation(out=T0[i][:], in_=ps[:, bank, :], func=AF.Tanh, scale=0.5), [], PS(bank) + ["T0_%d" % i])
            dve(lambda e: e.scalar_tensor_tensor(out=T1[i][:], in0=T0[i][:], scalar=1.0, in1=ps[:, bank, :], op0=ALU.add, op1=ALU.mult),
                ["T0_%d" % i], PS(bank) + ["T1_%d" % i])
            return i

        def gate_tail(i, b, br, chunk, tt, num_e, num_o, l_e, l_o, src_res, src_psum):
            rr, ww = ([], list(src_res)) if src_psum else (list(src_res), [])
            if num_o is None:
                dve(lambda e: e.reciprocal(out=T2[i][:], in_=l_e), rr, ww + ["T2_%d" % i])
                dve(lambda e: e.tensor_tensor(out=OO[i][:], in0=num_e, in1=T2[i][:], op=ALU.mult), rr + ["T2_%d" % i], ww + ["OO%d" % i])
            else:
                dve(lambda e: e.reciprocal(out=T2[i][0:64, :], in_=l_e), rr, ww + ["T2_%d" % i])
                dve(lambda e: e.reciprocal(out=T2[i][64:128, :], in_=l_o), rr, ww + ["T2_%d" % i])
                dve(lambda e: e.tensor_tensor(out=OO[i][0:64, :], in0=num_e, in1=T2[i][0:64, :], op=ALU.mult), rr + ["T2_%d" % i], ww + ["OO%d" % i])
                dve(lambda e: e.tensor_tensor(out=OO[i][64:128, :], in0=num_o, in1=T2[i][64:128, :], op=ALU.mult), rr + ["T2_%d" % i], ww + ["OO%d" % i])
            gp(lambda e: e.tensor_tensor(out=UST[i][:], in0=OO[i][:], in1=T1[i][:], op=ALU.mult), ["OO%d" % i, "T1_%d" % i], ["UST%d" % i])
            spd(lambda e: e.dma_start(out=uscr[b, br, chunk, :, tt * 512:(tt + 1) * 512], in_=UST[i][:]),
                ["UST%d" % i], [], "UST%d" % i)

        def norm_rows(src_dram, gbc, i, sscol):
            xr = ["R1a.%d" % i]
            nr = ["R1b.%d" % i]
            spd(lambda e: e.dma_start(out=XT[i][:], in_=src_dram), [], xr, "XT%d" % i)
            gp(lambda e: e.memset(SS[:, sscol:sscol + 1], 0.0), [], ["SS%d" % sscol])
            act(lambda e: e.activation(out=JUNK[:], in_=XT[i][:], func=AF.Square, accum_out=SS[:, sscol:sscol + 1]),
                xr + ["SS%d" % sscol], ["KT.4", "KT.5", "SS%d" % sscol])
            dve(lambda e: e.tensor_scalar(out=SS[:, sscol + 1:sscol + 2], in0=SS[:, sscol:sscol + 1], scalar1=1.0 / D, scalar2=EPS,
                                          op0=ALU.mult, op1=ALU.add), ["SS%d" % sscol], ["SS%d" % sscol])
            gp(lambda e: e.tensor_tensor(out=SS[:, sscol + 2:sscol + 3], in0=SS[:, sscol + 1:sscol + 2], in1=NEGH[:, 0:1], op=ALU.pow),
               ["SS%d" % sscol, "const"], ["SS%d" % sscol])
            dve(lambda e: e.scalar_tensor_tensor(out=XN[i][:], in0=XT[i][:], scalar=SS[:, sscol + 2:sscol + 3], in1=gbc[:],
                                                 op0=ALU.mult, op1=ALU.mult), xr + ["SS%d" % sscol, "const"], nr)
            return nr

        def transpose_rows(i, nr, dst3, dst_res):
            for kc in range(8):
                pe(lambda e, kc=kc: e.transpose(out=psT[:, kc * 128:(kc + 1) * 128], in_=XN[i][:, kc * 128:(kc + 1) * 128], identity=ident[:]),
                   nr + ["const"], PS(7))
            act(lambda e: e.activation(out=dst3, in_=psT.rearrange("p (k t) -> p k t", k=8), func=AF.Copy), [], PS(7) + dst_res)

        def setup():
            for (dst, src, nm) in ((ident, cID, "cid"), (PA, cPA, "cpa"), (PBm, cPB, "cpb"), (BD, cBD, "cbd"), (MF, cMF, "cmf")):
                gpd(lambda e, dst=dst, src=src: e.dma_start(out=dst[:], in_=src), [], ["const"], nm)
            for (dst, src, nm) in ((gpre_bc, g_pre, "cg1"), (gpost_bc, g_post, "cg2"), (gmem_bc, g_mem, "cg3")):
                spd(lambda e, dst=dst, src=src: e.dma_start(out=dst[:], in_=src.partition_broadcast(128)), [], ["const"], nm)
            spd(lambda e: e.dma_start(out=bmh[:], in_=bm), [], ["const"], "cbm")
            spd(lambda e: e.dma_start(out=qn_t[:], in_=qn), [], ["const"], "cqn")
            spd(lambda e: e.dma_start(out=kn_t[:], in_=kn), [], ["const"], "ckn")
            gp(lambda e: e.memset(TWOS[:], 2.0), [], ["const"])
            gp(lambda e: e.memset(NEGH[:], -0.5), [], ["const"])
            gp(lambda e: e.memset(EPSB[:], EPS), [], ["const"])
            dve(lambda e: e.tensor_scalar(out=bmh[:], in0=bmh[:], scalar1=0.5, scalar2=None, op0=ALU.mult), ["const"], ["const"])
            gpd(lambda e: e.dma_start(out=cosA[:], in_=tab["cosA"]), [], ["tabs"], "tcA")
            gpd(lambda e: e.dma_start(out=sinA[:], in_=tab["sinA"]), [], ["tabs"], "tsA")

        def stage0(b):
            pend = None
            for tt in range(32):
                i = tt % 4
                nr = norm_rows(x[b, tt * 128:(tt + 1) * 128, :], gpre_bc, i, 4 * (tt % 8))
                if pend is not None:
                    transpose_rows(*pend)
                pend = (i, nr, hT[:, :, tt * 128:(tt + 1) * 128], ["hT.%d" % (tt // 4)])
            transpose_rows(*pend)

        def mixer_m(b):
            for j in range(2):
                nr = norm_rows(mem[b, j * 128:(j + 1) * 128, :], gmem_bc, j, 32 + 4 * j)
                transpose_rows(j, nr, mnT[:, :, j * 128:(j + 1) * 128], ["R1b.m"])
            gpd(lambda e: e.dma_start(out=WM[:], in_=w_mem_v[:, :, 512:1024]), [], ["VT"], "WM")
            for j in range(2):
                for kc in range(8):
                    pe(lambda e, kc=kc, j=j: e.matmul(ps[:, 0, :], lhsT=mnT[:, kc, j * 128:(j + 1) * 128], rhs=WM[:, kc, :],
                                                       start=(kc == 0), stop=(kc == 7)), ["R1b.m", "VT"], PS(0))
                act(lambda e, j=j: e.activation(out=Vm[:, j, :], in_=ps[:, 0, :], func=AF.Copy), [], PS(0) + ["R1b.m"])
            for h in range(4):
                slot = W.next([(-(h * 128) - 1, 128, 0)])
                for kc in range(8):
                    pe(lambda e, kc=kc, slot=slot: e.matmul(ps[:, 1, 0:256], lhsT=WS[slot][:, kc, :], rhs=mnT[:, kc, :],
                                                             start=(kc == 0), stop=(kc == 7)), ["WS%d" % slot, "R1b.m"], PS(1))
                act(lambda e, h=h: e.activation(out=KmT[:, h, :], in_=ps[:, 1, 0:256], func=AF.Copy), [], PS(1) + ["R1b.m"])
            scale = 128.0 ** -0.5
            for h in range(4):
                slot = W.next([(MQ0 + h * 128, 128, 0)])
                for tt in range(8):
                    bank = tt % 2
                    proj_fm(slot, 0, tt * 512, 512, bank)
                    act(lambda e, tt=tt, bank=bank: e.activation(out=QT[:, tt * 512:(tt + 1) * 512], in_=ps[:, bank, :], func=AF.Copy),
                        [], PS(bank) + ["QT.%d" % tt])
                gslot = W.next([(GM0 + h * 128, 128, 0)])
                pend = None
                for tt in range(8):
                    i = ctr["pt"] % 2
                    ctr["pt"] += 1
                    ab, lb = (5, 6) if tt % 2 == 0 else (0, 1)
                    gb_ = 2 if tt % 2 == 0 else 7
                    for j in range(2):
                        pe(lambda e, j=j, tt=tt, h=h: e.matmul(ps[:, 3 + j, :], lhsT=KmT[:, h, j * 128:(j + 1) * 128], rhs=QT[:, tt * 512:(tt + 1) * 512],
                                                                 start=True, stop=True), ["R1b.m", "QT.%d" % tt], PS(3 + j))
                    gi = gate_part(gslot, tt, gb_)
                    act(lambda e, i=i: e.activation(out=PT[i][:].rearrange("p (a c) -> p a c", a=2), in_=ps[:, 3:5, :], func=AF.Exp, scale=scale),
                        [], PS(3, 4) + ["PT%d" % i])
                    for j in range(2):
                        pe(lambda e, j=j, i=i, h=h, ab=ab: e.matmul(ps[:, ab, :], lhsT=Vm[:, j, h * 128:(h + 1) * 128], rhs=PT[i][:, j * 512:(j + 1) * 512],
                                                                     start=(j == 0), stop=(j == 1)), ["R1b.m", "PT%d" % i], PS(ab))
                    for j in range(2):
                        pe(lambda e, j=j, i=i, lb=lb: e.matmul(ps[:, lb, :], lhsT=TWOS[:], rhs=PT[i][:, j * 512:(j + 1) * 512],
                                                                start=(j == 0), stop=(j == 1)), ["const", "PT%d" % i], PS(lb))
                    if pend is not None:
                        gate_tail(*pend)
                    pend = (gi, b, 2, h, tt, ps[:, ab, :], None, ps[:, lb, :], None, PS(ab, lb), True)
                gate_tail(*pend)

        def a_units(g):
            d = DILS[g]
            L = SEQ // d
            batches = []
            for r in range(d):
                pc = r * L
                units = [(pc, 64, [pc // 128], 1)]
                for a in range(L // 128 - 1):
                    units.append((pc + 128 * a + 64, 128, [pc // 128 + a, pc // 128 + a + 1], 0))
                units.append((pc + L - 64, 64, [(pc + L) // 128 - 1], 2))
                curb = None
                for u in units:
                    if curb is None or (u[0] + u[1] - curb[0]) > 512:
                        curb = [u[0], 0, []]
                        batches.append(curb)
                    curb[2].append(u)
                    curb[1] = u[0] + u[1] - curb[0]
            return batches

        def mixer_a_job(b, g, hp):
            d = DILS[g]
            L = SEQ // d
            ntile = min(512, L)
            cq = g * 512 + hp * 128
            slot = W.next([(AV0 + cq, 128, 0)])
            for j4 in range(8 if "v" in ASUB else 0):
                bank = j4 % 2
                for jj in range(4):
                    j = j4 * 4 + jj
                    sl = tokslice(g, 128 * j, 128)
                    for kc in range(8):
                        pe(lambda e, kc=kc, sl=sl, jj=jj, bank=bank, slot=slot: e.matmul(ps[:, bank, jj * 128:(jj + 1) * 128], lhsT=hT[:, kc, sl], rhs=WS[slot][:, kc, :],
                                                                                  start=(kc == 0), stop=(kc == 7)),
                           ["WS%d" % slot] + ht_res(g, 128 * j, 128), PS(bank))
                act(lambda e, j4=j4, bank=bank: e.activation(out=VT4[:, j4 * 4:(j4 + 1) * 4, 0:3:2, :],
                                                             in_=ps[:, bank, :].rearrange("p (j a c) -> p j a c", j=4, a=2), func=AF.Copy),
                    [], PS(bank) + ["VT"])
            for (c0, dstT, nm) in ((AK0 + cq, KT, "KT"), (AQ0 + cq, QT, "QT")):
                slot = W.next([(c0, 128, 0)])
                pend = None
                for tt in range(8 if "k" in ASUB else 0):
                    bank = tt % 2
                    proj_fm(slot, 0, tt * 512, 512, bank)
                    if d == 1:
                        dst = dstT[:, tt * 512:(tt + 1) * 512]
                        dres = RC(nm, tt * 512, 512)
                    else:
                        dst = dstT[:, :].rearrange("p (r m) -> p m r", r=d)[:, tt * 512 // d:(tt + 1) * 512 // d, :]
                        dres = RC(nm, 0, SEQ)
                    st = rope_p1(ps[:, bank, :], PS(bank), True, cosA, sinA, PA[:], slice(tt * 512, (tt + 1) * 512),
                                 dst, dres, 512, dsplit=d)
                    if pend is not None:
                        rope_p2(pend)
                    pend = st
                if pend is not None:
                    rope_p2(pend)
            if "t" not in ASUB:
                return
            allu = []
            for bi, (plo, width, units) in enumerate(a_units(g)):
                accs = (5, 6) if bi % 2 == 0 else (0, 1)
                for ui, u in enumerate(units):
                    allu.append((plo, width, accs, u, ui == len(units) - 1))
            supers = [allu[k:k + 2] for k in range(0, len(allu), 2)]
            OFFS = {0: [0, 128], 1: [192], 2: [0]}

            def emit_qk(sidx):
                banks = (3, 4) if sidx % 2 == 0 else (2, 7)
                for si, (plo, width, accs, (q0, nq, ktiles, kind), last) in enumerate(supers[sidx]):
                    for hh in range(2):
                        rows = slice(64 * hh, 64 * hh + 64)
                        for t, kt in enumerate(ktiles):
                            co = si * 256 + OFFS[kind][t]
                            pe(lambda e, rows=rows, kt=kt, co=co, q0=q0, nq=nq, bk=banks[hh]: e.matmul(ps[:, bk, co:co + nq], lhsT=KT[rows, kt * 128:(kt + 1) * 128],
                                                                                                 rhs=QT[rows, q0:q0 + nq], start=True, stop=True),
                               RC("KT", kt * 128, 128) + RC("QT", q0, nq), PS(banks[hh]))

            def emit_rest(sidx):
                i = sidx % 2
                banks = (3, 4) if i == 0 else (2, 7)
                bsl = slice(3, 5) if i == 0 else slice(2, 8, 5)
                act(lambda e: e.activation(out=PT[i][:].rearrange("p (h c) -> p h c", h=2), in_=ps[:, bsl, :], func=AF.Exp, scale=0.125),
                    [], PS(*banks) + ["PT%d" % i])
                dve(lambda e: e.tensor_tensor(out=PT[i][:], in0=PT[i][:], in1=MF[:], op=ALU.mult), ["PT%d" % i, "const"], ["PT%d" % i])
                for si, (plo, width, accs, (q0, nq, ktiles, kind), last) in enumerate(supers[sidx]):
                    c0 = q0 - plo
                    for hh in range(2):
                        accb = accs[hh]
                        for t, kt in enumerate(ktiles):
                            co = hh * 512 + si * 256 + OFFS[kind][t]
                            pe(lambda e, hh=hh, accb=accb, kt=kt, co=co, c0=c0, nq=nq, t=t, nk=len(ktiles):
                               e.matmul(ps[:, accb, c0:c0 + nq], lhsT=VT[:, kt, 64 * hh:64 * hh + 128], rhs=PT[i][:, co:co + nq],
                                        start=(t == 0), stop=(t == nk - 1)),
                               ["VT", "PT%d" % i], PS(accb))
                    if last:
                        ae, ao = accs
                        sl = tokslice(g, plo, width)
                        if g == 0:
                            act(lambda e, sl=sl, ae=ae, width=width: e.activation(out=ACCE[:, sl], in_=ps[:, ae, 0:width], func=AF.Copy), [], PS(ae) + R1A)
                            act(lambda e, sl=sl, ao=ao, width=width: e.activation(out=ACCO[:, sl], in_=ps[:, ao, 0:width], func=AF.Copy), [], PS(ao) + R1B)
                        else:
                            dve(lambda e, sl=sl, ae=ae, width=width: e.tensor_tensor(out=ACCE[:, sl], in0=ps[:, ae, 0:width], in1=ACCE[:, sl], op=ALU.add),
                                R1A, PS(ae) + R1A)
                            dve(lambda e, sl=sl, ao=ao, width=width: e.tensor_tensor(out=ACCO[:, sl], in0=ps[:, ao, 0:width], in1=ACCO[:, sl], op=ALU.add),
                                R1B, PS(ao) + R1B)

            emit_qk(0)
            for sidx in range(len(supers)):
                if sidx + 1 < len(supers):
                    emit_qk(sidx + 1)
                emit_rest(sidx)

        def mixer_a(b):
            gp(lambda e: e.memset(VT[:, :, 64:128], 2.0), [], ["VT"])
            for hp in range(4):
                for g in range(3):
                    if str(g) in AGR:
                        mixer_a_job(b, g, hp)
                if "e" not in ASUB:
                    continue
                gslot = W.next([(GA0 + hp * 128, 128, 0)])
                for m in range(4):
                    gis = [gate_part(gslot, 2 * m + q, 2 + q) for q in range(2)]
                    for q in range(2):
                        tt = 2 * m + q
                        i = gis[q]
                        cs = slice(tt * 512, (tt + 1) * 512)
                        act(lambda e, i=i, cs=cs: e.activation(out=T2[i][0:64, :], in_=ACCE[64:128, cs], func=AF.Ln), R1A, ["T2_%d" % i])
                        act(lambda e, i=i, cs=cs: e.activation(out=T2[i][64:128, :], in_=ACCO[0:64, cs], func=AF.Ln), R1B, ["T2_%d" % i])
                        act(lambda e, i=i: e.activation(out=T2[i][:], in_=T2[i][:], func=AF.Exp, scale=-1.0), ["T2_%d" % i], ["T2_%d" % i])
                        dve(lambda e, i=i, cs=cs: e.tensor_tensor(out=OO[i][0:64, :], in0=ACCE[0:64, cs], in1=T2[i][0:64, :], op=ALU.mult),
                            R1A + ["T2_%d" % i], ["OO%d" % i])
                        dve(lambda e, i=i, cs=cs: e.tensor_tensor(out=OO[i][64:128, :], in0=ACCO[64:128, cs], in1=T2[i][64:128, :], op=ALU.mult),
                            R1B + ["T2_%d" % i], ["OO%d" % i])
                        gp(lambda e, i=i: e.tensor_tensor(out=UST[i][:], in0=OO[i][:], in1=T1[i][:], op=ALU.mult), ["OO%d" % i, "T1_%d" % i], ["UST%d" % i])
                        spd(lambda e, i=i, tt=tt, hp=hp: e.dma_start(out=uscr[b, 0, hp, :, tt * 512:(tt + 1) * 512], in_=UST[i][:]),
                            ["UST%d" % i], [], "UST%d" % i)

        def qk_norm_rope(slot, gain, dstT, nm):
            PB4 = (0, 1, 4, 5)
            sts = {}

            def stA(tt):
                bank = PB4[tt % 4]
                i = tt % 2
                proj_fm(slot, 0, tt * 512, 512, bank)
                act(lambda e: e.activation(out=SQ[i][:], in_=ps[:, bank, :], func=AF.Square), [], PS(bank) + ["SQ%d" % i])

            def stB(tt):
                bank = PB4[tt % 4]
                i = tt % 2
                pe(lambda e: e.matmul(ps[:, 2, :], lhsT=BD[:], rhs=SQ[i][:], start=True, stop=True), ["SQ%d" % i, "const"], PS(2))
                act(lambda e: e.activation(out=RS[i][:], in_=ps[:, 2, :], func=AF.Ln, scale=1.0 / 64.0, bias=EPSB[:, 0:1]), ["const"], PS(2) + ["RS%d" % i])
                act(lambda e: e.activation(out=RS[i][:], in_=RS[i][:], func=AF.Exp, scale=-0.5), ["RS%d" % i], ["RS%d" % i])
                dve(lambda e: e.scalar_tensor_tensor(out=T0[i][:], in0=ps[:, bank, :], scalar=gain[:, 0:1], in1=RS[i][:],
                                                     op0=ALU.mult, op1=ALU.mult), ["RS%d" % i, "const"], PS(bank) + ["T0_%d" % i])
                sl = slice(tt * 512, (tt + 1) * 512)
                sts[tt] = rope_p1(T0[i][:], ["T0_%d" % i], False, cosB, sinB, PBm[:], sl, dstT[:, sl], RC(nm, tt * 512, 512), 512,
                                  tres=R1A, rbank=3, copy_eng="pool")

            for k in range(10):
                if k < 8:
                    stA(k)
                if 0 <= k - 1 < 8:
                    stB(k - 1)
                if 0 <= k - 2 < 8:
                    rope_p2(sts[k - 2])

        def mixer_b(b):
            gpd(lambda e: e.dma_start(out=cosB, in_=tab["cosB"]), [], R1A, "tcB")
            gpd(lambda e: e.dma_start(out=sinB, in_=tab["sinB"]), [], R1A, "tsB")
            for kv in range(2):
                slot = W.next([(BV0 + kv * 64, 64, 0), (BV0 + kv * 64, 64, 64)])
                for j4 in range(8):
                    bank = j4 % 2
                    for jj in range(4):
                        j = j4 * 4 + jj
                        for kc in range(8):
                            pe(lambda e, kc=kc, j=j, jj=jj, bank=bank, slot=slot: e.matmul(ps[:, bank, jj * 128:(jj + 1) * 128], lhsT=hT[:, kc, j * 128:(j + 1) * 128],
                                                                                 rhs=WS[slot][:, kc, :], start=(kc == 0), stop=(kc == 7)),
                               ["WS%d" % slot, "hT.%d" % (j // 4)], PS(bank))
                    act(lambda e, j4=j4, bank=bank: e.activation(out=VT4[:, j4 * 4:(j4 + 1) * 4, 0:3:2, :],
                                                                 in_=ps[:, bank, :].rearrange("p (j a c) -> p j a c", j=4, a=2), func=AF.Copy),
                        [], PS(bank) + ["VT"])
                slot = W.next([(BK0 + kv * 64, 64, 0), (BK0 + kv * 64, 64, 64)])
                qk_norm_rope(slot, kn_t, KT, "KT")
                for hq in range(2):
                    c = kv * 2 + hq
                    slot = W.next([(BQ0 + c * 128, 128, 0)])
                    qk_norm_rope(slot, qn_t, QT, "QT")
                    gslot = W.next([(GB0 + c * 128, 128, 0)])
                    gnext = gate_part(gslot, 0, 5)
                    for tt in range(8):
                        ae, ao = (4, 5) if tt % 2 == 0 else (6, 7)
                        qs = slice(tt * 512, (tt + 1) * 512)
                        gi = gnext

                        def qk(kc, u):
                            for hh in range(2):
                                rows = slice(64 * hh, 64 * hh + 64)
                                pe(lambda e, rows=rows, hh=hh, kc=kc, u=u, qs=qs: e.matmul(ps[:, 2 * u + hh, :], lhsT=KT[rows, kc * 128:(kc + 1) * 128], rhs=QT[rows, qs],
                                                                                  start=True, stop=True),
                                   RC("KT", kc * 128, 128) + ["QT.%d" % tt], PS(2 * u + hh))
                        qk(0, 0)
                        for kc in range(32):
                            u = kc % 2
                            i = kc % 2
                            if kc + 1 < 32:
                                qk(kc + 1, (kc + 1) % 2)
                            act(lambda e, u=u, i=i: e.activation(out=PT[i][:].rearrange("p (a c) -> p a c", a=2), in_=ps[:, 2 * u:2 * u + 2, :],
                                                                 func=AF.Exp, scale=0.125), [], PS(2 * u, 2 * u + 1) + ["PT%d" % i])
                            pe(lambda e, kc=kc, i=i, ae=ae: e.matmul(ps[:, ae, :], lhsT=VT[:, kc, 0:128], rhs=PT[i][:, 0:512], start=(kc == 0), stop=(kc == 31)),
                               ["VT", "PT%d" % i], PS(ae))
                            pe(lambda e, kc=kc, i=i, ao=ao: e.matmul(ps[:, ao, :], lhsT=VT[:, kc, 64:192], rhs=PT[i][:, 512:1024], start=(kc == 0), stop=(kc == 31)),
                               ["VT", "PT%d" % i], PS(ao))
                        if tt + 1 < 8:
                            gnext = gate_part(gslot, tt + 1, 7 if tt % 2 == 0 else 5)
                        gate_tail(gi, b, 1, c, tt, ps[0:64, ae, :], ps[64:128, ao, :], ps[64:128, ae, :], ps[0:64, ao, :],
                                  PS(ae, ao), True)

        def phase_b():
            for k3 in range(3):
                gpd(lambda e, k3=k3: e.dma_start(out=WMG[:, :, k3 * 1024:(k3 + 1) * 1024], in_=w_in_v[:, :, MG0 + k3 * 1024:MG0 + (k3 + 1) * 1024]),
                    [], ["WMG"], "WMG")
                gpd(lambda e, k3=k3: e.dma_start(out=WBR[k3][:], in_=w_br_v[k3]), [], ["WBR"], "WBR%d" % k3)
            gpd(lambda e: e.dma_start(out=WOUT[:], in_=w_out_v), [], ["WOUT"], "WOUT")
            tiles = [(b, tt) for b in range(NB) for tt in range(8)]

            def loads(n):
                b, tt = tiles[n]
                i = n % 2
                spd(lambda e: e.dma_start(out=XB[i][:], in_=x[b, tt * 512:(tt + 1) * 512, :].rearrange("(j p) c -> p j c", p=128)),
                    [], ["XB%d" % i], "XB%d" % i)
                for br in range(3):
                    spd(lambda e, br=br: e.dma_start(out=UB[i][:, br, :, :], in_=uscr[b, br, :, :, tt * 512:(tt + 1) * 512].rearrange("k p t -> p k t")),
                        [], ["UB%d" % i], "UB%d_%d" % (i, br))

            def norm_p1(i, j):
                k = j % 2
                sc = 4 * j
                gp(lambda e: e.memset(SS[:, sc:sc + 1], 0.0), [], ["SSn%d" % j])
                act(lambda e: e.activation(out=JUNKB[:], in_=XB[i][:, j, :], func=AF.Square, accum_out=SS[:, sc:sc + 1]),
                    ["XB%d" % i, "SSn%d" % j], ["JUNKB", "SSn%d" % j])
                dve(lambda e: e.tensor_scalar(out=SS[:, sc + 1:sc + 2], in0=SS[:, sc:sc + 1], scalar1=1.0 / D, scalar2=EPS, op0=ALU.mult, op1=ALU.add),
                    ["SSn%d" % j], ["SSn%d" % j])
                gp(lambda e: e.tensor_tensor(out=SS[:, sc + 2:sc + 3], in0=SS[:, sc + 1:sc + 2], in1=NEGH[:, 0:1], op=ALU.pow), ["SSn%d" % j, "const"], ["SSn%d" % j])
                dve(lambda e: e.scalar_tensor_tensor(out=XNB[k][:], in0=XB[i][:, j, :], scalar=SS[:, sc + 2:sc + 3], in1=gpre_bc[:],
                                                     op0=ALU.mult, op1=ALU.mult), ["XB%d" % i, "SSn%d" % j, "const"], ["XNB%d" % k])
                return k

            def norm_p2(j, k):
                for kc in range(8):
                    pe(lambda e, kc=kc: e.transpose(out=psT[:, kc * 128:(kc + 1) * 128], in_=XNB[k][:, kc * 128:(kc + 1) * 128], identity=ident[:]),
                       ["XNB%d" % k, "const"], PS(7))
                act(lambda e: e.activation(out=HB[:, :, j * 128:(j + 1) * 128], in_=psT.rearrange("p (k t) -> p k t", k=8), func=AF.Copy),
                    [], PS(7) + ["HB"])

            def build_hb(i):
                kk = norm_p1(i, 0)
                for j in range(4):
                    kn_ = norm_p1(i, j + 1) if j + 1 < 4 else None
                    norm_p2(j, kk)
                    kk = kn_

            def tile_body(n, b, tt, i):
                have_next = n + 1 < len(tiles)
                if have_next:
                    loads(n + 1)
                for c in range(8):
                    for br in range(3):
                        q = (c * 3 + br) % 2
                        bm_, by_ = (0, 1) if q == 0 else (2, 3)
                        for kc in range(8):
                            pe(lambda e, kc=kc, br=br, c=c, bm_=bm_: e.matmul(ps[:, bm_, :], lhsT=WMG[:, kc, br * 1024 + c * 128:br * 1024 + (c + 1) * 128], rhs=HB[:, kc, :],
                                                                             start=(kc == 0), stop=(kc == 7)), ["WMG", "HB"], PS(bm_))
                        for k4 in range(4):
                            pe(lambda e, k4=k4, br=br, c=c, by_=by_: e.matmul(ps[:, by_, :], lhsT=WBR[br][:, k4, c * 128:(c + 1) * 128], rhs=UB[i][:, br, k4, :],
                                                                             start=(k4 == 0), stop=(k4 == 3)), ["WBR", "UB%d" % i], PS(by_))
                        act(lambda e, q=q, bm_=bm_, br=br, c=c: e.activation(out=TGB[q][:], in_=ps[:, bm_, :], func=AF.Tanh, scale=0.5,
                                                                             bias=bmh[:, br * 8 + c:br * 8 + c + 1]), ["const"], PS(bm_) + ["TGB%d" % q])
                        if br == 0:
                            dve(lambda e, q=q, by_=by_: e.scalar_tensor_tensor(out=MACC[:], in0=TGB[q][:], scalar=1.0, in1=ps[:, by_, :], op0=ALU.add, op1=ALU.mult),
                                ["TGB%d" % q], PS(by_) + ["MACC"])
                        else:
                            dve(lambda e, q=q, by_=by_: e.scalar_tensor_tensor(out=TMPB[0][:], in0=TGB[q][:], scalar=1.0, in1=ps[:, by_, :], op0=ALU.add, op1=ALU.mult),
                                ["TGB%d" % q], PS(by_) + ["TMPB0"])
                            gp(lambda e, q=q: e.tensor_tensor(out=MACC[:], in0=MACC[:], in1=TMPB[0][:], op=ALU.add), ["MACC", "TMPB0"], ["MACC"])
                    act(lambda e, c=c: e.activation(out=MT[:, c, :], in_=MACC[:], func=AF.Copy, scale=0.5), ["MACC"], ["MT"])
                inext = (n + 1) % 2
                kk = norm_p1(inext, 0) if have_next else None
                for j in range(4):
                    k = j % 2
                    b0, b1 = (4, 5) if j % 2 == 0 else (0, 1)
                    for hf, bk in ((0, b0), (1, b1)):
                        for kc in range(8):
                            pe(lambda e, kc=kc, j=j, hf=hf, bk=bk: e.matmul(ps[:, bk, :], lhsT=MT[:, kc, j * 128:(j + 1) * 128], rhs=WOUT[:, kc, hf * 512:(hf + 1) * 512],
                                                                           start=(kc == 0), stop=(kc == 7)), ["MT", "WOUT"], PS(bk))
                    if have_next:
                        kn_ = norm_p1(inext, j + 1) if j + 1 < 4 else None
                        norm_p2(j, kk)
                        kk = kn_
                    sc = 32 + 4 * j
                    gp(lambda e, sc=sc: e.memset(SS[:, sc:sc + 1], 0.0), [], ["SSp%d" % j])
                    act(lambda e, sc=sc, b0=b0: e.activation(out=JUNKB[:].rearrange("p (a c) -> p a c", a=2), in_=ps[:, b0:b0 + 2, :], func=AF.Square,
                                                           accum_out=SS[:, sc:sc + 1]), ["SSp%d" % j], PS(b0, b1) + ["JUNKB", "SSp%d" % j])
                    dve(lambda e, sc=sc: e.tensor_scalar(out=SS[:, sc + 1:sc + 2], in0=SS[:, sc:sc + 1], scalar1=1.0 / D, scalar2=EPS, op0=ALU.mult, op1=ALU.add),
                        ["SSp%d" % j], ["SSp%d" % j])
                    gp(lambda e, sc=sc: e.tensor_tensor(out=SS[:, sc + 2:sc + 3], in0=SS[:, sc + 1:sc + 2], in1=NEGH[:, 0:1], op=ALU.pow), ["SSp%d" % j, "const"], ["SSp%d" % j])
                    dve(lambda e, k=k, sc=sc, b0=b0: e.scalar_tensor_tensor(out=YT[k][:].rearrange("p (a c) -> p a c", a=2), in0=ps[:, b0:b0 + 2, :],
                                                                          scalar=SS[:, sc + 2:sc + 3], in1=gpost_bc[:].rearrange("p (a c) -> p a c", a=2),
                                                                          op0=ALU.mult, op1=ALU.mult), ["SSp%d" % j, "const"], PS(b0, b1) + ["YT%d" % k])
                    gp(lambda e, k=k, j=j: e.tensor_tensor(out=YT[k][:], in0=YT[k][:], in1=XB[i][:, j, :], op=ALU.add), ["YT%d" % k, "XB%d" % i], ["YT%d" % k])
                    spd(lambda e, k=k, j=j: e.dma_start(out=y[b, tt * 512 + j * 128:tt * 512 + (j + 1) * 128, :], in_=YT[k][:]),
                        ["YT%d" % k], ["yout%d" % k], "YT%d" % k)

            loads(0)
            build_hb(0)
            for n, (b, tt) in enumerate(tiles):
                tile_body(n, b, tt, n % 2)
            S.op("sp", lambda e: e.nop(), ["yout0", "yout1"], [])

        def gen():
            ctr.update({k: 0 for k in ctr})
            setup()
            for b in range(NB if "2" in STAGES else 1):
                if "0" in STAGES:
                    stage0(b)
                if "m" in STAGES:
                    mixer_m(b)
                if "a" in STAGES:
                    mixer_a(b)
                if "b" in STAGES:
                    mixer_b(b)
            S.barrier(["pe", "act", "dve", "pool", "sp"])
            if "p" in STAGES:
                phase_b()


        S.dry = True
        gen()
        S.dry = False
        gen()
        keys = S.finalize()
        semmap = {k: es.enter_context(nc.semaphore("s%d" % n)) for n, k in enumerate(keys)}
        with nc.Block() as block:
            @block.sync
            def _(e):
                S.emit("sp", e, semmap)

            @block.tensor
            def _(e):
                S.emit("pe", e, semmap)

            @block.scalar
            def _(e):
                S.emit("act", e, semmap)

            @block.vector
            def _(e):
                S.emit("dve", e, semmap)

            @block.gpsimd
            def _(e):
                S.emit("pool", e, semmap)
    return nc


_CACHE = {}


def kernel(x, mem, g_pre, w_in, b_merge, q_norm, k_norm, g_mem, w_mem_kv, w_br_a, w_br_b, w_br_m, w_out, g_post):
    f = lambda a: np.ascontiguousarray(np.asarray(a), dtype=np.float32)
    x = f(x); mem = f(mem)
    consts = _const_tables()
    shared = {
        "w_in": f(w_in)[0], "w_mem": f(w_mem_kv)[0], "w_br0": f(w_br_a)[0], "w_br1": f(w_br_b)[0], "w_br2": f(w_br_m)[0],
        "w_out": f(w_out)[0], "g_pre": f(g_pre), "g_mem": f(g_mem), "g_post": f(g_post),
        "bm": np.ascontiguousarray(f(b_merge)[0].reshape(24, 128).T),
        "qn": np.ascontiguousarray(np.tile(f(q_norm)[0], 2).reshape(128, 1)),
        "kn": np.ascontiguousarray(np.tile(f(k_norm)[0], 2).reshape(128, 1)),
    }
    shared.update(consts)
    if "nc" not in _CACHE:
        _CACHE["nc"] = build_program()
    nc = _CACHE["nc"]
    in_maps = []
    for c in range(NCORES):
        m = dict(shared)
        m["x"] = np.ascontiguousarray(x[c * NB:(c + 1) * NB])
        m["mem"] = np.ascontiguousarray(mem[c * NB:(c + 1) * NB])
        in_maps.append(m)
    res = run_bass_kernel_spmd(nc, in_maps, core_ids=list(range(NCORES)))
    out = np.concatenate([np.asarray(r["y"]) for r in res.results], axis=0)
    return out.astype(np.float32)
```

```python
import numpy as np
from contextlib import ExitStack
import concourse.bass as bass
import concourse.mybir as mybir
from concourse.bass_utils import run_bass_kernel_spmd

F32 = mybir.dt.float32
BF16 = mybir.dt.bfloat16
AF = mybir.ActivationFunctionType
ALU = mybir.AluOpType

NCORES = 8
NB = 2
SEQ = 4096
D = 1024
INW = 10496
AQ0, AK0, AV0, BQ0, BK0, BV0, MQ0, GA0, GB0, GM0, MG0 = 0, 1536, 3072, 4608, 5120, 5248, 5376, 5888, 6400, 6912, 7424
DILS = (1, 4, 16)
EPS = 1e-6
POOL_ELEMS = 102400
import os
STAGES = os.environ.get("MK_STAGES", "0mabp2")
ASUB = os.environ.get("MK_ASUB", "vkte")
AGR = os.environ.get("MK_AG", "012")


class Op:
    __slots__ = ("eng", "fn", "deps", "sig", "is_dma", "dma_key", "waits", "inc")

    def __init__(self, eng, fn, is_dma=False, dma_key=None):
        self.eng = eng
        self.fn = fn
        self.deps = []
        self.is_dma = is_dma
        self.dma_key = dma_key
        self.sig = None
        self.waits = []
        self.inc = None


class Res:
    __slots__ = ("w", "r")

    def __init__(self):
        self.w = None
        self.r = {}


class Sched:
    def __init__(self):
        self.ops = []
        self.res = {}
        self.dry = False
        self.last = {}
        self.last_dma = {}

    def _res(self, k):
        r = self.res.get(k)
        if r is None:
            r = self.res[k] = Res()
        return r

    def op(self, eng, fn, reads=(), writes=(), dma_key=None):
        if self.dry:
            return None
        o = Op(eng, fn, is_dma=dma_key is not None, dma_key=dma_key)
        deps = []
        for k in reads:
            r = self._res(k)
            if r.w is not None:
                deps.append((r.w, 0))
        for k in writes:
            r = self._res(k)
            if r.w is not None:
                deps.append((r.w, 1))
            for ro in r.r.values():
                deps.append((ro, 2))
        seen = set()
        for d, kind in deps:
            if id(d) in seen or d is o:
                continue
            if d.eng == eng and not d.is_dma and not o.is_dma:
                if eng == "pe" or kind != 0:
                    continue
            seen.add(id(d))
            o.deps.append(d)
        rk = ("dma", len(self.ops)) if o.is_dma else eng
        for k in reads:
            self._res(k).r[rk] = o
        for k in writes:
            r = self._res(k)
            r.w = o
            r.r = {}
        self.ops.append(o)
        if o.is_dma:
            self.last_dma[dma_key] = o
        else:
            self.last[eng] = o
        return o

    def barrier(self, engines):
        if self.dry:
            return
        prev = list(self.last.values()) + list(self.last_dma.values())
        for e in engines:
            o = Op(e, lambda eng: eng.nop())
            o.deps = [p for p in prev]
            self.ops.append(o)
            self.last[e] = o
        self.res = {}

    def finalize(self):
        need = set()
        for o in self.ops:
            for d in o.deps:
                need.add(id(d))
        cnt = {}
        for o in self.ops:
            if o.is_dma:
                k = ("dma", o.dma_key)
                cnt[k] = cnt.get(k, 0) + 16
                o.sig = (k, cnt[k])
                o.inc = (k, 16)
            elif id(o) in need:
                k = ("eng", o.eng)
                cnt[k] = cnt.get(k, 0) + 1
                o.sig = (k, cnt[k])
                o.inc = (k, 1)
        waited = {}
        for o in self.ops:
            w = waited.setdefault(o.eng, {})
            best = {}
            for d in o.deps:
                k, v = d.sig
                if w.get(k, 0) >= v:
                    continue
                if best.get(k, 0) < v:
                    best[k] = v
            for k, v in best.items():
                w[k] = v
                o.waits.append((k, v))
        return sorted(cnt.keys(), key=str)

    def emit(self, engname, eng, semmap):
        for o in self.ops:
            if o.eng != engname:
                continue
            for k, v in o.waits:
                eng.wait_ge(semmap[k], v)
            ins = o.fn(eng)
            if o.inc is not None:
                ins.then_inc(semmap[o.inc[0]], o.inc[1])


def RC(name, c0, n, gran=512):
    return ["%s.%d" % (name, t) for t in range(c0 // gran, (c0 + n - 1) // gran + 1)]


def PTR(i):
    return ["PT%d.0" % i, "PT%d.1" % i]


def PS(*banks):
    return ["ps%d" % b for b in banks]


def _const_tables():
    t = np.arange(SEQ, dtype=np.float64)
    d = np.arange(128)
    dd = d % 64
    fA = dd % 32
    invA = 10000.0 ** (-fA.astype(np.float64) * 2.0 / 64.0)
    angA = invA[:, None] * t[None, :]
    cosA = np.cos(angA)
    sinA = np.sin(angA) * np.where(dd < 32, -1.0, 1.0)[:, None]
    PA = np.zeros((128, 128), np.float32)
    for i in range(128):
        part = i + 32 if (i % 64) < 32 else i - 32
        PA[part, i] = 1.0
    j = dd % 32
    fB = j % 16
    invB = 10000.0 ** (-fB.astype(np.float64) * 2.0 / 32.0)
    rowp = np.floor(t / 64.0)
    colp = t % 64
    posB = np.where((dd // 32)[:, None] == 0, rowp[None, :], colp[None, :])
    angB = invB[:, None] * posB
    cosB = np.cos(angB)
    sinB = np.sin(angB) * np.where(j < 16, -1.0, 1.0)[:, None]
    PBm = np.zeros((128, 128), np.float32)
    for i in range(128):
        part = i + 16 if (i % 32) < 16 else i - 16
        PBm[part, i] = 1.0
    BD = np.zeros((128, 128), np.float32)
    BD[0:64, 0:64] = 1.0
    BD[64:128, 64:128] = 1.0
    ii = np.arange(128)[:, None]
    cc = np.arange(128)[None, :]
    M1 = (ii >= cc).astype(np.float32)
    M2 = (ii <= cc).astype(np.float32)
    MF = np.concatenate([M1, M2] * 4, axis=1)
    ident = np.eye(128, dtype=np.float32)
    f = lambda a: np.ascontiguousarray(a, dtype=np.float32)
    return dict(cosA=f(cosA), sinA=f(sinA), cosB=f(cosB), sinB=f(sinB), PA=f(PA), PB=f(PBm), BD=f(BD),
                MF=f(MF), ident=f(ident))


def build_program():
    nc = bass.Bass("TRN2", target_bir_lowering=False)
    din = lambda n, s: nc.dram_tensor(n, s, F32, kind="ExternalInput").ap()
    x = din("x", [NB, SEQ, D])
    mem = din("mem", [NB, 256, D])
    w_in = din("w_in", [D, INW])
    w_mem = din("w_mem", [D, 1024])
    w_br = [din("w_br%d" % i, [512, D]) for i in range(3)]
    w_out = din("w_out", [D, D])
    g_pre = din("g_pre", [1, D])
    g_mem = din("g_mem", [1, D])
    g_post = din("g_post", [1, D])
    bm = din("bm", [128, 24])
    qn = din("qn", [128, 1])
    kn = din("kn", [128, 1])
    tab = {k: din(k, [128, SEQ]) for k in ("cosA", "sinA", "cosB", "sinB")}
    cPA = din("PA", [128, 128])
    cPB = din("PB", [128, 128])
    cBD = din("BD", [128, 128])
    cMF = din("MF", [128, 1024])
    cID = din("ident", [128, 128])
    y = nc.dram_tensor("y", [NB, SEQ, D], F32, kind="ExternalOutput").ap()
    uscr = nc.dram_tensor("uscr", [NB, 3, 4, 128, SEQ], BF16).ap()

    S = Sched()
    with ExitStack() as es:
        pool = es.enter_context(nc.sbuf_tensor("pool", [128, POOL_ELEMS], BF16))
        ps = es.enter_context(nc.psum_tensor("ps", [128, 8, 512], F32))
        psT = ps[:, 7, :].bitcast(BF16)

        cur = [0]

        def alloc(n, dt=BF16):
            ne = n if dt == BF16 else 2 * n
            o = cur[0]
            cur[0] += ne
            assert cur[0] <= POOL_ELEMS, ("sbuf overflow", cur[0])
            a = pool[:, o:o + ne]
            return a.bitcast(F32) if dt == F32 else a

        ident = alloc(128); PA = alloc(128); PBm = alloc(128); BD = alloc(128); TWOS = alloc(128)
        MF = alloc(1024)
        gpre_bc = alloc(1024, F32); gpost_bc = alloc(1024, F32); gmem_bc = alloc(1024, F32)
        bmh = alloc(24, F32); qn_t = alloc(1, F32); kn_t = alloc(1, F32)
        NEGH = alloc(512, F32)
        EPSB = alloc(2, F32)
        SS = alloc(64, F32)
        phase_base = cur[0]

        hT = alloc(8 * SEQ).rearrange("p (k t) -> p k t", k=8)
        cosA = alloc(SEQ); sinA = alloc(SEQ)
        R1a = alloc(8192); R1b = alloc(8192)
        ACCE = R1a.bitcast(F32); ACCO = R1b.bitcast(F32)
        cosB = R1a[:, 0:SEQ]; sinB = R1a[:, SEQ:2 * SEQ]
        mnT = R1b[:, 4096:6144].rearrange("p (k t) -> p k t", k=8)
        KmT = R1b[:, 6144:7168].rearrange("p (h t) -> p h t", h=4)
        Vm = R1b[:, 7168:8192].rearrange("p (j c) -> p j c", j=2)
        QT = alloc(SEQ); KT = alloc(SEQ)
        XT = [R1a[:, 2048 * k:2048 * (k + 1)].bitcast(F32) for k in range(4)]
        XN = [R1b[:, 1024 * k:1024 * (k + 1)] for k in range(4)]
        JUNK = KT[:, 2048:3072]
        R1A = ["R1a.%d" % k for k in range(4)]
        R1B = ["R1b.%d" % k for k in range(4)] + ["R1b.m"]
        VTraw = alloc(32 * 192)
        VT = VTraw.rearrange("p (j c) -> p j c", c=192)
        VT4 = VTraw.rearrange("p (j a c) -> p j a c", a=3, c=64)
        WM = VTraw[:, 0:4096].rearrange("p (k c) -> p k c", k=8)
        NWS = 4
        WS = [alloc(1024).rearrange("p (k c) -> p k c", k=8) for _ in range(NWS)]
        QRAW = [alloc(512) for _ in range(2)]
        T0 = [alloc(512, F32) for _ in range(2)]
        T1 = [alloc(512, F32) for _ in range(2)]
        T2 = [alloc(512, F32) for _ in range(2)]
        OO = [alloc(512, F32) for _ in range(2)]
        PT = [alloc(1024) for _ in range(2)]
        SQ = [alloc(512) for _ in range(2)]
        RS = [alloc(512, F32) for _ in range(2)]
        UST = [alloc(512) for _ in range(2)]
        endA = cur[0]

        cur[0] = phase_base
        WMG = alloc(8 * 3072).rearrange("p (k c) -> p k c", k=8)
        WBR = [alloc(4 * 1024).rearrange("p (k c) -> p k c", k=4) for _ in range(3)]
        WOUT = alloc(8 * 1024).rearrange("p (k c) -> p k c", k=8)
        XB = [alloc(4 * 1024, F32).rearrange("p (j c) -> p j c", j=4) for _ in range(2)]
        HB = alloc(8 * 512).rearrange("p (k t) -> p k t", k=8)
        XNB = [alloc(1024) for _ in range(2)]
        UB = [alloc(12 * 512).rearrange("p (r k t) -> p r k t", r=3, k=4) for _ in range(2)]
        MT = alloc(8 * 512).rearrange("p (k t) -> p k t", k=8)
        TGB = [alloc(512, F32) for _ in range(2)]
        MACC = alloc(512, F32)
        TMPB = [alloc(512, F32)]
        YT = [alloc(1024, F32) for _ in range(2)]
        JUNKB = alloc(1024)
        endB = cur[0]

        pe = lambda fn, r=(), w=(): S.op("pe", fn, r, w)
        act = lambda fn, r=(), w=(): S.op("act", fn, r, w)
        dve = lambda fn, r=(), w=(): S.op("dve", fn, r, w)
        gp = lambda fn, r=(), w=(): S.op("pool", fn, r, w)
        spd = lambda fn, r, w, key: S.op("sp", fn, r, w, dma_key=key)
        gpd = lambda fn, r, w, key: S.op("pool", fn, r, w, dma_key=key)

        w_in_v = w_in.rearrange("(k p) c -> p k c", p=128)
        w_mem_v = w_mem.rearrange("(k p) c -> p k c", p=128)
        w_out_v = w_out.rearrange("(k p) c -> p k c", p=128)
        w_br_v = [w.rearrange("(k p) c -> p k c", p=128) for w in w_br]

        class WStream:
            def __init__(self):
                self.specs = []
                self.i = 0
                self.issued = 0

            def _issue(self, k):
                slot = k % NWS
                for (c0, n, d0) in self.specs[k]:
                    srcv = w_in_v
                    if c0 < 0:
                        srcv = w_mem_v
                        c0 = -c0 - 1
                    gpd(lambda e, slot=slot, c0=c0, n=n, d0=d0, srcv=srcv: e.dma_start(out=WS[slot][:, :, d0:d0 + n], in_=srcv[:, :, c0:c0 + n]),
                        [], ["WS%d" % slot], "WS%d" % slot)

            def next(self, pieces):
                if S.dry:
                    self.specs.append(pieces)
                    return 0
                k = self.i
                self.i += 1
                while self.issued < min(len(self.specs), k + NWS - 1):
                    self._issue(self.issued)
                    self.issued += 1
                return k % NWS

        W = WStream()

        def tokslice(g, p0, n):
            d = DILS[g]
            if d == 1:
                return slice(p0, p0 + n)
            L = SEQ // d
            r, m0 = p0 // L, p0 % L
            assert m0 + n <= L
            s0 = m0 * d + r
            return slice(s0, s0 + (n - 1) * d + 1, d)

        HT_ALL = ["hT.%d" % t for t in range(8)]

        def ht_res(g, p0, n):
            return RC("hT", p0, n) if DILS[g] == 1 else HT_ALL

        def proj_fm(slot, g, p0, n, bank, col0=0):
            sl = tokslice(g, p0, n)
            for kc in range(8):
                pe(lambda e, kc=kc: e.matmul(ps[:, bank, col0:col0 + n], lhsT=WS[slot][:, kc, :], rhs=hT[:, kc, sl],
                                             start=(kc == 0), stop=(kc == 7)),
                   ["WS%d" % slot] + ht_res(g, p0, n), PS(bank))

        ctr = {"rope": 0, "pt": 0, "ep": 0, "x": 0, "pj": 0}

        def rope_p1(src, src_res, src_is_psum, cs, sn, perm, sl, dst, dst_res, n, tres=("tabs",), dsplit=1, rbank=2, copy_eng="act"):
            i = ctr["rope"] % 2
            ctr["rope"] += 1
            rw = (lambda r, w: ([], r + w)) if src_is_psum else (lambda r, w: (r, w))
            r_, w_ = rw(src_res, ["QRAW%d" % i])
            if copy_eng == "act":
                act(lambda e: e.activation(out=QRAW[i][:, 0:n], in_=src, func=AF.Copy), r_, w_)
            else:
                gp(lambda e: e.tensor_copy(out=QRAW[i][:, 0:n], in_=src), r_, w_)
            r_, w_ = rw(src_res, ["T1_%d" % i])
            dve(lambda e: e.tensor_tensor(out=T1[i][:, 0:n], in0=src, in1=cs[:, sl], op=ALU.mult), r_ + list(tres), w_)
            return (i, sn, perm, sl, dst, dst_res, n, tres, dsplit, rbank)

        def rope_p2(st):
            i, sn, perm, sl, dst, dst_res, n, tres, dsplit, rbank = st
            pe(lambda e: e.matmul(ps[:, rbank, 0:n], lhsT=perm, rhs=QRAW[i][:, 0:n], start=True, stop=True),
               ["QRAW%d" % i, "const"], PS(rbank))
            dve(lambda e: e.tensor_tensor(out=T2[i][:, 0:n], in0=ps[:, rbank, 0:n], in1=sn[:, sl], op=ALU.mult),
                list(tres), PS(rbank) + ["T2_%d" % i])
            a0, a1 = T1[i][:, 0:n], T2[i][:, 0:n]
            if dsplit > 1:
                a0 = a0.rearrange("p (m r) -> p m r", r=dsplit)
                a1 = a1.rearrange("p (m r) -> p m r", r=dsplit)
            gp(lambda e: e.tensor_tensor(out=dst, in0=a0, in1=a1, op=ALU.add),
               ["T1_%d" % i, "T2_%d" % i], dst_res)

        def rope_tile(*a, **k):
            rope_p2(rope_p1(*a, **k))

        def gate_part(slot, tt, bank):
            i = ctr["ep"] % 2
            ctr["ep"] += 1
            proj_fm(slot, 0, tt * 512, 512, bank)
            act(lambda e: e.activation(out=T0[i][:], in_=ps[:, bank, :], func=AF.Tanh, scale=0.5), [], PS(bank) + ["T0_%d" % i])
            dve(lambda e: e.scalar_tensor_tensor(out=T1[i][:], in0=T0[i][:], scalar=1.0, in1=ps[:, bank, :], op0=ALU.add, op1=ALU.mult),
                ["T0_%d" % i], PS(bank) + ["T1_%d" % i])
            return i

        def gate_tail(i, b, br, chunk, tt, num_e, num_o, l_e, l_o, src_res, src_psum):
            rr, ww = ([], list(src_res)) if src_psum else (list(src_res), [])
            if num_o is None:
                dve(lambda e: e.reciprocal(out=T2[i][:], in_=l_e), rr, ww + ["T2_%d" % i])
                dve(lambda e: e.tensor_tensor(out=OO[i][:], in0=num_e, in1=T2[i][:], op=ALU.mult), rr + ["T2_%d" % i], ww + ["OO%d" % i])
            else:
                dve(lambda e: e.reciprocal(out=T2[i][0:64, :], in_=l_e), rr, ww + ["T2_%d" % i])
                dve(lambda e: e.reciprocal(out=T2[i][64:128, :], in_=l_o), rr, ww + ["T2_%d" % i])
                dve(lambda e: e.tensor_tensor(out=OO[i][0:64, :], in0=num_e, in1=T2[i][0:64, :], op=ALU.mult), rr + ["T2_%d" % i], ww + ["OO%d" % i])
                dve(lambda e: e.tensor_tensor(out=OO[i][64:128, :], in0=num_o, in1=T2[i][64:128, :], op=ALU.mult), rr + ["T2_%d" % i], ww + ["OO%d" % i])
            gp(lambda e: e.tensor_tensor(out=UST[i][:], in0=OO[i][:], in1=T1[i][:], op=ALU.mult), ["OO%d" % i, "T1_%d" % i], ["UST%d" % i])
            spd(lambda e: e.dma_start(out=uscr[b, br, chunk, :, tt * 512:(tt + 1) * 512], in_=UST[i][:]),
                ["UST%d" % i], [], "UST%d" % i)

        def norm_rows(src_dram, gbc, i, sscol):
            xr = ["R1a.%d" % i]
            nr = ["R1b.%d" % i]
            spd(lambda e: e.dma_start(out=XT[i][:], in_=src_dram), [], xr, "XT%d" % i)
            gp(lambda e: e.memset(SS[:, sscol:sscol + 1], 0.0), [], ["SS%d" % sscol])
            act(lambda e: e.activation(out=JUNK[:], in_=XT[i][:], func=AF.Square, accum_out=SS[:, sscol:sscol + 1]),
                xr + ["SS%d" % sscol], ["KT.4", "KT.5", "SS%d" % sscol])
            dve(lambda e: e.tensor_scalar(out=SS[:, sscol + 1:sscol + 2], in0=SS[:, sscol:sscol + 1], scalar1=1.0 / D, scalar2=EPS,
                                          op0=ALU.mult, op1=ALU.add), ["SS%d" % sscol], ["SS%d" % sscol])
            gp(lambda e: e.tensor_tensor(out=SS[:, sscol + 2:sscol + 3], in0=SS[:, sscol + 1:sscol + 2], in1=NEGH[:, 0:1], op=ALU.pow),
               ["SS%d" % sscol, "const"], ["SS%d" % sscol])
            dve(lambda e: e.scalar_tensor_tensor(out=XN[i][:], in0=XT[i][:], scalar=SS[:, sscol + 2:sscol + 3], in1=gbc[:],
                                                 op0=ALU.mult, op1=ALU.mult), xr + ["SS%d" % sscol, "const"], nr)
            return nr

        def transpose_rows(i, nr, dst3, dst_res):
            for kc in range(8):
                pe(lambda e, kc=kc: e.transpose(out=psT[:, kc * 128:(kc + 1) * 128], in_=XN[i][:, kc * 128:(kc + 1) * 128], identity=ident[:]),
                   nr + ["const"], PS(7))
            act(lambda e: e.activation(out=dst3, in_=psT.rearrange("p (k t) -> p k t", k=8), func=AF.Copy), [], PS(7) + dst_res)

        def setup():
            for (dst, src, nm) in ((ident, cID, "cid"), (PA, cPA, "cpa"), (PBm, cPB, "cpb"), (BD, cBD, "cbd"), (MF, cMF, "cmf")):
                gpd(lambda e, dst=dst, src=src: e.dma_start(out=dst[:], in_=src), [], ["const"], nm)
            for (dst, src, nm) in ((gpre_bc, g_pre, "cg1"), (gpost_bc, g_post, "cg2"), (gmem_bc, g_mem, "cg3")):
                spd(lambda e, dst=dst, src=src: e.dma_start(out=dst[:], in_=src.partition_broadcast(128)), [], ["const"], nm)
            spd(lambda e: e.dma_start(out=bmh[:], in_=bm), [], ["const"], "cbm")
            spd(lambda e: e.dma_start(out=qn_t[:], in_=qn), [], ["const"], "cqn")
            spd(lambda e: e.dma_start(out=kn_t[:], in_=kn), [], ["const"], "ckn")
            gp(lambda e: e.memset(TWOS[:], 2.0), [], ["const"])
            gp(lambda e: e.memset(NEGH[:], -0.5), [], ["const"])
            gp(lambda e: e.memset(EPSB[:], EPS), [], ["const"])
            dve(lambda e: e.tensor_scalar(out=bmh[:], in0=bmh[:], scalar1=0.5, scalar2=None, op0=ALU.mult), ["const"], ["const"])
            gpd(lambda e: e.dma_start(out=cosA[:], in_=tab["cosA"]), [], ["tabs"], "tcA")
            gpd(lambda e: e.dma_start(out=sinA[:], in_=tab["sinA"]), [], ["tabs"], "tsA")

        def stage0(b):
            pend = None
            for tt in range(32):
                i = tt % 4
                nr = norm_rows(x[b, tt * 128:(tt + 1) * 128, :], gpre_bc, i, 4 * (tt % 8))
                if pend is not None:
                    transpose_rows(*pend)
                pend = (i, nr, hT[:, :, tt * 128:(tt + 1) * 128], ["hT.%d" % (tt // 4)])
            transpose_rows(*pend)

        def mixer_m(b):
            for j in range(2):
                nr = norm_rows(mem[b, j * 128:(j + 1) * 128, :], gmem_bc, j, 32 + 4 * j)
                transpose_rows(j, nr, mnT[:, :, j * 128:(j + 1) * 128], ["R1b.m"])
            gpd(lambda e: e.dma_start(out=WM[:], in_=w_mem_v[:, :, 512:1024]), [], ["VT"], "WM")
            for j in range(2):
                for kc in range(8):
                    pe(lambda e, kc=kc, j=j: e.matmul(ps[:, 0, :], lhsT=mnT[:, kc, j * 128:(j + 1) * 128], rhs=WM[:, kc, :],
                                                       start=(kc == 0), stop=(kc == 7)), ["R1b.m", "VT"], PS(0))
                act(lambda e, j=j: e.activation(out=Vm[:, j, :], in_=ps[:, 0, :], func=AF.Copy), [], PS(0) + ["R1b.m"])
            for h in range(4):
                slot = W.next([(-(h * 128) - 1, 128, 0)])
                for kc in range(8):
                    pe(lambda e, kc=kc, slot=slot: e.matmul(ps[:, 1, 0:256], lhsT=WS[slot][:, kc, :], rhs=mnT[:, kc, :],
                                                             start=(kc == 0), stop=(kc == 7)), ["WS%d" % slot, "R1b.m"], PS(1))
                act(lambda e, h=h: e.activation(out=KmT[:, h, :], in_=ps[:, 1, 0:256], func=AF.Copy), [], PS(1) + ["R1b.m"])
            scale = 128.0 ** -0.5
            for h in range(4):
                slot = W.next([(MQ0 + h * 128, 128, 0)])
                for tt in range(8):
                    bank = tt % 2
                    proj_fm(slot, 0, tt * 512, 512, bank)
                    act(lambda e, tt=tt, bank=bank: e.activation(out=QT[:, tt * 512:(tt + 1) * 512], in_=ps[:, bank, :], func=AF.Copy),
                        [], PS(bank) + ["QT.%d" % tt])
                gslot = W.next([(GM0 + h * 128, 128, 0)])
                pend = None
                for tt in range(8):
                    i = ctr["pt"] % 2
                    ctr["pt"] += 1
                    ab, lb = (5, 6) if tt % 2 == 0 else (0, 1)
                    gb_ = 2 if tt % 2 == 0 else 7
                    for j in range(2):
                        pe(lambda e, j=j, tt=tt, h=h: e.matmul(ps[:, 3 + j, :], lhsT=KmT[:, h, j * 128:(j + 1) * 128], rhs=QT[:, tt * 512:(tt + 1) * 512],
                                                                 start=True, stop=True), ["R1b.m", "QT.%d" % tt], PS(3 + j))
                    gi = gate_part(gslot, tt, gb_)
                    act(lambda e, i=i: e.activation(out=PT[i][:].rearrange("p (a c) -> p a c", a=2), in_=ps[:, 3:5, :], func=AF.Exp, scale=scale),
                        [], PS(3, 4) + [*PTR(i)])
                    for j in range(2):
                        pe(lambda e, j=j, i=i, h=h, ab=ab: e.matmul(ps[:, ab, :], lhsT=Vm[:, j, h * 128:(h + 1) * 128], rhs=PT[i][:, j * 512:(j + 1) * 512],
                                                                     start=(j == 0), stop=(j == 1)), ["R1b.m", *PTR(i)], PS(ab))
                    for j in range(2):
                        pe(lambda e, j=j, i=i, lb=lb: e.matmul(ps[:, lb, :], lhsT=TWOS[:], rhs=PT[i][:, j * 512:(j + 1) * 512],
                                                                start=(j == 0), stop=(j == 1)), ["const", *PTR(i)], PS(lb))
                    if pend is not None:
                        gate_tail(*pend)
                    pend = (gi, b, 2, h, tt, ps[:, ab, :], None, ps[:, lb, :], None, PS(ab, lb), True)
                gate_tail(*pend)

        def a_units(g):
            d = DILS[g]
            L = SEQ // d
            batches = []
            for r in range(d):
                pc = r * L
                units = [(pc, 64, [pc // 128], 1)]
                for a in range(L // 128 - 1):
                    units.append((pc + 128 * a + 64, 128, [pc // 128 + a, pc // 128 + a + 1], 0))
                units.append((pc + L - 64, 64, [(pc + L) // 128 - 1], 2))
                curb = None
                for u in units:
                    if curb is None or (u[0] + u[1] - curb[0]) > 512:
                        curb = [u[0], 0, []]
                        batches.append(curb)
                    curb[2].append(u)
                    curb[1] = u[0] + u[1] - curb[0]
            return batches

        def mixer_a_job(b, g, hp):
            d = DILS[g]
            L = SEQ // d
            ntile = min(512, L)
            cq = g * 512 + hp * 128
            slot = W.next([(AV0 + cq, 128, 0)])
            for j4 in range(8 if "v" in ASUB else 0):
                bank = j4 % 2
                for jj in range(4):
                    j = j4 * 4 + jj
                    sl = tokslice(g, 128 * j, 128)
                    for kc in range(8):
                        pe(lambda e, kc=kc, sl=sl, jj=jj, bank=bank, slot=slot: e.matmul(ps[:, bank, jj * 128:(jj + 1) * 128], lhsT=hT[:, kc, sl], rhs=WS[slot][:, kc, :],
                                                                                  start=(kc == 0), stop=(kc == 7)),
                           ["WS%d" % slot] + ht_res(g, 128 * j, 128), PS(bank))
                act(lambda e, j4=j4, bank=bank: e.activation(out=VT4[:, j4 * 4:(j4 + 1) * 4, 0:3:2, :],
                                                             in_=ps[:, bank, :].rearrange("p (j a c) -> p j a c", j=4, a=2), func=AF.Copy),
                    [], PS(bank) + ["VT"])
            for (c0, dstT, nm) in ((AK0 + cq, KT, "KT"), (AQ0 + cq, QT, "QT")):
                slot = W.next([(c0, 128, 0)])
                pend = None
                for tt in range(8 if "k" in ASUB else 0):
                    bank = tt % 2
                    proj_fm(slot, 0, tt * 512, 512, bank)
                    if d == 1:
                        dst = dstT[:, tt * 512:(tt + 1) * 512]
                        dres = RC(nm, tt * 512, 512)
                    else:
                        dst = dstT[:, :].rearrange("p (r m) -> p m r", r=d)[:, tt * 512 // d:(tt + 1) * 512 // d, :]
                        dres = RC(nm, 0, SEQ)
                    st = rope_p1(ps[:, bank, :], PS(bank), True, cosA, sinA, PA[:], slice(tt * 512, (tt + 1) * 512),
                                 dst, dres, 512, dsplit=d)
                    if pend is not None:
                        rope_p2(pend)
                    pend = st
                if pend is not None:
                    rope_p2(pend)
            if "t" not in ASUB:
                return
            allu = []
            for bi, (plo, width, units) in enumerate(a_units(g)):
                accs = (5, 6) if bi % 2 == 0 else (0, 1)
                for ui, u in enumerate(units):
                    allu.append((plo, width, accs, u, ui == len(units) - 1))
            supers = [allu[k:k + 2] for k in range(0, len(allu), 2)]
            OFFS = {0: [0, 128], 1: [192], 2: [0]}

            def emit_qk(sidx):
                banks = (3, 4) if sidx % 2 == 0 else (2, 7)
                for si, (plo, width, accs, (q0, nq, ktiles, kind), last) in enumerate(supers[sidx]):
                    for hh in range(2):
                        rows = slice(64 * hh, 64 * hh + 64)
                        for t, kt in enumerate(ktiles):
                            co = si * 256 + OFFS[kind][t]
                            pe(lambda e, rows=rows, kt=kt, co=co, q0=q0, nq=nq, bk=banks[hh]: e.matmul(ps[:, bk, co:co + nq], lhsT=KT[rows, kt * 128:(kt + 1) * 128],
                                                                                                 rhs=QT[rows, q0:q0 + nq], start=True, stop=True),
                               RC("KT", kt * 128, 128) + RC("QT", q0, nq), PS(banks[hh]))

            def emit_rest(sidx):
                i = sidx % 2
                banks = (3, 4) if i == 0 else (2, 7)
                for hh in range(2):
                    half = PT[i][:, hh * 512:(hh + 1) * 512]
                    pr = "PT%d.%d" % (i, hh)
                    act(lambda e, half=half, bk=banks[hh]: e.activation(out=half, in_=ps[:, bk, :], func=AF.Exp, scale=0.125), [], PS(banks[hh]) + [pr])
                    dve(lambda e, half=half: e.tensor_tensor(out=half, in0=half, in1=MF[:, 0:512], op=ALU.mult), [pr, "const"], [pr])
                evacs = []
                for hh in range(2):
                    pr = "PT%d.%d" % (i, hh)
                    for si, (plo, width, accs, (q0, nq, ktiles, kind), last) in enumerate(supers[sidx]):
                        c0 = q0 - plo
                        accb = accs[hh]
                        for t, kt in enumerate(ktiles):
                            co = hh * 512 + si * 256 + OFFS[kind][t]
                            pe(lambda e, hh=hh, accb=accb, kt=kt, co=co, c0=c0, nq=nq, t=t, nk=len(ktiles):
                               e.matmul(ps[:, accb, c0:c0 + nq], lhsT=VT[:, kt, 64 * hh:64 * hh + 128], rhs=PT[i][:, co:co + nq],
                                        start=(t == 0), stop=(t == nk - 1)),
                               ["VT", pr], PS(accb))
                        if last and hh == 1:
                            evacs.append((plo, width, accs))
                for (plo, width, accs) in evacs:
                    ae, ao = accs
                    sl = tokslice(g, plo, width)
                    if g == 0:
                        act(lambda e, sl=sl, ae=ae, width=width: e.activation(out=ACCE[:, sl], in_=ps[:, ae, 0:width], func=AF.Copy), [], PS(ae) + R1A)
                        act(lambda e, sl=sl, ao=ao, width=width: e.activation(out=ACCO[:, sl], in_=ps[:, ao, 0:width], func=AF.Copy), [], PS(ao) + R1B)
                    else:
                        dve(lambda e, sl=sl, ae=ae, width=width: e.tensor_tensor(out=ACCE[:, sl], in0=ps[:, ae, 0:width], in1=ACCE[:, sl], op=ALU.add),
                            R1A, PS(ae) + R1A)
                        dve(lambda e, sl=sl, ao=ao, width=width: e.tensor_tensor(out=ACCO[:, sl], in0=ps[:, ao, 0:width], in1=ACCO[:, sl], op=ALU.add),
                            R1B, PS(ao) + R1B)

            emit_qk(0)
            for sidx in range(len(supers)):
                if sidx + 1 < len(supers):
                    emit_qk(sidx + 1)
                emit_rest(sidx)

        def mixer_a(b):
            gp(lambda e: e.memset(VT[:, :, 64:128], 2.0), [], ["VT"])
            for hp in range(4):
                for g in range(3):
                    if str(g) in AGR:
                        mixer_a_job(b, g, hp)
                if "e" not in ASUB:
                    continue
                gslot = W.next([(GA0 + hp * 128, 128, 0)])
                for m in range(4):
                    gis = [gate_part(gslot, 2 * m + q, 2 + q) for q in range(2)]
                    for q in range(2):
                        tt = 2 * m + q
                        i = gis[q]
                        cs = slice(tt * 512, (tt + 1) * 512)
                        act(lambda e, i=i, cs=cs: e.activation(out=T2[i][0:64, :], in_=ACCE[64:128, cs], func=AF.Ln), R1A, ["T2_%d" % i])
                        act(lambda e, i=i, cs=cs: e.activation(out=T2[i][64:128, :], in_=ACCO[0:64, cs], func=AF.Ln), R1B, ["T2_%d" % i])
                        act(lambda e, i=i: e.activation(out=T2[i][:], in_=T2[i][:], func=AF.Exp, scale=-1.0), ["T2_%d" % i], ["T2_%d" % i])
                        dve(lambda e, i=i, cs=cs: e.tensor_tensor(out=OO[i][0:64, :], in0=ACCE[0:64, cs], in1=T2[i][0:64, :], op=ALU.mult),
                            R1A + ["T2_%d" % i], ["OO%d" % i])
                        dve(lambda e, i=i, cs=cs: e.tensor_tensor(out=OO[i][64:128, :], in0=ACCO[64:128, cs], in1=T2[i][64:128, :], op=ALU.mult),
                            R1B + ["T2_%d" % i], ["OO%d" % i])
                        gp(lambda e, i=i: e.tensor_tensor(out=UST[i][:], in0=OO[i][:], in1=T1[i][:], op=ALU.mult), ["OO%d" % i, "T1_%d" % i], ["UST%d" % i])
                        spd(lambda e, i=i, tt=tt, hp=hp: e.dma_start(out=uscr[b, 0, hp, :, tt * 512:(tt + 1) * 512], in_=UST[i][:]),
                            ["UST%d" % i], [], "UST%d" % i)

        def qk_norm_rope(slot, gain, dstT, nm):
            PB4 = (0, 1, 4, 5)
            sts = {}

            def stA(tt):
                bank = PB4[tt % 4]
                i = tt % 2
                proj_fm(slot, 0, tt * 512, 512, bank)
                act(lambda e: e.activation(out=SQ[i][:], in_=ps[:, bank, :], func=AF.Square), [], PS(bank) + ["SQ%d" % i])

            def stB(tt):
                bank = PB4[tt % 4]
                i = tt % 2
                pe(lambda e: e.matmul(ps[:, 2, :], lhsT=BD[:], rhs=SQ[i][:], start=True, stop=True), ["SQ%d" % i, "const"], PS(2))
                act(lambda e: e.activation(out=RS[i][:], in_=ps[:, 2, :], func=AF.Ln, scale=1.0 / 64.0, bias=EPSB[:, 0:1]), ["const"], PS(2) + ["RS%d" % i])
                act(lambda e: e.activation(out=RS[i][:], in_=RS[i][:], func=AF.Exp, scale=-0.5), ["RS%d" % i], ["RS%d" % i])
                dve(lambda e: e.scalar_tensor_tensor(out=T0[i][:], in0=ps[:, bank, :], scalar=gain[:, 0:1], in1=RS[i][:],
                                                     op0=ALU.mult, op1=ALU.mult), ["RS%d" % i, "const"], PS(bank) + ["T0_%d" % i])
                sl = slice(tt * 512, (tt + 1) * 512)
                sts[tt] = rope_p1(T0[i][:], ["T0_%d" % i], False, cosB, sinB, PBm[:], sl, dstT[:, sl], RC(nm, tt * 512, 512), 512,
                                  tres=R1A, rbank=3, copy_eng="pool")

            for k in range(10):
                if k < 8:
                    stA(k)
                if 0 <= k - 1 < 8:
                    stB(k - 1)
                if 0 <= k - 2 < 8:
                    rope_p2(sts[k - 2])

        def mixer_b(b):
            gpd(lambda e: e.dma_start(out=cosB, in_=tab["cosB"]), [], R1A, "tcB")
            gpd(lambda e: e.dma_start(out=sinB, in_=tab["sinB"]), [], R1A, "tsB")
            for kv in range(2):
                slot = W.next([(BV0 + kv * 64, 64, 0), (BV0 + kv * 64, 64, 64)])
                for j4 in range(8):
                    bank = j4 % 2
                    for jj in range(4):
                        j = j4 * 4 + jj
                        for kc in range(8):
                            pe(lambda e, kc=kc, j=j, jj=jj, bank=bank, slot=slot: e.matmul(ps[:, bank, jj * 128:(jj + 1) * 128], lhsT=hT[:, kc, j * 128:(j + 1) * 128],
                                                                                 rhs=WS[slot][:, kc, :], start=(kc == 0), stop=(kc == 7)),
                               ["WS%d" % slot, "hT.%d" % (j // 4)], PS(bank))
                    act(lambda e, j4=j4, bank=bank: e.activation(out=VT4[:, j4 * 4:(j4 + 1) * 4, 0:3:2, :],
                                                                 in_=ps[:, bank, :].rearrange("p (j a c) -> p j a c", j=4, a=2), func=AF.Copy),
                        [], PS(bank) + ["VT"])
                slot = W.next([(BK0 + kv * 64, 64, 0), (BK0 + kv * 64, 64, 64)])
                qk_norm_rope(slot, kn_t, KT, "KT")
                for hq in range(2):
                    c = kv * 2 + hq
                    slot = W.next([(BQ0 + c * 128, 128, 0)])
                    qk_norm_rope(slot, qn_t, QT, "QT")
                    gslot = W.next([(GB0 + c * 128, 128, 0)])
                    gnext = gate_part(gslot, 0, 5)
                    for tt in range(8):
                        ae, ao = (4, 5) if tt % 2 == 0 else (6, 7)
                        qs = slice(tt * 512, (tt + 1) * 512)
                        gi = gnext

                        def qk(kc, u):
                            for hh in range(2):
                                rows = slice(64 * hh, 64 * hh + 64)
                                pe(lambda e, rows=rows, hh=hh, kc=kc, u=u, qs=qs: e.matmul(ps[:, 2 * u + hh, :], lhsT=KT[rows, kc * 128:(kc + 1) * 128], rhs=QT[rows, qs],
                                                                                  start=True, stop=True),
                                   RC("KT", kc * 128, 128) + ["QT.%d" % tt], PS(2 * u + hh))
                        qk(0, 0)
                        for kc in range(32):
                            u = kc % 2
                            i = kc % 2
                            if kc + 1 < 32:
                                qk(kc + 1, (kc + 1) % 2)
                            act(lambda e, u=u, i=i: e.activation(out=PT[i][:].rearrange("p (a c) -> p a c", a=2), in_=ps[:, 2 * u:2 * u + 2, :],
                                                                 func=AF.Exp, scale=0.125), [], PS(2 * u, 2 * u + 1) + [*PTR(i)])
                            pe(lambda e, kc=kc, i=i, ae=ae: e.matmul(ps[:, ae, :], lhsT=VT[:, kc, 0:128], rhs=PT[i][:, 0:512], start=(kc == 0), stop=(kc == 31)),
                               ["VT", *PTR(i)], PS(ae))
                            pe(lambda e, kc=kc, i=i, ao=ao: e.matmul(ps[:, ao, :], lhsT=VT[:, kc, 64:192], rhs=PT[i][:, 512:1024], start=(kc == 0), stop=(kc == 31)),
                               ["VT", *PTR(i)], PS(ao))
                        if tt + 1 < 8:
                            gnext = gate_part(gslot, tt + 1, 7 if tt % 2 == 0 else 5)
                        gate_tail(gi, b, 1, c, tt, ps[0:64, ae, :], ps[64:128, ao, :], ps[64:128, ae, :], ps[0:64, ao, :],
                                  PS(ae, ao), True)

        def phase_b():
            for k3 in range(3):
                gpd(lambda e, k3=k3: e.dma_start(out=WMG[:, :, k3 * 1024:(k3 + 1) * 1024], in_=w_in_v[:, :, MG0 + k3 * 1024:MG0 + (k3 + 1) * 1024]),
                    [], ["WMG%d" % k3], "WMG%d" % k3)
                gpd(lambda e, k3=k3: e.dma_start(out=WBR[k3][:], in_=w_br_v[k3]), [], ["WBR%d" % k3], "WBR%d" % k3)
            gpd(lambda e: e.dma_start(out=WOUT[:], in_=w_out_v), [], ["WOUT"], "WOUT")
            tiles = [(b, tt) for b in range(NB) for tt in range(8)]

            def loads(n):
                b, tt = tiles[n]
                i = n % 2
                spd(lambda e: e.dma_start(out=XB[i][:], in_=x[b, tt * 512:(tt + 1) * 512, :].rearrange("(j p) c -> p j c", p=128)),
                    [], ["XB%d" % i], "XB%d" % i)
                for br in range(3):
                    spd(lambda e, br=br: e.dma_start(out=UB[i][:, br, :, :], in_=uscr[b, br, :, :, tt * 512:(tt + 1) * 512].rearrange("k p t -> p k t")),
                        [], ["UB%d" % i], "UB%d_%d" % (i, br))

            def norm_p1(i, j):
                k = j % 2
                sc = 4 * j
                gp(lambda e: e.memset(SS[:, sc:sc + 1], 0.0), [], ["SSn%d" % j])
                act(lambda e: e.activation(out=JUNKB[:], in_=XB[i][:, j, :], func=AF.Square, accum_out=SS[:, sc:sc + 1]),
                    ["XB%d" % i, "SSn%d" % j], ["JUNKB", "SSn%d" % j])
                dve(lambda e: e.tensor_scalar(out=SS[:, sc + 1:sc + 2], in0=SS[:, sc:sc + 1], scalar1=1.0 / D, scalar2=EPS, op0=ALU.mult, op1=ALU.add),
                    ["SSn%d" % j], ["SSn%d" % j])
                gp(lambda e: e.tensor_tensor(out=SS[:, sc + 2:sc + 3], in0=SS[:, sc + 1:sc + 2], in1=NEGH[:, 0:1], op=ALU.pow), ["SSn%d" % j, "const"], ["SSn%d" % j])
                dve(lambda e: e.scalar_tensor_tensor(out=XNB[k][:], in0=XB[i][:, j, :], scalar=SS[:, sc + 2:sc + 3], in1=gpre_bc[:],
                                                     op0=ALU.mult, op1=ALU.mult), ["XB%d" % i, "SSn%d" % j, "const"], ["XNB%d" % k])
                return k

            def norm_p2(j, k):
                for kc in range(8):
                    pe(lambda e, kc=kc: e.transpose(out=psT[:, kc * 128:(kc + 1) * 128], in_=XNB[k][:, kc * 128:(kc + 1) * 128], identity=ident[:]),
                       ["XNB%d" % k, "const"], PS(7))
                act(lambda e: e.activation(out=HB[:, :, j * 128:(j + 1) * 128], in_=psT.rearrange("p (k t) -> p k t", k=8), func=AF.Copy),
                    [], PS(7) + ["HB"])

            def build_hb(i):
                kk = norm_p1(i, 0)
                for j in range(4):
                    kn_ = norm_p1(i, j + 1) if j + 1 < 4 else None
                    norm_p2(j, kk)
                    kk = kn_

            def tile_body(n, b, tt, i):
                have_next = n + 1 < len(tiles)
                if have_next:
                    loads(n + 1)
                for c in range(8):
                    for br in range(3):
                        q = (c * 3 + br) % 2
                        bm_, by_ = (0, 1) if q == 0 else (2, 3)
                        for kc in range(8):
                            pe(lambda e, kc=kc, br=br, c=c, bm_=bm_: e.matmul(ps[:, bm_, :], lhsT=WMG[:, kc, br * 1024 + c * 128:br * 1024 + (c + 1) * 128], rhs=HB[:, kc, :],
                                                                             start=(kc == 0), stop=(kc == 7)), ["WMG%d" % br, "HB"], PS(bm_))
                        for k4 in range(4):
                            pe(lambda e, k4=k4, br=br, c=c, by_=by_: e.matmul(ps[:, by_, :], lhsT=WBR[br][:, k4, c * 128:(c + 1) * 128], rhs=UB[i][:, br, k4, :],
                                                                             start=(k4 == 0), stop=(k4 == 3)), ["WBR%d" % br, "UB%d" % i], PS(by_))
                        act(lambda e, q=q, bm_=bm_, br=br, c=c: e.activation(out=TGB[q][:], in_=ps[:, bm_, :], func=AF.Tanh, scale=0.5,
                                                                             bias=bmh[:, br * 8 + c:br * 8 + c + 1]), ["const"], PS(bm_) + ["TGB%d" % q])
                        if br == 0:
                            dve(lambda e, q=q, by_=by_: e.scalar_tensor_tensor(out=MACC[:], in0=TGB[q][:], scalar=1.0, in1=ps[:, by_, :], op0=ALU.add, op1=ALU.mult),
                                ["TGB%d" % q], PS(by_) + ["MACC"])
                        else:
                            dve(lambda e, q=q, by_=by_: e.scalar_tensor_tensor(out=TMPB[0][:], in0=TGB[q][:], scalar=1.0, in1=ps[:, by_, :], op0=ALU.add, op1=ALU.mult),
                                ["TGB%d" % q], PS(by_) + ["TMPB0"])
                            gp(lambda e, q=q: e.tensor_tensor(out=MACC[:], in0=MACC[:], in1=TMPB[0][:], op=ALU.add), ["MACC", "TMPB0"], ["MACC"])
                    act(lambda e, c=c: e.activation(out=MT[:, c, :], in_=MACC[:], func=AF.Copy, scale=0.5), ["MACC"], ["MT"])
                inext = (n + 1) % 2
                kk = norm_p1(inext, 0) if have_next else None
                for j in range(4):
                    k = j % 2
                    b0, b1 = (4, 5) if j % 2 == 0 else (0, 1)
                    for hf, bk in ((0, b0), (1, b1)):
                        for kc in range(8):
                            pe(lambda e, kc=kc, j=j, hf=hf, bk=bk: e.matmul(ps[:, bk, :], lhsT=MT[:, kc, j * 128:(j + 1) * 128], rhs=WOUT[:, kc, hf * 512:(hf + 1) * 512],
                                                                           start=(kc == 0), stop=(kc == 7)), ["MT", "WOUT"], PS(bk))
                    if have_next:
                        kn_ = norm_p1(inext, j + 1) if j + 1 < 4 else None
                        norm_p2(j, kk)
                        kk = kn_
                    sc = 32 + 4 * j
                    gp(lambda e, sc=sc: e.memset(SS[:, sc:sc + 1], 0.0), [], ["SSp%d" % j])
                    act(lambda e, sc=sc, b0=b0: e.activation(out=JUNKB[:].rearrange("p (a c) -> p a c", a=2), in_=ps[:, b0:b0 + 2, :], func=AF.Square,
                                                           accum_out=SS[:, sc:sc + 1]), ["SSp%d" % j], PS(b0, b1) + ["JUNKB", "SSp%d" % j])
                    dve(lambda e, sc=sc: e.tensor_scalar(out=SS[:, sc + 1:sc + 2], in0=SS[:, sc:sc + 1], scalar1=1.0 / D, scalar2=EPS, op0=ALU.mult, op1=ALU.add),
                        ["SSp%d" % j], ["SSp%d" % j])
                    gp(lambda e, sc=sc: e.tensor_tensor(out=SS[:, sc + 2:sc + 3], in0=SS[:, sc + 1:sc + 2], in1=NEGH[:, 0:1], op=ALU.pow), ["SSp%d" % j, "const"], ["SSp%d" % j])
                    dve(lambda e, k=k, sc=sc, b0=b0: e.scalar_tensor_tensor(out=YT[k][:].rearrange("p (a c) -> p a c", a=2), in0=ps[:, b0:b0 + 2, :],
                                                                          scalar=SS[:, sc + 2:sc + 3], in1=gpost_bc[:].rearrange("p (a c) -> p a c", a=2),
                                                                          op0=ALU.mult, op1=ALU.mult), ["SSp%d" % j, "const"], PS(b0, b1) + ["YT%d" % k])
                    gp(lambda e, k=k, j=j: e.tensor_tensor(out=YT[k][:], in0=YT[k][:], in1=XB[i][:, j, :], op=ALU.add), ["YT%d" % k, "XB%d" % i], ["YT%d" % k])
                    spd(lambda e, k=k, j=j: e.dma_start(out=y[b, tt * 512 + j * 128:tt * 512 + (j + 1) * 128, :], in_=YT[k][:]),
                        ["YT%d" % k], ["yout%d" % k], "YT%d" % k)

            loads(0)
            build_hb(0)
            for n, (b, tt) in enumerate(tiles):
                tile_body(n, b, tt, n % 2)
            S.op("sp", lambda e: e.nop(), ["yout0", "yout1"], [])

        def gen():
            ctr.update({k: 0 for k in ctr})
            setup()
            for b in range(NB if "2" in STAGES else 1):
                if "0" in STAGES:
                    stage0(b)
                if "m" in STAGES:
                    mixer_m(b)
                if "a" in STAGES:
                    mixer_a(b)
                if "b" in STAGES:
                    mixer_b(b)
            S.barrier(["pe", "act", "dve", "pool", "sp"])
            if "p" in STAGES:
                phase_b()


        S.dry = True
        gen()
        S.dry = False
        gen()
        keys = S.finalize()
        semmap = {k: es.enter_context(nc.semaphore("s%d" % n)) for n, k in enumerate(keys)}
        with nc.Block() as block:
            @block.sync
            def _(e):
                S.emit("sp", e, semmap)

            @block.tensor
            def _(e):
                S.emit("pe", e, semmap)

            @block.scalar
            def _(e):
                S.emit("act", e, semmap)

            @block.vector
            def _(e):
                S.emit("dve", e, semmap)

            @block.gpsimd
            def _(e):
                S.emit("pool", e, semmap)
    return nc


_CACHE = {}


def kernel(x, mem, g_pre, w_in, b_merge, q_norm, k_norm, g_mem, w_mem_kv, w_br_a, w_br_b, w_br_m, w_out, g_post):
    f = lambda a: np.ascontiguousarray(np.asarray(a), dtype=np.float32)
    x = f(x); mem = f(mem)
    consts = _const_tables()
    shared = {
        "w_in": f(w_in)[0], "w_mem": f(w_mem_kv)[0], "w_br0": f(w_br_a)[0], "w_br1": f(w_br_b)[0], "w_br2": f(w_br_m)[0],
        "w_out": f(w_out)[0], "g_pre": f(g_pre), "g_mem": f(g_mem), "g_post": f(g_post),
        "bm": np.ascontiguousarray(f(b_merge)[0].reshape(24, 128).T),
        "qn": np.ascontiguousarray(np.tile(f(q_norm)[0], 2).reshape(128, 1)),
        "kn": np.ascontiguousarray(np.tile(f(k_norm)[0], 2).reshape(128, 1)),
    }
    shared.update(consts)
    if "nc" not in _CACHE:
        _CACHE["nc"] = build_program()
    nc = _CACHE["nc"]
    in_maps = []
    for c in range(NCORES):
        m = dict(shared)
        m["x"] = np.ascontiguousarray(x[c * NB:(c + 1) * NB])
        m["mem"] = np.ascontiguousarray(mem[c * NB:(c + 1) * NB])
        in_maps.append(m)
    res = run_bass_kernel_spmd(nc, in_maps, core_ids=list(range(NCORES)))
    out = np.concatenate([np.asarray(r["y"]) for r in res.results], axis=0)
    return out.astype(np.float32)
```

```python
import numpy as np
from contextlib import ExitStack
import concourse.bass as bass
import concourse.mybir as mybir
from concourse.bass_utils import run_bass_kernel_spmd

F32 = mybir.dt.float32
BF16 = mybir.dt.bfloat16
AF = mybir.ActivationFunctionType
ALU = mybir.AluOpType

NCORES = 8
NB = 2
SEQ = 4096
D = 1024
INW = 10496
AQ0, AK0, AV0, BQ0, BK0, BV0, MQ0, GA0, GB0, GM0, MG0 = 0, 1536, 3072, 4608, 5120, 5248, 5376, 5888, 6400, 6912, 7424
DILS = (1, 4, 16)
EPS = 1e-6
POOL_ELEMS = 102400
import os
STAGES = os.environ.get("MK_STAGES", "0mabp2")
ASUB = os.environ.get("MK_ASUB", "vkte")
AGR = os.environ.get("MK_AG", "012")


class Op:
    __slots__ = ("eng", "fn", "deps", "sig", "is_dma", "dma_key", "waits", "inc")

    def __init__(self, eng, fn, is_dma=False, dma_key=None):
        self.eng = eng
        self.fn = fn
        self.deps = []
        self.is_dma = is_dma
        self.dma_key = dma_key
        self.sig = None
        self.waits = []
        self.inc = None


class Res:
    __slots__ = ("w", "r")

    def __init__(self):
        self.w = None
        self.r = {}


class Sched:
    def __init__(self):
        self.ops = []
        self.res = {}
        self.dry = False
        self.last = {}
        self.last_dma = {}

    def _res(self, k):
        r = self.res.get(k)
        if r is None:
            r = self.res[k] = Res()
        return r

    def op(self, eng, fn, reads=(), writes=(), dma_key=None):
        if self.dry:
            return None
        o = Op(eng, fn, is_dma=dma_key is not None, dma_key=dma_key)
        deps = []
        for k in reads:
            r = self._res(k)
            if r.w is not None:
                deps.append((r.w, 0))
        for k in writes:
            r = self._res(k)
            if r.w is not None:
                deps.append((r.w, 1))
            for ro in r.r.values():
                deps.append((ro, 2))
        seen = set()
        for d, kind in deps:
            if id(d) in seen or d is o:
                continue
            if d.eng == eng and not d.is_dma and not o.is_dma:
                if eng == "pe" or kind != 0:
                    continue
            seen.add(id(d))
            o.deps.append(d)
        rk = ("dma", len(self.ops)) if o.is_dma else eng
        for k in reads:
            self._res(k).r[rk] = o
        for k in writes:
            r = self._res(k)
            r.w = o
            r.r = {}
        self.ops.append(o)
        if o.is_dma:
            self.last_dma[dma_key] = o
        else:
            self.last[eng] = o
        return o

    def barrier(self, engines):
        if self.dry:
            return
        prev = list(self.last.values()) + list(self.last_dma.values())
        for e in engines:
            o = Op(e, lambda eng: eng.nop())
            o.deps = [p for p in prev]
            self.ops.append(o)
            self.last[e] = o
        self.res = {}

    def finalize(self):
        need = set()
        for o in self.ops:
            for d in o.deps:
                need.add(id(d))
        cnt = {}
        for o in self.ops:
            if o.is_dma:
                k = ("dma", o.dma_key)
                cnt[k] = cnt.get(k, 0) + 16
                o.sig = (k, cnt[k])
                o.inc = (k, 16)
            elif id(o) in need:
                k = ("eng", o.eng)
                cnt[k] = cnt.get(k, 0) + 1
                o.sig = (k, cnt[k])
                o.inc = (k, 1)
        waited = {}
        for o in self.ops:
            w = waited.setdefault(o.eng, {})
            best = {}
            for d in o.deps:
                k, v = d.sig
                if w.get(k, 0) >= v:
                    continue
                if best.get(k, 0) < v:
                    best[k] = v
            for k, v in best.items():
                w[k] = v
                o.waits.append((k, v))
        return sorted(cnt.keys(), key=str)

    def emit(self, engname, eng, semmap):
        for o in self.ops:
            if o.eng != engname:
                continue
            for k, v in o.waits:
                eng.wait_ge(semmap[k], v)
            ins = o.fn(eng)
            if o.inc is not None:
                ins.then_inc(semmap[o.inc[0]], o.inc[1])


def RC(name, c0, n, gran=512):
    return ["%s.%d" % (name, t) for t in range(c0 // gran, (c0 + n - 1) // gran + 1)]


def PTR(i):
    return ["PT%d.0" % i, "PT%d.1" % i]


def PS(*banks):
    return ["ps%d" % b for b in banks]


def _const_tables():
    t = np.arange(SEQ, dtype=np.float64)
    d = np.arange(128)
    dd = d % 64
    fA = dd % 32
    invA = 10000.0 ** (-fA.astype(np.float64) * 2.0 / 64.0)
    angA = invA[:, None] * t[None, :]
    cosA = np.cos(angA)
    sinA = np.sin(angA) * np.where(dd < 32, -1.0, 1.0)[:, None]
    PA = np.zeros((128, 128), np.float32)
    for i in range(128):
        part = i + 32 if (i % 64) < 32 else i - 32
        PA[part, i] = 1.0
    j = dd % 32
    fB = j % 16
    invB = 10000.0 ** (-fB.astype(np.float64) * 2.0 / 32.0)
    rowp = np.floor(t / 64.0)
    colp = t % 64
    posB = np.where((dd // 32)[:, None] == 0, rowp[None, :], colp[None, :])
    angB = invB[:, None] * posB
    cosB = np.cos(angB)
    sinB = np.sin(angB) * np.where(j < 16, -1.0, 1.0)[:, None]
    PBm = np.zeros((128, 128), np.float32)
    for i in range(128):
        part = i + 16 if (i % 32) < 16 else i - 16
        PBm[part, i] = 1.0
    BD = np.zeros((128, 128), np.float32)
    BD[0:64, 0:64] = 1.0
    BD[64:128, 64:128] = 1.0
    ii = np.arange(128)[:, None]
    cc = np.arange(128)[None, :]
    M1 = (ii >= cc).astype(np.float32)
    M2 = (ii <= cc).astype(np.float32)
    MF = np.concatenate([M1, M2] * 4, axis=1)
    ident = np.eye(128, dtype=np.float32)
    f = lambda a: np.ascontiguousarray(a, dtype=np.float32)
    return dict(cosA=f(cosA), sinA=f(sinA), cosB=f(cosB), sinB=f(sinB), PA=f(PA), PB=f(PBm), BD=f(BD),
                MF=f(MF), ident=f(ident))


def build_program():
    nc = bass.Bass("TRN2", target_bir_lowering=False)
    din = lambda n, s: nc.dram_tensor(n, s, F32, kind="ExternalInput").ap()
    x = din("x", [NB, SEQ, D])
    mem = din("mem", [NB, 256, D])
    w_in = din("w_in", [D, INW])
    w_mem = din("w_mem", [D, 1024])
    w_br = [din("w_br%d" % i, [512, D]) for i in range(3)]
    w_out = din("w_out", [D, D])
    g_pre = din("g_pre", [1, D])
    g_mem = din("g_mem", [1, D])
    g_post = din("g_post", [1, D])
    bm = din("bm", [128, 24])
    qn = din("qn", [128, 1])
    kn = din("kn", [128, 1])
    tab = {k: din(k, [128, SEQ]) for k in ("cosA", "sinA", "cosB", "sinB")}
    cPA = din("PA", [128, 128])
    cPB = din("PB", [128, 128])
    cBD = din("BD", [128, 128])
    cMF = din("MF", [128, 1024])
    cID = din("ident", [128, 128])
    y = nc.dram_tensor("y", [NB, SEQ, D], F32, kind="ExternalOutput").ap()
    uscr = nc.dram_tensor("uscr", [NB, 3, 4, 128, SEQ], BF16).ap()

    S = Sched()
    with ExitStack() as es:
        pool = es.enter_context(nc.sbuf_tensor("pool", [128, POOL_ELEMS], BF16))
        ps = es.enter_context(nc.psum_tensor("ps", [128, 8, 512], F32))
        psT = ps[:, 7, :].bitcast(BF16)

        cur = [0]

        def alloc(n, dt=BF16):
            ne = n if dt == BF16 else 2 * n
            o = cur[0]
            cur[0] += ne
            assert cur[0] <= POOL_ELEMS, ("sbuf overflow", cur[0])
            a = pool[:, o:o + ne]
            return a.bitcast(F32) if dt == F32 else a

        ident = alloc(128); PA = alloc(128); PBm = alloc(128); BD = alloc(128); TWOS = alloc(128)
        MF = alloc(1024)
        gpre_bc = alloc(1024, F32); gpost_bc = alloc(1024, F32); gmem_bc = alloc(1024, F32)
        bmh = alloc(24, F32); qn_t = alloc(1, F32); kn_t = alloc(1, F32)
        NEGH = alloc(512, F32)
        EPSB = alloc(2, F32)
        SS = alloc(64, F32)
        phase_base = cur[0]

        hT = alloc(8 * SEQ).rearrange("p (k t) -> p k t", k=8)
        cosA = alloc(SEQ); sinA = alloc(SEQ)
        R1a = alloc(8192); R1b = alloc(8192)
        ACCE = R1a.bitcast(F32); ACCO = R1b.bitcast(F32)
        cosB = R1a[:, 0:SEQ]; sinB = R1a[:, SEQ:2 * SEQ]
        mnT = R1b[:, 4096:6144].rearrange("p (k t) -> p k t", k=8)
        KmT = R1b[:, 6144:7168].rearrange("p (h t) -> p h t", h=4)
        Vm = R1b[:, 7168:8192].rearrange("p (j c) -> p j c", j=2)
        QT = alloc(SEQ); KT = alloc(SEQ)
        XT = [R1a[:, 2048 * k:2048 * (k + 1)].bitcast(F32) for k in range(4)]
        XN = [R1b[:, 1024 * k:1024 * (k + 1)] for k in range(4)]
        JUNK = KT[:, 2048:3072]
        R1A = ["R1a.%d" % k for k in range(4)]
        R1B = ["R1b.%d" % k for k in range(4)] + ["R1b.m"]
        VTraw = alloc(32 * 192)
        VT = VTraw.rearrange("p (j c) -> p j c", c=192)
        VT4 = VTraw.rearrange("p (j a c) -> p j a c", a=3, c=64)
        WM = VTraw[:, 0:4096].rearrange("p (k c) -> p k c", k=8)
        NWS = 4
        WS = [alloc(1024).rearrange("p (k c) -> p k c", k=8) for _ in range(NWS)]
        QRAW = [alloc(512) for _ in range(2)]
        T0 = [alloc(512, F32) for _ in range(2)]
        T1 = [alloc(512, F32) for _ in range(2)]
        T2 = [alloc(512, F32) for _ in range(2)]
        OO = [alloc(512, F32) for _ in range(2)]
        PT = [alloc(1024) for _ in range(2)]
        SQ = [alloc(512) for _ in range(2)]
        RS = [alloc(512, F32) for _ in range(2)]
        UST = [alloc(512) for _ in range(2)]
        endA = cur[0]

        cur[0] = phase_base
        WMG = alloc(8 * 3072).rearrange("p (k c) -> p k c", k=8)
        WBR = [alloc(4 * 1024).rearrange("p (k c) -> p k c", k=4) for _ in range(3)]
        WOUT = alloc(8 * 1024).rearrange("p (k c) -> p k c", k=8)
        XB = [alloc(4 * 1024, F32).rearrange("p (j c) -> p j c", j=4) for _ in range(2)]
        HB = alloc(8 * 512).rearrange("p (k t) -> p k t", k=8)
        XNB = [alloc(1024) for _ in range(2)]
        UB = [alloc(12 * 512).rearrange("p (r k t) -> p r k t", r=3, k=4) for _ in range(2)]
        MT = alloc(8 * 512).rearrange("p (k t) -> p k t", k=8)
        TGB = [alloc(512, F32) for _ in range(2)]
        MACC = alloc(512, F32)
        TMPB = [alloc(512, F32)]
        YT = [alloc(1024, F32) for _ in range(2)]
        JUNKB = alloc(1024)
        endB = cur[0]

        pe = lambda fn, r=(), w=(): S.op("pe", fn, r, w)
        act = lambda fn, r=(), w=(): S.op("act", fn, r, w)
        dve = lambda fn, r=(), w=(): S.op("dve", fn, r, w)
        gp = lambda fn, r=(), w=(): S.op("pool", fn, r, w)
        spd = lambda fn, r, w, key: S.op("sp", fn, r, w, dma_key=key)
        gpd = lambda fn, r, w, key: S.op("pool", fn, r, w, dma_key=key)

        w_in_v = w_in.rearrange("(k p) c -> p k c", p=128)
        w_mem_v = w_mem.rearrange("(k p) c -> p k c", p=128)
        w_out_v = w_out.rearrange("(k p) c -> p k c", p=128)
        w_br_v = [w.rearrange("(k p) c -> p k c", p=128) for w in w_br]

        class WStream:
            def __init__(self):
                self.specs = []
                self.i = 0
                self.issued = 0

            def _issue(self, k):
                slot = k % NWS
                for (c0, n, d0) in self.specs[k]:
                    srcv = w_in_v
                    if c0 < 0:
                        srcv = w_mem_v
                        c0 = -c0 - 1
                    gpd(lambda e, slot=slot, c0=c0, n=n, d0=d0, srcv=srcv: e.dma_start(out=WS[slot][:, :, d0:d0 + n], in_=srcv[:, :, c0:c0 + n]),
                        [], ["WS%d" % slot], "WS%d" % slot)

            def next(self, pieces):
                if S.dry:
                    self.specs.append(pieces)
                    return 0
                k = self.i
                self.i += 1
                while self.issued < min(len(self.specs), k + NWS - 1):
                    self._issue(self.issued)
                    self.issued += 1
                return k % NWS

        W = WStream()

        def tokslice(g, p0, n):
            d = DILS[g]
            if d == 1:
                return slice(p0, p0 + n)
            L = SEQ // d
            r, m0 = p0 // L, p0 % L
            assert m0 + n <= L
            s0 = m0 * d + r
            return slice(s0, s0 + (n - 1) * d + 1, d)

        HT_ALL = ["hT.%d" % t for t in range(8)]

        def ht_res(g, p0, n):
            return RC("hT", p0, n) if DILS[g] == 1 else HT_ALL

        def proj_fm(slot, g, p0, n, bank, col0=0):
            sl = tokslice(g, p0, n)
            for kc in range(8):
                pe(lambda e, kc=kc: e.matmul(ps[:, bank, col0:col0 + n], lhsT=WS[slot][:, kc, :], rhs=hT[:, kc, sl],
                                             start=(kc == 0), stop=(kc == 7)),
                   ["WS%d" % slot] + ht_res(g, p0, n), PS(bank))

        ctr = {"rope": 0, "pt": 0, "ep": 0, "x": 0, "pj": 0}

        def rope_p1(src, src_res, src_is_psum, cs, sn, perm, sl, dst, dst_res, n, tres=("tabs",), dsplit=1, rbank=2, copy_eng="act"):
            i = ctr["rope"] % 2
            ctr["rope"] += 1
            rw = (lambda r, w: ([], r + w)) if src_is_psum else (lambda r, w: (r, w))
            r_, w_ = rw(src_res, ["QRAW%d" % i])
            if copy_eng == "act":
                act(lambda e: e.activation(out=QRAW[i][:, 0:n], in_=src, func=AF.Copy), r_, w_)
            else:
                gp(lambda e: e.tensor_copy(out=QRAW[i][:, 0:n], in_=src), r_, w_)
            r_, w_ = rw(src_res, ["T1_%d" % i])
            dve(lambda e: e.tensor_tensor(out=T1[i][:, 0:n], in0=src, in1=cs[:, sl], op=ALU.mult), r_ + list(tres), w_)
            return (i, sn, perm, sl, dst, dst_res, n, tres, dsplit, rbank)

        def rope_p2(st):
            i, sn, perm, sl, dst, dst_res, n, tres, dsplit, rbank = st
            pe(lambda e: e.matmul(ps[:, rbank, 0:n], lhsT=perm, rhs=QRAW[i][:, 0:n], start=True, stop=True),
               ["QRAW%d" % i, "const"], PS(rbank))
            dve(lambda e: e.tensor_tensor(out=T2[i][:, 0:n], in0=ps[:, rbank, 0:n], in1=sn[:, sl], op=ALU.mult),
                list(tres), PS(rbank) + ["T2_%d" % i])
            a0, a1 = T1[i][:, 0:n], T2[i][:, 0:n]
            if dsplit > 1:
                a0 = a0.rearrange("p (m r) -> p m r", r=dsplit)
                a1 = a1.rearrange("p (m r) -> p m r", r=dsplit)
            gp(lambda e: e.tensor_tensor(out=dst, in0=a0, in1=a1, op=ALU.add),
               ["T1_%d" % i, "T2_%d" % i], dst_res)

        def rope_tile(*a, **k):
            rope_p2(rope_p1(*a, **k))

        def gate_part(slot, tt, bank):
            i = ctr["ep"] % 2
            ctr["ep"] += 1
            proj_fm(slot, 0, tt * 512, 512, bank)
            act(lambda e: e.activation(out=T0[i][:], in_=ps[:, bank, :], func=AF.Tanh, scale=0.5), [], PS(bank) + ["T0_%d" % i])
            dve(lambda e: e.scalar_tensor_tensor(out=T1[i][:], in0=T0[i][:], scalar=1.0, in1=ps[:, bank, :], op0=ALU.add, op1=ALU.mult),
                ["T0_%d" % i], PS(bank) + ["T1_%d" % i])
            return i

        def gate_tail(i, b, br, chunk, tt, num_e, num_o, l_e, l_o, src_res, src_psum):
            rr, ww = ([], list(src_res)) if src_psum else (list(src_res), [])
            if num_o is None:
                dve(lambda e: e.reciprocal(out=T2[i][:], in_=l_e), rr, ww + ["T2_%d" % i])
                dve(lambda e: e.tensor_tensor(out=OO[i][:], in0=num_e, in1=T2[i][:], op=ALU.mult), rr + ["T2_%d" % i], ww + ["OO%d" % i])
            else:
                dve(lambda e: e.reciprocal(out=T2[i][0:64, :], in_=l_e), rr, ww + ["T2_%d" % i])
                dve(lambda e: e.reciprocal(out=T2[i][64:128, :], in_=l_o), rr, ww + ["T2_%d" % i])
                dve(lambda e: e.tensor_tensor(out=OO[i][0:64, :], in0=num_e, in1=T2[i][0:64, :], op=ALU.mult), rr + ["T2_%d" % i], ww + ["OO%d" % i])
                dve(lambda e: e.tensor_tensor(out=OO[i][64:128, :], in0=num_o, in1=T2[i][64:128, :], op=ALU.mult), rr + ["T2_%d" % i], ww + ["OO%d" % i])
            gp(lambda e: e.tensor_tensor(out=UST[i][:], in0=OO[i][:], in1=T1[i][:], op=ALU.mult), ["OO%d" % i, "T1_%d" % i], ["UST%d" % i])
            spd(lambda e: e.dma_start(out=uscr[b, br, chunk, :, tt * 512:(tt + 1) * 512], in_=UST[i][:]),
                ["UST%d" % i], [], "UST%d" % i)

        def norm_rows(src_dram, gbc, i, sscol):
            xr = ["R1a.%d" % i]
            nr = ["R1b.%d" % i]
            spd(lambda e: e.dma_start(out=XT[i][:], in_=src_dram), [], xr, "XT%d" % i)
            gp(lambda e: e.memset(SS[:, sscol:sscol + 1], 0.0), [], ["SS%d" % sscol])
            act(lambda e: e.activation(out=JUNK[:], in_=XT[i][:], func=AF.Square, accum_out=SS[:, sscol:sscol + 1]),
                xr + ["SS%d" % sscol], ["KT.4", "KT.5", "SS%d" % sscol])
            dve(lambda e: e.tensor_scalar(out=SS[:, sscol + 1:sscol + 2], in0=SS[:, sscol:sscol + 1], scalar1=1.0 / D, scalar2=EPS,
                                          op0=ALU.mult, op1=ALU.add), ["SS%d" % sscol], ["SS%d" % sscol])
            gp(lambda e: e.tensor_tensor(out=SS[:, sscol + 2:sscol + 3], in0=SS[:, sscol + 1:sscol + 2], in1=NEGH[:, 0:1], op=ALU.pow),
               ["SS%d" % sscol, "const"], ["SS%d" % sscol])
            dve(lambda e: e.scalar_tensor_tensor(out=XN[i][:], in0=XT[i][:], scalar=SS[:, sscol + 2:sscol + 3], in1=gbc[:],
                                                 op0=ALU.mult, op1=ALU.mult), xr + ["SS%d" % sscol, "const"], nr)
            return nr

        def transpose_rows(i, nr, dst3, dst_res):
            for kc in range(8):
                pe(lambda e, kc=kc: e.transpose(out=psT[:, kc * 128:(kc + 1) * 128], in_=XN[i][:, kc * 128:(kc + 1) * 128], identity=ident[:]),
                   nr + ["const"], PS(7))
            act(lambda e: e.activation(out=dst3, in_=psT.rearrange("p (k t) -> p k t", k=8), func=AF.Copy), [], PS(7) + dst_res)

        def setup():
            for (dst, src, nm) in ((ident, cID, "cid"), (PA, cPA, "cpa"), (PBm, cPB, "cpb"), (BD, cBD, "cbd"), (MF, cMF, "cmf")):
                gpd(lambda e, dst=dst, src=src: e.dma_start(out=dst[:], in_=src), [], ["const"], nm)
            for (dst, src, nm) in ((gpre_bc, g_pre, "cg1"), (gpost_bc, g_post, "cg2"), (gmem_bc, g_mem, "cg3")):
                spd(lambda e, dst=dst, src=src: e.dma_start(out=dst[:], in_=src.partition_broadcast(128)), [], ["const"], nm)
            spd(lambda e: e.dma_start(out=bmh[:], in_=bm), [], ["const"], "cbm")
            spd(lambda e: e.dma_start(out=qn_t[:], in_=qn), [], ["const"], "cqn")
            spd(lambda e: e.dma_start(out=kn_t[:], in_=kn), [], ["const"], "ckn")
            gp(lambda e: e.memset(TWOS[:], 2.0), [], ["const"])
            gp(lambda e: e.memset(NEGH[:], -0.5), [], ["const"])
            gp(lambda e: e.memset(EPSB[:], EPS), [], ["const"])
            dve(lambda e: e.tensor_scalar(out=bmh[:], in0=bmh[:], scalar1=0.5, scalar2=None, op0=ALU.mult), ["const"], ["const"])
            gpd(lambda e: e.dma_start(out=cosA[:], in_=tab["cosA"]), [], ["tabs"], "tcA")
            gpd(lambda e: e.dma_start(out=sinA[:], in_=tab["sinA"]), [], ["tabs"], "tsA")

        def stage0(b):
            pend = None
            for tt in range(32):
                i = tt % 4
                nr = norm_rows(x[b, tt * 128:(tt + 1) * 128, :], gpre_bc, i, 4 * (tt % 8))
                if pend is not None:
                    transpose_rows(*pend)
                pend = (i, nr, hT[:, :, tt * 128:(tt + 1) * 128], ["hT.%d" % (tt // 4)])
            transpose_rows(*pend)

        def mixer_m(b):
            for j in range(2):
                nr = norm_rows(mem[b, j * 128:(j + 1) * 128, :], gmem_bc, j, 32 + 4 * j)
                transpose_rows(j, nr, mnT[:, :, j * 128:(j + 1) * 128], ["R1b.m"])
            gpd(lambda e: e.dma_start(out=WM[:], in_=w_mem_v[:, :, 512:1024]), [], ["VT"], "WM")
            for j in range(2):
                for kc in range(8):
                    pe(lambda e, kc=kc, j=j: e.matmul(ps[:, 0, :], lhsT=mnT[:, kc, j * 128:(j + 1) * 128], rhs=WM[:, kc, :],
                                                       start=(kc == 0), stop=(kc == 7)), ["R1b.m", "VT"], PS(0))
                act(lambda e, j=j: e.activation(out=Vm[:, j, :], in_=ps[:, 0, :], func=AF.Copy), [], PS(0) + ["R1b.m"])
            for h in range(4):
                slot = W.next([(-(h * 128) - 1, 128, 0)])
                for kc in range(8):
                    pe(lambda e, kc=kc, slot=slot: e.matmul(ps[:, 1, 0:256], lhsT=WS[slot][:, kc, :], rhs=mnT[:, kc, :],
                                                             start=(kc == 0), stop=(kc == 7)), ["WS%d" % slot, "R1b.m"], PS(1))
                act(lambda e, h=h: e.activation(out=KmT[:, h, :], in_=ps[:, 1, 0:256], func=AF.Copy), [], PS(1) + ["R1b.m"])
            scale = 128.0 ** -0.5
            for h in range(4):
                slot = W.next([(MQ0 + h * 128, 128, 0)])
                for tt in range(8):
                    bank = tt % 2
                    proj_fm(slot, 0, tt * 512, 512, bank)
                    act(lambda e, tt=tt, bank=bank: e.activation(out=QT[:, tt * 512:(tt + 1) * 512], in_=ps[:, bank, :], func=AF.Copy),
                        [], PS(bank) + ["QT.%d" % tt])
                gslot = W.next([(GM0 + h * 128, 128, 0)])
                pend = None
                for tt in range(8):
                    i = ctr["pt"] % 2
                    ctr["pt"] += 1
                    ab, lb = (5, 6) if tt % 2 == 0 else (0, 1)
                    gb_ = 2 if tt % 2 == 0 else 7
                    for j in range(2):
                        pe(lambda e, j=j, tt=tt, h=h: e.matmul(ps[:, 3 + j, :], lhsT=KmT[:, h, j * 128:(j + 1) * 128], rhs=QT[:, tt * 512:(tt + 1) * 512],
                                                                 start=True, stop=True), ["R1b.m", "QT.%d" % tt], PS(3 + j))
                    gi = gate_part(gslot, tt, gb_)
                    act(lambda e, i=i: e.activation(out=PT[i][:].rearrange("p (a c) -> p a c", a=2), in_=ps[:, 3:5, :], func=AF.Exp, scale=scale),
                        [], PS(3, 4) + [*PTR(i)])
                    for j in range(2):
                        pe(lambda e, j=j, i=i, h=h, ab=ab: e.matmul(ps[:, ab, :], lhsT=Vm[:, j, h * 128:(h + 1) * 128], rhs=PT[i][:, j * 512:(j + 1) * 512],
                                                                     start=(j == 0), stop=(j == 1)), ["R1b.m", *PTR(i)], PS(ab))
                    for j in range(2):
                        pe(lambda e, j=j, i=i, lb=lb: e.matmul(ps[:, lb, :], lhsT=TWOS[:], rhs=PT[i][:, j * 512:(j + 1) * 512],
                                                                start=(j == 0), stop=(j == 1)), ["const", *PTR(i)], PS(lb))
                    if pend is not None:
                        gate_tail(*pend)
                    pend = (gi, b, 2, h, tt, ps[:, ab, :], None, ps[:, lb, :], None, PS(ab, lb), True)
                gate_tail(*pend)

        def a_units(g):
            d = DILS[g]
            L = SEQ // d
            batches = []
            for r in range(d):
                pc = r * L
                units = [(pc, 64, [pc // 128], 1)]
                for a in range(L // 128 - 1):
                    units.append((pc + 128 * a + 64, 128, [pc // 128 + a, pc // 128 + a + 1], 0))
                units.append((pc + L - 64, 64, [(pc + L) // 128 - 1], 2))
                curb = None
                for u in units:
                    if curb is None or (u[0] + u[1] - curb[0]) > 512:
                        curb = [u[0], 0, []]
                        batches.append(curb)
                    curb[2].append(u)
                    curb[1] = u[0] + u[1] - curb[0]
            return batches

        def mixer_a_job(b, g, hp):
            d = DILS[g]
            L = SEQ // d
            ntile = min(512, L)
            cq = g * 512 + hp * 128
            slot = W.next([(AV0 + cq, 128, 0)])
            for j4 in range(8 if "v" in ASUB else 0):
                bank = j4 % 2
                for jj in range(4):
                    j = j4 * 4 + jj
                    sl = tokslice(g, 128 * j, 128)
                    for kc in range(8):
                        pe(lambda e, kc=kc, sl=sl, jj=jj, bank=bank, slot=slot: e.matmul(ps[:, bank, jj * 128:(jj + 1) * 128], lhsT=hT[:, kc, sl], rhs=WS[slot][:, kc, :],
                                                                                  start=(kc == 0), stop=(kc == 7)),
                           ["WS%d" % slot] + ht_res(g, 128 * j, 128), PS(bank))
                act(lambda e, j4=j4, bank=bank: e.activation(out=VT4[:, j4 * 4:(j4 + 1) * 4, 0:3:2, :],
                                                             in_=ps[:, bank, :].rearrange("p (j a c) -> p j a c", j=4, a=2), func=AF.Copy),
                    [], PS(bank) + ["VT"])
            for (c0, dstT, nm) in ((AK0 + cq, KT, "KT"), (AQ0 + cq, QT, "QT")):
                slot = W.next([(c0, 128, 0)])
                pend = None
                for tt in range(8 if "k" in ASUB else 0):
                    bank = tt % 2
                    proj_fm(slot, 0, tt * 512, 512, bank)
                    if d == 1:
                        dst = dstT[:, tt * 512:(tt + 1) * 512]
                        dres = RC(nm, tt * 512, 512)
                    else:
                        dst = dstT[:, :].rearrange("p (r m) -> p m r", r=d)[:, tt * 512 // d:(tt + 1) * 512 // d, :]
                        dres = RC(nm, 0, SEQ)
                    st = rope_p1(ps[:, bank, :], PS(bank), True, cosA, sinA, PA[:], slice(tt * 512, (tt + 1) * 512),
                                 dst, dres, 512, dsplit=d)
                    if pend is not None:
                        rope_p2(pend)
                    pend = st
                if pend is not None:
                    rope_p2(pend)
            if "t" not in ASUB:
                return
            allu = []
            for bi, (plo, width, units) in enumerate(a_units(g)):
                accs = (5, 6) if bi % 2 == 0 else (0, 1)
                for ui, u in enumerate(units):
                    allu.append((plo, width, accs, u, ui == len(units) - 1))
            supers = [allu[k:k + 2] for k in range(0, len(allu), 2)]
            OFFS = {0: [0, 128], 1: [192], 2: [0]}

            def emit_qk(sidx):
                banks = (3, 4) if sidx % 2 == 0 else (2, 7)
                for si, (plo, width, accs, (q0, nq, ktiles, kind), last) in enumerate(supers[sidx]):
                    for hh in range(2):
                        rows = slice(64 * hh, 64 * hh + 64)
                        for t, kt in enumerate(ktiles):
                            co = si * 256 + OFFS[kind][t]
                            pe(lambda e, rows=rows, kt=kt, co=co, q0=q0, nq=nq, bk=banks[hh]: e.matmul(ps[:, bk, co:co + nq], lhsT=KT[rows, kt * 128:(kt + 1) * 128],
                                                                                                 rhs=QT[rows, q0:q0 + nq], start=True, stop=True),
                               RC("KT", kt * 128, 128) + RC("QT", q0, nq), PS(banks[hh]))

            def emit_rest(sidx):
                i = sidx % 2
                banks = (3, 4) if i == 0 else (2, 7)
                for hh in range(2):
                    half = PT[i][:, hh * 512:(hh + 1) * 512]
                    pr = "PT%d.%d" % (i, hh)
                    act(lambda e, half=half, bk=banks[hh]: e.activation(out=half, in_=ps[:, bk, :], func=AF.Exp, scale=0.125), [], PS(banks[hh]) + [pr])
                    dve(lambda e, half=half: e.tensor_tensor(out=half, in0=half, in1=MF[:, 0:512], op=ALU.mult), [pr, "const"], [pr])
                evacs = []
                for hh in range(2):
                    pr = "PT%d.%d" % (i, hh)
                    for si, (plo, width, accs, (q0, nq, ktiles, kind), last) in enumerate(supers[sidx]):
                        c0 = q0 - plo
                        accb = accs[hh]
                        for t, kt in enumerate(ktiles):
                            co = hh * 512 + si * 256 + OFFS[kind][t]
                            pe(lambda e, hh=hh, accb=accb, kt=kt, co=co, c0=c0, nq=nq, t=t, nk=len(ktiles):
                               e.matmul(ps[:, accb, c0:c0 + nq], lhsT=VT[:, kt, 64 * hh:64 * hh + 128], rhs=PT[i][:, co:co + nq],
                                        start=(t == 0), stop=(t == nk - 1)),
                               ["VT", pr], PS(accb))
                        if last and hh == 1:
                            evacs.append((plo, width, accs))
                for (plo, width, accs) in evacs:
                    ae, ao = accs
                    sl = tokslice(g, plo, width)
                    if g == 0:
                        act(lambda e, sl=sl, ae=ae, width=width: e.activation(out=ACCE[:, sl], in_=ps[:, ae, 0:width], func=AF.Copy), [], PS(ae) + R1A)
                        act(lambda e, sl=sl, ao=ao, width=width: e.activation(out=ACCO[:, sl], in_=ps[:, ao, 0:width], func=AF.Copy), [], PS(ao) + R1B)
                    else:
                        dve(lambda e, sl=sl, ae=ae, width=width: e.tensor_tensor(out=ACCE[:, sl], in0=ps[:, ae, 0:width], in1=ACCE[:, sl], op=ALU.add),
                            R1A, PS(ae) + R1A)
                        dve(lambda e, sl=sl, ao=ao, width=width: e.tensor_tensor(out=ACCO[:, sl], in0=ps[:, ao, 0:width], in1=ACCO[:, sl], op=ALU.add),
                            R1B, PS(ao) + R1B)

            emit_qk(0)
            for sidx in range(len(supers)):
                if sidx + 1 < len(supers):
                    emit_qk(sidx + 1)
                emit_rest(sidx)

        def mixer_a(b):
            gp(lambda e: e.memset(VT[:, :, 64:128], 2.0), [], ["VT"])
            for hp in range(4):
                for g in range(3):
                    if str(g) in AGR:
                        mixer_a_job(b, g, hp)
                if "e" not in ASUB:
                    continue
                gslot = W.next([(GA0 + hp * 128, 128, 0)])
                for m in range(4):
                    gis = [gate_part(gslot, 2 * m + q, 2 + q) for q in range(2)]
                    for q in range(2):
                        tt = 2 * m + q
                        i = gis[q]
                        cs = slice(tt * 512, (tt + 1) * 512)
                        act(lambda e, i=i, cs=cs: e.activation(out=T2[i][0:64, :], in_=ACCE[64:128, cs], func=AF.Ln), R1A, ["T2_%d" % i])
                        act(lambda e, i=i, cs=cs: e.activation(out=T2[i][64:128, :], in_=ACCO[0:64, cs], func=AF.Ln), R1B, ["T2_%d" % i])
                        act(lambda e, i=i: e.activation(out=T2[i][:], in_=T2[i][:], func=AF.Exp, scale=-1.0), ["T2_%d" % i], ["T2_%d" % i])
                        dve(lambda e, i=i, cs=cs: e.tensor_tensor(out=OO[i][0:64, :], in0=ACCE[0:64, cs], in1=T2[i][0:64, :], op=ALU.mult),
                            R1A + ["T2_%d" % i], ["OO%d" % i])
                        dve(lambda e, i=i, cs=cs: e.tensor_tensor(out=OO[i][64:128, :], in0=ACCO[64:128, cs], in1=T2[i][64:128, :], op=ALU.mult),
                            R1B + ["T2_%d" % i], ["OO%d" % i])
                        gp(lambda e, i=i: e.tensor_tensor(out=UST[i][:], in0=OO[i][:], in1=T1[i][:], op=ALU.mult), ["OO%d" % i, "T1_%d" % i], ["UST%d" % i])
                        spd(lambda e, i=i, tt=tt, hp=hp: e.dma_start(out=uscr[b, 0, hp, :, tt * 512:(tt + 1) * 512], in_=UST[i][:]),
                            ["UST%d" % i], [], "UST%d" % i)

        def qk_norm_rope(slot, gain, dstT, nm):
            PB4 = (0, 1, 4, 5)
            sts = {}

            def stA(tt):
                bank = PB4[tt % 4]
                i = tt % 2
                proj_fm(slot, 0, tt * 512, 512, bank)
                act(lambda e: e.activation(out=SQ[i][:], in_=ps[:, bank, :], func=AF.Square), [], PS(bank) + ["SQ%d" % i])

            def stB(tt):
                bank = PB4[tt % 4]
                i = tt % 2
                pe(lambda e: e.matmul(ps[:, 2, :], lhsT=BD[:], rhs=SQ[i][:], start=True, stop=True), ["SQ%d" % i, "const"], PS(2))
                act(lambda e: e.activation(out=RS[i][:], in_=ps[:, 2, :], func=AF.Ln, scale=1.0 / 64.0, bias=EPSB[:, 0:1]), ["const"], PS(2) + ["RS%d" % i])
                act(lambda e: e.activation(out=RS[i][:], in_=RS[i][:], func=AF.Exp, scale=-0.5), ["RS%d" % i], ["RS%d" % i])
                dve(lambda e: e.scalar_tensor_tensor(out=T0[i][:], in0=ps[:, bank, :], scalar=gain[:, 0:1], in1=RS[i][:],
                                                     op0=ALU.mult, op1=ALU.mult), ["RS%d" % i, "const"], PS(bank) + ["T0_%d" % i])
                sl = slice(tt * 512, (tt + 1) * 512)
                sts[tt] = rope_p1(T0[i][:], ["T0_%d" % i], False, cosB, sinB, PBm[:], sl, dstT[:, sl], RC(nm, tt * 512, 512), 512,
                                  tres=R1A, rbank=3, copy_eng="act")

            for k in range(10):
                if k < 8:
                    stA(k)
                if 0 <= k - 1 < 8:
                    stB(k - 1)
                if 0 <= k - 2 < 8:
                    rope_p2(sts[k - 2])

        def mixer_b(b):
            gpd(lambda e: e.dma_start(out=cosB, in_=tab["cosB"]), [], R1A, "tcB")
            gpd(lambda e: e.dma_start(out=sinB, in_=tab["sinB"]), [], R1A, "tsB")
            for kv in range(2):
                slot = W.next([(BV0 + kv * 64, 64, 0), (BV0 + kv * 64, 64, 64)])
                for j4 in range(8):
                    bank = j4 % 2
                    for jj in range(4):
                        j = j4 * 4 + jj
                        for kc in range(8):
                            pe(lambda e, kc=kc, j=j, jj=jj, bank=bank, slot=slot: e.matmul(ps[:, bank, jj * 128:(jj + 1) * 128], lhsT=hT[:, kc, j * 128:(j + 1) * 128],
                                                                                 rhs=WS[slot][:, kc, :], start=(kc == 0), stop=(kc == 7)),
                               ["WS%d" % slot, "hT.%d" % (j // 4)], PS(bank))
                    act(lambda e, j4=j4, bank=bank: e.activation(out=VT4[:, j4 * 4:(j4 + 1) * 4, 0:3:2, :],
                                                                 in_=ps[:, bank, :].rearrange("p (j a c) -> p j a c", j=4, a=2), func=AF.Copy),
                        [], PS(bank) + ["VT"])
                slot = W.next([(BK0 + kv * 64, 64, 0), (BK0 + kv * 64, 64, 64)])
                qk_norm_rope(slot, kn_t, KT, "KT")
                for hq in range(2):
                    c = kv * 2 + hq
                    slot = W.next([(BQ0 + c * 128, 128, 0)])
                    qk_norm_rope(slot, qn_t, QT, "QT")
                    gslot = W.next([(GB0 + c * 128, 128, 0)])
                    gnext = gate_part(gslot, 0, 5)
                    for tt in range(8):
                        ae, ao = (4, 5) if tt % 2 == 0 else (6, 7)
                        qs = slice(tt * 512, (tt + 1) * 512)
                        gi = gnext

                        def qk(kc, u):
                            for hh in range(2):
                                rows = slice(64 * hh, 64 * hh + 64)
                                pe(lambda e, rows=rows, hh=hh, kc=kc, u=u, qs=qs: e.matmul(ps[:, 2 * u + hh, :], lhsT=KT[rows, kc * 128:(kc + 1) * 128], rhs=QT[rows, qs],
                                                                                  start=True, stop=True),
                                   RC("KT", kc * 128, 128) + ["QT.%d" % tt], PS(2 * u + hh))
                        qk(0, 0)
                        for kc in range(32):
                            u = kc % 2
                            i = kc % 2
                            if kc + 1 < 32:
                                qk(kc + 1, (kc + 1) % 2)
                            act(lambda e, u=u, i=i: e.activation(out=PT[i][:].rearrange("p (a c) -> p a c", a=2), in_=ps[:, 2 * u:2 * u + 2, :],
                                                                 func=AF.Exp, scale=0.125), [], PS(2 * u, 2 * u + 1) + [*PTR(i)])
                            pe(lambda e, kc=kc, i=i, ae=ae: e.matmul(ps[:, ae, :], lhsT=VT[:, kc, 0:128], rhs=PT[i][:, 0:512], start=(kc == 0), stop=(kc == 31)),
                               ["VT", *PTR(i)], PS(ae))
                            pe(lambda e, kc=kc, i=i, ao=ao: e.matmul(ps[:, ao, :], lhsT=VT[:, kc, 64:192], rhs=PT[i][:, 512:1024], start=(kc == 0), stop=(kc == 31)),
                               ["VT", *PTR(i)], PS(ao))
                        if tt + 1 < 8:
                            gnext = gate_part(gslot, tt + 1, 7 if tt % 2 == 0 else 5)
                        gate_tail(gi, b, 1, c, tt, ps[0:64, ae, :], ps[64:128, ao, :], ps[64:128, ae, :], ps[0:64, ao, :],
                                  PS(ae, ao), True)

        def phase_b():
            for k3 in range(3):
                gpd(lambda e, k3=k3: e.dma_start(out=WMG[:, :, k3 * 1024:(k3 + 1) * 1024], in_=w_in_v[:, :, MG0 + k3 * 1024:MG0 + (k3 + 1) * 1024]),
                    [], ["WMG%d" % k3], "WMG%d" % k3)
                gpd(lambda e, k3=k3: e.dma_start(out=WBR[k3][:], in_=w_br_v[k3]), [], ["WBR%d" % k3], "WBR%d" % k3)
            gpd(lambda e: e.dma_start(out=WOUT[:], in_=w_out_v), [], ["WOUT"], "WOUT")
            tiles = [(b, tt) for b in range(NB) for tt in range(8)]

            def loads(n):
                b, tt = tiles[n]
                i = n % 2
                spd(lambda e: e.dma_start(out=XB[i][:], in_=x[b, tt * 512:(tt + 1) * 512, :].rearrange("(j p) c -> p j c", p=128)),
                    [], ["XB%d" % i], "XB%d" % i)
                for br in range(3):
                    spd(lambda e, br=br: e.dma_start(out=UB[i][:, br, :, :], in_=uscr[b, br, :, :, tt * 512:(tt + 1) * 512].rearrange("k p t -> p k t")),
                        [], ["UB%d" % i], "UB%d_%d" % (i, br))

            def norm_p1(i, j):
                k = j % 2
                sc = 4 * j
                gp(lambda e: e.memset(SS[:, sc:sc + 1], 0.0), [], ["SSn%d" % j])
                act(lambda e: e.activation(out=JUNKB[:], in_=XB[i][:, j, :], func=AF.Square, accum_out=SS[:, sc:sc + 1]),
                    ["XB%d" % i, "SSn%d" % j], ["JUNKB", "SSn%d" % j])
                dve(lambda e: e.tensor_scalar(out=SS[:, sc + 1:sc + 2], in0=SS[:, sc:sc + 1], scalar1=1.0 / D, scalar2=EPS, op0=ALU.mult, op1=ALU.add),
                    ["SSn%d" % j], ["SSn%d" % j])
                gp(lambda e: e.tensor_tensor(out=SS[:, sc + 2:sc + 3], in0=SS[:, sc + 1:sc + 2], in1=NEGH[:, 0:1], op=ALU.pow), ["SSn%d" % j, "const"], ["SSn%d" % j])
                dve(lambda e: e.scalar_tensor_tensor(out=XNB[k][:], in0=XB[i][:, j, :], scalar=SS[:, sc + 2:sc + 3], in1=gpre_bc[:],
                                                     op0=ALU.mult, op1=ALU.mult), ["XB%d" % i, "SSn%d" % j, "const"], ["XNB%d" % k])
                return k

            def norm_p2(j, k):
                for kc in range(8):
                    pe(lambda e, kc=kc: e.transpose(out=psT[:, kc * 128:(kc + 1) * 128], in_=XNB[k][:, kc * 128:(kc + 1) * 128], identity=ident[:]),
                       ["XNB%d" % k, "const"], PS(7))
                act(lambda e: e.activation(out=HB[:, :, j * 128:(j + 1) * 128], in_=psT.rearrange("p (k t) -> p k t", k=8), func=AF.Copy),
                    [], PS(7) + ["HB"])

            def build_hb(i):
                kk = norm_p1(i, 0)
                for j in range(4):
                    kn_ = norm_p1(i, j + 1) if j + 1 < 4 else None
                    norm_p2(j, kk)
                    kk = kn_

            def tile_body(n, b, tt, i):
                have_next = n + 1 < len(tiles)
                if have_next:
                    loads(n + 1)
                for c in range(8):
                    for br in range(3):
                        q = (c * 3 + br) % 2
                        bm_, by_ = (0, 1) if q == 0 else (2, 3)
                        for kc in range(8):
                            pe(lambda e, kc=kc, br=br, c=c, bm_=bm_: e.matmul(ps[:, bm_, :], lhsT=WMG[:, kc, br * 1024 + c * 128:br * 1024 + (c + 1) * 128], rhs=HB[:, kc, :],
                                                                             start=(kc == 0), stop=(kc == 7)), ["WMG%d" % br, "HB"], PS(bm_))
                        for k4 in range(4):
                            pe(lambda e, k4=k4, br=br, c=c, by_=by_: e.matmul(ps[:, by_, :], lhsT=WBR[br][:, k4, c * 128:(c + 1) * 128], rhs=UB[i][:, br, k4, :],
                                                                             start=(k4 == 0), stop=(k4 == 3)), ["WBR%d" % br, "UB%d" % i], PS(by_))
                        act(lambda e, q=q, bm_=bm_, br=br, c=c: e.activation(out=TGB[q][:], in_=ps[:, bm_, :], func=AF.Tanh, scale=0.5,
                                                                             bias=bmh[:, br * 8 + c:br * 8 + c + 1]), ["const"], PS(bm_) + ["TGB%d" % q])
                        if br == 0:
                            dve(lambda e, q=q, by_=by_: e.scalar_tensor_tensor(out=MACC[:], in0=TGB[q][:], scalar=1.0, in1=ps[:, by_, :], op0=ALU.add, op1=ALU.mult),
                                ["TGB%d" % q], PS(by_) + ["MACC"])
                        else:
                            dve(lambda e, q=q, by_=by_: e.scalar_tensor_tensor(out=TMPB[0][:], in0=TGB[q][:], scalar=1.0, in1=ps[:, by_, :], op0=ALU.add, op1=ALU.mult),
                                ["TGB%d" % q], PS(by_) + ["TMPB0"])
                            gp(lambda e, q=q: e.tensor_tensor(out=MACC[:], in0=MACC[:], in1=TMPB[0][:], op=ALU.add), ["MACC", "TMPB0"], ["MACC"])
                    act(lambda e, c=c: e.activation(out=MT[:, c, :], in_=MACC[:], func=AF.Copy, scale=0.5), ["MACC"], ["MT"])
                inext = (n + 1) % 2
                kk = norm_p1(inext, 0) if have_next else None
                for j in range(4):
                    k = j % 2
                    b0, b1 = (4, 5) if j % 2 == 0 else (0, 1)
                    for hf, bk in ((0, b0), (1, b1)):
                        for kc in range(8):
                            pe(lambda e, kc=kc, j=j, hf=hf, bk=bk: e.matmul(ps[:, bk, :], lhsT=MT[:, kc, j * 128:(j + 1) * 128], rhs=WOUT[:, kc, hf * 512:(hf + 1) * 512],
                                                                           start=(kc == 0), stop=(kc == 7)), ["MT", "WOUT"], PS(bk))
                    if have_next:
                        kn_ = norm_p1(inext, j + 1) if j + 1 < 4 else None
                        norm_p2(j, kk)
                        kk = kn_
                    sc = 32 + 4 * j
                    gp(lambda e, sc=sc: e.memset(SS[:, sc:sc + 1], 0.0), [], ["SSp%d" % j])
                    act(lambda e, sc=sc, b0=b0: e.activation(out=JUNKB[:].rearrange("p (a c) -> p a c", a=2), in_=ps[:, b0:b0 + 2, :], func=AF.Square,
                                                           accum_out=SS[:, sc:sc + 1]), ["SSp%d" % j], PS(b0, b1) + ["JUNKB", "SSp%d" % j])
                    dve(lambda e, sc=sc: e.tensor_scalar(out=SS[:, sc + 1:sc + 2], in0=SS[:, sc:sc + 1], scalar1=1.0 / D, scalar2=EPS, op0=ALU.mult, op1=ALU.add),
                        ["SSp%d" % j], ["SSp%d" % j])
                    gp(lambda e, sc=sc: e.tensor_tensor(out=SS[:, sc + 2:sc + 3], in0=SS[:, sc + 1:sc + 2], in1=NEGH[:, 0:1], op=ALU.pow), ["SSp%d" % j, "const"], ["SSp%d" % j])
                    dve(lambda e, k=k, sc=sc, b0=b0: e.scalar_tensor_tensor(out=YT[k][:].rearrange("p (a c) -> p a c", a=2), in0=ps[:, b0:b0 + 2, :],
                                                                          scalar=SS[:, sc + 2:sc + 3], in1=gpost_bc[:].rearrange("p (a c) -> p a c", a=2),
                                                                          op0=ALU.mult, op1=ALU.mult), ["SSp%d" % j, "const"], PS(b0, b1) + ["YT%d" % k])
                    gp(lambda e, k=k, j=j: e.tensor_tensor(out=YT[k][:], in0=YT[k][:], in1=XB[i][:, j, :], op=ALU.add), ["YT%d" % k, "XB%d" % i], ["YT%d" % k])
                    spd(lambda e, k=k, j=j: e.dma_start(out=y[b, tt * 512 + j * 128:tt * 512 + (j + 1) * 128, :], in_=YT[k][:]),
                        ["YT%d" % k], ["yout%d" % k], "YT%d" % k)

            loads(0)
            build_hb(0)
            for n, (b, tt) in enumerate(tiles):
                tile_body(n, b, tt, n % 2)
            S.op("sp", lambda e: e.nop(), ["yout0", "yout1"], [])

        def gen():
            ctr.update({k: 0 for k in ctr})
            setup()
            for b in range(NB if "2" in STAGES else 1):
                if "0" in STAGES:
                    stage0(b)
                if "m" in STAGES:
                    mixer_m(b)
                if "a" in STAGES:
                    mixer_a(b)
                if "b" in STAGES:
                    mixer_b(b)
            S.barrier(["pe", "act", "dve", "pool", "sp"])
            if "p" in STAGES:
                phase_b()


        S.dry = True
        gen()
        S.dry = False
        gen()
        keys = S.finalize()
        semmap = {k: es.enter_context(nc.semaphore("s%d" % n)) for n, k in enumerate(keys)}
        with nc.Block() as block:
            @block.sync
            def _(e):
                S.emit("sp", e, semmap)

            @block.tensor
            def _(e):
                S.emit("pe", e, semmap)

            @block.scalar
            def _(e):
                S.emit("act", e, semmap)

            @block.vector
            def _(e):
                S.emit("dve", e, semmap)

            @block.gpsimd
            def _(e):
                S.emit("pool", e, semmap)
    return nc


_CACHE = {}


def kernel(x, mem, g_pre, w_in, b_merge, q_norm, k_norm, g_mem, w_mem_kv, w_br_a, w_br_b, w_br_m, w_out, g_post):
    f = lambda a: np.ascontiguousarray(np.asarray(a), dtype=np.float32)
    x = f(x); mem = f(mem)
    consts = _const_tables()
    shared = {
        "w_in": f(w_in)[0], "w_mem": f(w_mem_kv)[0], "w_br0": f(w_br_a)[0], "w_br1": f(w_br_b)[0], "w_br2": f(w_br_m)[0],
        "w_out": f(w_out)[0], "g_pre": f(g_pre), "g_mem": f(g_mem), "g_post": f(g_post),
        "bm": np.ascontiguousarray(f(b_merge)[0].reshape(24, 128).T),
        "qn": np.ascontiguousarray(np.tile(f(q_norm)[0], 2).reshape(128, 1)),
        "kn": np.ascontiguousarray(np.tile(f(k_norm)[0], 2).reshape(128, 1)),
    }
    shared.update(consts)
    if "nc" not in _CACHE:
        _CACHE["nc"] = build_program()
    nc = _CACHE["nc"]
    in_maps = []
    for c in range(NCORES):
        m = dict(shared)
        m["x"] = np.ascontiguousarray(x[c * NB:(c + 1) * NB])
        m["mem"] = np.ascontiguousarray(mem[c * NB:(c + 1) * NB])
        in_maps.append(m)
    res = run_bass_kernel_spmd(nc, in_maps, core_ids=list(range(NCORES)))
    out = np.concatenate([np.asarray(r["y"]) for r in res.results], axis=0)
    return out.astype(np.float32)
```
